# Optimizing a Trainium2 kernel written in Bass

```python
import jax, jax.numpy as jnp
from jax import lax
import numpy as np

D_MODEL = 1024
BATCH = 8
SEQ = 8192
DEPTH = 1
DEC_BATCH = 8
DEC_SEQ = 64
PAST_LEN = 4096

CHUNK = 64
N_HEADS = 8
N_KV_HEADS = 2
GROUP = N_HEADS // N_KV_HEADS
HEAD_DIM = 64
ATTN_WIDTH = N_HEADS * HEAD_DIM
KV_WIDTH = N_KV_HEADS * HEAD_DIM
WINDOW = 128
WINDOW_CHUNKS = WINDOW // CHUNK
ROT_DIM = HEAD_DIM // 4
ROPE_THETA = 500000.0
LRU_WIDTH = D_MODEL - ATTN_WIDTH
LRU_BLOCKS = 8
LRU_BLOCK = LRU_WIDTH // LRU_BLOCKS
CONV_WIDTH = 4
LRU_C = 8.0
MIX_WIDTH = ATTN_WIDTH + LRU_WIDTH
IN_COLS = ATTN_WIDTH + 2 * KV_WIDTH + 2 * LRU_WIDTH
D_FF = 2816
PLE_DIM = 256
ALPHA = (2.0 * DEPTH) ** 0.25
BETA = (8.0 * DEPTH) ** -0.25
LN_EPS = 1e-5
NEG_INF = -1e30

kernel_name = 'hybrid_streaming_swa_rglru_step'


def _layer_norm(x, g, b):
    xf = x.astype(jnp.float32)
    mu = jnp.mean(xf, -1, keepdims=True)
    var = jnp.mean(jnp.square(xf - mu), -1, keepdims=True)
    y = (xf - mu) * lax.rsqrt(var + LN_EPS)
    return (y * g.astype(jnp.float32) + b.astype(jnp.float32)).astype(x.dtype)


def _swiglu(x, wg, wu, wd):
    return (jax.nn.silu(x @ wg) * (x @ wu)) @ wd


def _rope(x, pos):
    half = ROT_DIM // 2
    inv = jnp.power(jnp.float32(ROPE_THETA), -jnp.arange(half, dtype=jnp.float32) * (2.0 / ROT_DIM))
    ang = pos.astype(jnp.float32)[:, None] * inv[None, :]
    cos = jnp.cos(ang)[None, :, None, :].astype(x.dtype)
    sin = jnp.sin(ang)[None, :, None, :].astype(x.dtype)
    x1 = x[..., :half]
    x2 = x[..., half:ROT_DIM]
    return jnp.concatenate([x1 * cos - x2 * sin, x2 * cos + x1 * sin, x[..., ROT_DIM:]], axis=-1)


def _sink_attention(q, kb, vb, sinks, valid):
    s = jnp.einsum('bncvgd,bnkvd->bnvgck', q, kb).astype(jnp.float32) * (HEAD_DIM ** -0.5)
    if valid is not None:
        s = jnp.where(valid[None, :, None, None, None, :], s, NEG_INF)
    sink = sinks.astype(jnp.float32).reshape(N_KV_HEADS, GROUP)[None, None, :, :, None, None]
    m = jnp.maximum(jnp.max(s, axis=-1, keepdims=True), sink)
    pr = jnp.exp(s - m)
    denom = jnp.sum(pr, axis=-1, keepdims=True) + jnp.exp(sink - m)
    w = (pr / denom).astype(vb.dtype)
    return jnp.einsum('bnvgck,bnkvd->bncvgd', w, vb)


def _prompt_attention(q, k, v, sinks):
    B, S = q.shape[0], q.shape[1]
    nc = S // CHUNK
    qc = q.reshape(B, nc, CHUNK, N_KV_HEADS, GROUP, HEAD_DIM)

    def band(t):
        tc = t.reshape(B, nc, CHUNK, N_KV_HEADS, HEAD_DIM)
        tp = jnp.pad(tc, ((0, 0), (WINDOW_CHUNKS, 0), (0, 0), (0, 0), (0, 0)))
        return jnp.concatenate([tp[:, j:j + nc] for j in range(WINDOW_CHUNKS + 1)], axis=2)

    key_chunk = jnp.arange(nc)[:, None] - WINDOW_CHUNKS + (jnp.arange((WINDOW_CHUNKS + 1) * CHUNK) // CHUNK)[None, :]
    o = _sink_attention(qc, band(k), band(v), sinks, key_chunk >= 0)
    return o.reshape(B, S, ATTN_WIDTH), k[:, -WINDOW:], v[:, -WINDOW:]


def _sample_attention(q, k, v, k_past, v_past, sinks):
    B, T = q.shape[0], q.shape[1]
    k_all = jnp.concatenate([k_past.astype(k.dtype), k], axis=1)
    v_all = jnp.concatenate([v_past.astype(v.dtype), v], axis=1)
    qc = q.reshape(B, 1, T, N_KV_HEADS, GROUP, HEAD_DIM)
    o = _sink_attention(qc, k_all[:, None], v_all[:, None], sinks, None)
    return o.reshape(B, T, ATTN_WIDTH), k_all[:, -WINDOW:], v_all[:, -WINDOW:]


def _lin_combine(c1, c2):
    a1, b1 = c1
    a2, b2 = c2
    return a1 * a2, a2 * b1 + b2


def _rglru(xb, conv_prev, h_prev, conv_w, conv_b, w_a, b_a, w_x, b_x, lam):
    B, T = xb.shape[0], xb.shape[1]
    xp = jnp.concatenate([conv_prev.astype(xb.dtype), xb], axis=1)
    new_conv = xp[:, -(CONV_WIDTH - 1):]
    xc = conv_b + conv_w[0] * xp[:, 0:T]
    for j in range(1, CONV_WIDTH):
        xc = xc + conv_w[j] * xp[:, j:j + T]
    xf = xc.astype(jnp.float32)
    xr = xf.reshape(B, T, LRU_BLOCKS, LRU_BLOCK)
    r = jax.nn.sigmoid(jnp.einsum('btnc,ncd->btnd', xr, w_a.astype(jnp.float32)).reshape(B, T, LRU_WIDTH) + b_a.astype(jnp.float32))
    i = jax.nn.sigmoid(jnp.einsum('btnc,ncd->btnd', xr, w_x.astype(jnp.float32)).reshape(B, T, LRU_WIDTH) + b_x.astype(jnp.float32))
    log_a = -LRU_C * jax.nn.softplus(-lam.astype(jnp.float32)) * r
    a = jnp.exp(log_a)
    u = jnp.sqrt(-jnp.expm1(2.0 * log_a)) * (i * xf)
    u = u.at[:, 0].add(a[:, 0] * h_prev.astype(jnp.float32))
    _, hs = lax.associative_scan(_lin_combine, (a, u), axis=1)
    return hs, new_conv, hs[:, -1]


def _layer(x, p, pos, k_past, v_past, conv_prev, h_prev, prm):
    B, T = x.shape[0], x.shape[1]
    h = _layer_norm(ALPHA * x + 0.5 * _swiglu(x, prm['ffn1_wg'], prm['ffn1_wu'], prm['ffn1_wd']), prm['ln1_g'], prm['ln1_b'])
    z = h @ prm['w_in']
    o1 = ATTN_WIDTH
    o2 = o1 + KV_WIDTH
    o3 = o2 + KV_WIDTH
    o4 = o3 + LRU_WIDTH
    q = _rope(z[..., :o1].reshape(B, T, N_HEADS, HEAD_DIM), pos)
    k = _rope(z[..., o1:o2].reshape(B, T, N_KV_HEADS, HEAD_DIM), pos)
    v = z[..., o2:o3].reshape(B, T, N_KV_HEADS, HEAD_DIM)
    xb = z[..., o3:o4]
    gb = z[..., o4:]
    if k_past is None:
        attn, new_k, new_v = _prompt_attention(q, k, v, prm['attn_sinks'])
    else:
        attn, new_k, new_v = _sample_attention(q, k, v, k_past, v_past, prm['attn_sinks'])
    hs, new_conv, new_h = _rglru(xb, conv_prev, h_prev, prm['conv_w'], prm['conv_b'], prm['lru_wa'], prm['lru_ba'], prm['lru_wx'], prm['lru_bx'], prm['lru_lambda'])
    lru = (hs * jax.nn.gelu(gb.astype(jnp.float32))).astype(x.dtype)
    mix = jnp.concatenate([attn, lru], axis=-1) @ prm['w_out']
    h = _layer_norm(ALPHA * h + mix, prm['ln2_g'], prm['ln2_b'])
    h = _layer_norm(ALPHA * h + 0.5 * _swiglu(h, prm['ffn2_wg'], prm['ffn2_wu'], prm['ffn2_wd']), prm['ln3_g'], prm['ln3_b'])
    y = h + jax.nn.sigmoid(h @ prm['w_ple_gate']) * (p @ prm['w_ple'])
    return y, new_k, new_v, new_conv, new_h


def setup_inputs(seed: int = 0) -> dict:
    key = jax.random.key(seed)
    ks = jax.random.split(key, 40)
    f32 = jnp.float32
    L = DEPTH
    D = D_MODEL

    def nrm(k, shape, scale):
        return jax.random.normal(k, shape, f32) * scale

    u = jax.random.uniform(ks[30], (L, LRU_WIDTH), f32, 0.9, 0.999)
    s = u ** (1.0 / LRU_C)
    lam = jnp.log(s) - jnp.log1p(-s)
    return {
        'x_prompt': nrm(ks[0], (BATCH, SEQ, D), 1.0),
        'x_sample': nrm(ks[1], (DEC_BATCH, DEC_SEQ, D), 1.0),
        'p_prompt': nrm(ks[2], (L, BATCH, SEQ, PLE_DIM), 1.0),
        'p_sample': nrm(ks[3], (L, DEC_BATCH, DEC_SEQ, PLE_DIM), 1.0),
        'cache_k': nrm(ks[4], (L, DEC_BATCH, WINDOW, N_KV_HEADS, HEAD_DIM), 1.0),
        'cache_v': nrm(ks[5], (L, DEC_BATCH, WINDOW, N_KV_HEADS, HEAD_DIM), 1.0),
        'state_conv': nrm(ks[6], (L, DEC_BATCH, CONV_WIDTH - 1, LRU_WIDTH), 1.0),
        'state_h': nrm(ks[7], (L, DEC_BATCH, LRU_WIDTH), 0.5),
        'ffn1_wg': nrm(ks[8], (L, D, D_FF), D ** -0.5),
        'ffn1_wu': nrm(ks[9], (L, D, D_FF), D ** -0.5),
        'ffn1_wd': nrm(ks[10], (L, D_FF, D), BETA * D_FF ** -0.5),
        'ln1_g': 1.0 + nrm(ks[11], (L, D), 0.02),
        'ln1_b': nrm(ks[12], (L, D), 0.02),
        'w_in': nrm(ks[13], (L, D, IN_COLS), D ** -0.5),
        'attn_sinks': nrm(ks[14], (L, N_HEADS), 0.5),
        'conv_w': nrm(ks[15], (L, CONV_WIDTH, LRU_WIDTH), CONV_WIDTH ** -0.5),
        'conv_b': nrm(ks[16], (L, LRU_WIDTH), 0.02),
        'lru_wa': nrm(ks[17], (L, LRU_BLOCKS, LRU_BLOCK, LRU_BLOCK), LRU_BLOCK ** -0.5),
        'lru_ba': nrm(ks[18], (L, LRU_WIDTH), 0.02),
        'lru_wx': nrm(ks[19], (L, LRU_BLOCKS, LRU_BLOCK, LRU_BLOCK), LRU_BLOCK ** -0.5),
        'lru_bx': nrm(ks[20], (L, LRU_WIDTH), 0.02),
        'lru_lambda': lam,
        'w_out': nrm(ks[21], (L, MIX_WIDTH, D), BETA * MIX_WIDTH ** -0.5),
        'ln2_g': 1.0 + nrm(ks[22], (L, D), 0.02),
        'ln2_b': nrm(ks[23], (L, D), 0.02),
        'ffn2_wg': nrm(ks[24], (L, D, D_FF), D ** -0.5),
        'ffn2_wu': nrm(ks[25], (L, D, D_FF), D ** -0.5),
        'ffn2_wd': nrm(ks[26], (L, D_FF, D), BETA * D_FF ** -0.5),
        'ln3_g': 1.0 + nrm(ks[27], (L, D), 0.02),
        'ln3_b': nrm(ks[28], (L, D), 0.02),
        'w_ple': nrm(ks[29], (L, PLE_DIM, D), PLE_DIM ** -0.5),
        'w_ple_gate': nrm(ks[31], (L, D, D), D ** -0.5),
    }


def reference(x_prompt, x_sample, p_prompt, p_sample, cache_k, cache_v, state_conv, state_h,
              ffn1_wg, ffn1_wu, ffn1_wd, ln1_g, ln1_b, w_in, attn_sinks, conv_w, conv_b,
              lru_wa, lru_ba, lru_wx, lru_bx, lru_lambda, w_out, ln2_g, ln2_b,
              ffn2_wg, ffn2_wu, ffn2_wd, ln3_g, ln3_b, w_ple, w_ple_gate):
    pos_prompt = jnp.arange(x_prompt.shape[1])
    pos_sample = PAST_LEN + jnp.arange(x_sample.shape[1])
    yp, ys = x_prompt, x_sample
    kp_l, vp_l, cp_l, hp_l = [], [], [], []
    ks_l, vs_l, cs_l, hs_l = [], [], [], []
    for i in range(DEPTH):
        prm = dict(ffn1_wg=ffn1_wg[i], ffn1_wu=ffn1_wu[i], ffn1_wd=ffn1_wd[i], ln1_g=ln1_g[i], ln1_b=ln1_b[i],
                   w_in=w_in[i], attn_sinks=attn_sinks[i], conv_w=conv_w[i], conv_b=conv_b[i],
                   lru_wa=lru_wa[i], lru_ba=lru_ba[i], lru_wx=lru_wx[i], lru_bx=lru_bx[i], lru_lambda=lru_lambda[i],
                   w_out=w_out[i], ln2_g=ln2_g[i], ln2_b=ln2_b[i], ffn2_wg=ffn2_wg[i], ffn2_wu=ffn2_wu[i],
                   ffn2_wd=ffn2_wd[i], ln3_g=ln3_g[i], ln3_b=ln3_b[i], w_ple=w_ple[i], w_ple_gate=w_ple_gate[i])
        bp = yp.shape[0]
        conv0 = jnp.zeros((bp, CONV_WIDTH - 1, LRU_WIDTH), yp.dtype)
        h0 = jnp.zeros((bp, LRU_WIDTH), jnp.float32)
        yp, kp, vp, cp, hp = _layer(yp, p_prompt[i], pos_prompt, None, None, conv0, h0, prm)
        ys, kn, vn, cn, hn = _layer(ys, p_sample[i], pos_sample, cache_k[i], cache_v[i], state_conv[i], state_h[i], prm)
        kp_l.append(kp); vp_l.append(vp); cp_l.append(cp); hp_l.append(hp)
        ks_l.append(kn); vs_l.append(vn); cs_l.append(cn); hs_l.append(hn)
    return (yp, ys, jnp.stack(kp_l), jnp.stack(vp_l), jnp.stack(cp_l), jnp.stack(hp_l),
            jnp.stack(ks_l), jnp.stack(vs_l), jnp.stack(cs_l), jnp.stack(hs_l))
```

```python
import numpy as np
import ml_dtypes
from contextlib import ExitStack

import concourse.bass as bass
import concourse.mybir as mybir
from concourse.bass_utils import run_bass_kernel_spmd

F32 = mybir.dt.float32
BF16 = mybir.dt.bfloat16
AF = mybir.ActivationFunctionType
ALU = mybir.AluOpType

D = 1024
SEQ = 8192
TS = 64
DFF = 2816
NF = DFF // 128
INC = 1792
PLE = 256
WINDOW = 128
ALPHA = 2.0 ** 0.25
LN_EPS = 1e-5
ROPE_THETA = 500000.0
PAST_LEN = 4096
TILE = 512
NS = 6
SLOT = 4096
NBLK = 47

HEAD_PERM = [0, 2, 1, 3, 4, 6, 5, 7]


class Op:
    __slots__ = ("eng", "fn", "waits", "signal", "dsem", "dval", "idx", "cnt")

    def __init__(self, eng, fn):
        self.eng = eng
        self.fn = fn
        self.waits = []
        self.signal = False
        self.dsem = None
        self.dval = 0
        self.idx = 0
        self.cnt = 0


class Prog:
    ENGS = ("pe", "act", "dve", "pool", "sp")

    def __init__(self):
        self.streams = {e: [] for e in self.ENGS}
        self.last_w = {}
        self.readers = {}
        self.known = {e: {} for e in self.ENGS}
        self.dma_cnt = {}
        self.final_sems = set()

    def _dep(self, op, d):
        if d is None or d is op:
            return
        if d.dsem is not None:
            key = ("d", d.dsem)
            val = d.dval
            if d.dsem in self.final_sems:
                val = -1
                if self.known[op.eng].get(key, 0) == -1:
                    return
                self.known[op.eng][key] = -1
                op.waits.append((key, d, True))
                return
            if self.known[op.eng].get(key, 0) >= val:
                return
            self.known[op.eng][key] = val
            op.waits.append((key, d, False))
        else:
            if d.eng == op.eng and op.eng == "pe" and op.dsem is None:
                return
            key = ("e", d.eng)
            if self.known[op.eng].get(key, -1) >= d.idx:
                return
            self.known[op.eng][key] = d.idx
            d.signal = True
            op.waits.append((key, d, False))

    def op(self, eng, fn, reads=(), writes=(), dsem=None):
        o = Op(eng, fn)
        st = self.streams[eng]
        o.idx = len(st)
        if dsem is not None:
            o.dsem = dsem
            self.dma_cnt[dsem] = self.dma_cnt.get(dsem, 0) + 16
            o.dval = self.dma_cnt[dsem]
        deps = []
        for n in reads:
            deps.append(self.last_w.get(n))
            if n.startswith("ps"):
                deps.extend(r for r in self.readers.get(n, ()) if r.eng != eng)
        for n in writes:
            deps.append(self.last_w.get(n))
            deps.extend(self.readers.get(n, ()))
        best = {}
        for d in deps:
            if d is None or d is o:
                continue
            if d.dsem is not None:
                key = ("d", d.dsem)
                if key not in best or best[key].dval < d.dval:
                    best[key] = d
            else:
                key = ("e", d.eng)
                if key not in best or best[key].idx < d.idx:
                    best[key] = d
        for d in best.values():
            self._dep(o, d)
        for n in reads:
            self.readers.setdefault(n, []).append(o)
        for n in writes:
            self.last_w[n] = o
            self.readers[n] = []
        st.append(o)
        return o

    def emit(self, nc, block, esem, dsems):
        for e in self.ENGS:
            c = 0
            for o in self.streams[e]:
                if o.dsem is None and o.signal:
                    c += 1
                    o.cnt = c
        prog = self

        def run(engname):
            def body(eng):
                for o in prog.streams[engname]:
                    for key, d, fin in o.waits:
                        if key[0] == "d":
                            v = prog.dma_cnt[d.dsem] if fin else d.dval
                            eng.wait_ge(dsems[d.dsem], v)
                        else:
                            eng.wait_ge(esem[d.eng], d.cnt)
                    ins = o.fn(eng)
                    if o.dsem is not None:
                        ins.then_inc(dsems[o.dsem], 16)
                    elif o.signal:
                        ins.then_inc(esem[engname], 1)
                if engname == "sp":
                    for k, v in prog.dma_cnt.items():
                        eng.wait_ge(dsems[k], v)
            return body

        block.tensor(run("pe"))
        block.scalar(run("act"))
        block.vector(run("dve"))
        block.gpsimd(run("pool"))
        block.sync(run("sp"))


def block_table():
    t = []
    for ffn in (1, 2):
        pass
    blocks = []
    for b in range(11):
        blocks.append(("up", 1, b))
    for hf in range(2):
        for g in range(3):
            blocks.append(("dn", 1, hf, g))
    for i in (0, 2, 3, 4, 5):
        blocks.append(("in", i))
    for hf in range(2):
        blocks.append(("oa", hf))
        blocks.append(("ol", hf))
    for b in range(11):
        blocks.append(("up", 2, b))
    for hf in range(2):
        for g in range(3):
            blocks.append(("dn", 2, hf, g))
    for hf in range(2):
        blocks.append(("pg", hf))
        blocks.append(("pp", hf))
    assert len(blocks) == NBLK
    return blocks


BLOCKS = block_table()
BIDX = {b: i for i, b in enumerate(BLOCKS)}


def blk_extent(b):
    k = b[0]
    if k == "up":
        return 128, 4096
    if k == "dn":
        nf = 8 if b[3] < 2 else 6
        return 128, nf * 512
    if k == "in":
        return (128, 4096) if b[1] in (0, 3, 4) else (128, 1024)
    if k == "oa":
        return 128, 2048
    if k == "ol":
        return 128, 2048
    if k == "pg":
        return 128, 4096
    if k == "pp":
        return 128, 1024
    raise ValueError(b)


def build(ntiles_prompt=SEQ // TILE, with_sample=True):
    NTOK = ntiles_prompt * TILE + (TS if with_sample else 0)
    NPOS = SEQ + TS
    nc = bass.Bass("TRN2", target_bir_lowering=False)
    P = Prog()
    es = ExitStack()

    def din(name, shape, dt=F32):
        return nc.dram_tensor(name, list(shape), dt, kind="ExternalInput").ap()

    def dout(name, shape, dt=F32):
        return nc.dram_tensor(name, list(shape), dt, kind="ExternalOutput").ap()

    x_d = din("x", [NTOK, D])
    p_d = din("p", [NTOK, PLE])
    ck_d = din("cache_k", [128, 128])
    cv_d = din("cache_v", [128, 128])
    sc_d = din("state_conv", [3, 512])
    sh_d = din("state_h", [512])
    wg_d = {1: din("ffn1_wg", [D, DFF]), 2: din("ffn2_wg", [D, DFF])}
    wu_d = {1: din("ffn1_wu", [D, DFF]), 2: din("ffn2_wu", [D, DFF])}
    wd_d = {1: din("ffn1_wd", [DFF, D]), 2: din("ffn2_wd", [DFF, D])}
    win_d = din("w_in", [D, INC])
    wout_d = din("w_out", [D, D])
    wpg_d = din("w_ple_gate", [D, D])
    wpp_d = din("w_ple", [PLE, D])
    lng_d = [din("ln%d_g" % i, [D]) for i in (1, 2, 3)]
    lnb_d = [din("ln%d_b" % i, [D]) for i in (1, 2, 3)]
    sink_d = din("attn_sinks", [8])
    convw_d = din("conv_w", [4, 512])
    convb_d = din("conv_b", [512])
    wa_d = din("lru_wa", [8, 64, 64])
    ba_d = din("lru_ba", [512])
    wx_d = din("lru_wx", [8, 64, 64])
    bx_d = din("lru_bx", [512])
    lam_d = din("lru_lambda", [512])
    rope_d = din("rope", [2, 128, NPOS])
    ident_d = din("ident", [128, 128])
    perm_d = din("perm", [128, 3, 128])

    y_d = dout("y", [NTOK, D])
    outs_state = {}
    for tag in ("p", "s"):
        outs_state[tag] = dict(
            k=dout("newk_" + tag, [128, 128]), v=dout("newv_" + tag, [128, 128]),
            c=dout("conv_" + tag, [3, 512]), h=dout("h_" + tag, [512]))

    ws_d = nc.dram_tensor("wstream", [NBLK, 128, SLOT], BF16, kind="Internal").ap()

    def sb(name, shape, dt=F32):
        return es.enter_context(nc.sbuf_tensor(name, list(shape), dt))

    hbuf = [sb("hbuf%d" % s, [128, D]) for s in range(4)]
    actT = sb("actT", [128, 8, TILE], BF16)
    hid = sb("hid", [128, NF, TILE], BF16)
    ring = [sb("ring%d" % i, [128, SLOT], BF16) for i in range(NS)]
    qT = sb("qT", [128, 4, TILE], BF16)
    kT = sb("kT", [128, 2, 128 + TILE], BF16)
    Vb = sb("Vb", [128, 5, 2, 66], BF16)
    xbuf = sb("xbuf", [128, 4, 4 + TILE])
    attnT = sb("attnT", [128, 4, TILE], BF16)
    atm = [sb("atm%d" % i, [64, 256]) for i in range(3)]
    esl = sb("esl", [64, 8])
    sink64 = sb("sink64", [64, 8])
    lruT = sb("lruT", [128, 4, TILE], BF16)
    PT = [sb("PT%d" % i, [128, 2, 256], BF16) for i in range(4)]
    rec = [sb("rec%d" % i, [64, 4]) for i in range(3)]
    rope_c = sb("rope_c", [128, TILE])
    rope_s = sb("rope_s", [128, TILE])
    gbc = [sb("gbc%d" % i, [128, D]) for i in range(3)]
    bbc = [sb("bbc%d" % i, [128, D]) for i in range(3)]
    pbuf = sb("pbuf", [128, 4, PLE])
    pT = sb("pT", [128, 2, TILE], BF16)
    ftA = [sb("ftA%d" % i, [128, TILE]) for i in range(2)]
    ftB = [sb("ftB%d" % i, [128, TILE]) for i in range(2)]
    ident = sb("ident_sb", [128, 128])
    perm = sb("perm_b", [128, 3, 128], BF16)
    qb = [sb("qb%d" % i, [128, TILE], BF16) for i in range(2)]
    ones_b = sb("ones_b", [128, 64], BF16)
    cw = sb("cw", [128, 4, 4])
    cb = sb("cb", [128, 4])
    hba = sb("hba", [128, 4])
    hbx = sb("hbx", [128, 4])
    lam = sb("lam", [128, 4])
    chalf = sb("chalf", [128, 4])
    cfull = sb("cfull", [128, 4])
    ltmp = sb("ltmp", [128, 4])
    wbd = sb("wbd", [128, 2, 4, 128], BF16)
    hcar = sb("hcar", [128, 4])
    LXC = [sb("L_xc%d" % i, [128, TILE]) for i in range(2)]
    LXCB = [sb("L_xcb%d" % i, [128, TILE], BF16) for i in range(2)]
    L_tr = sb("L_tr", [128, TILE])
    LA = sb("LA", [128, 2, TILE])
    LS = sb("LS", [128, 2, TILE])
    LU = sb("LU", [128, 2, TILE])
    L_ti = sb("L_ti", [128, TILE])
    LHS = [sb("L_hs%d" % i, [128, TILE]) for i in range(2)]
    L_g = sb("L_g", [128, TILE])
    L_g2 = sb("L_g2", [128, TILE])
    L_gb = sb("L_gb", [128, TILE])
    stats_l = [sb("stats%d" % i, [128, 2, 6]) for i in range(4)]
    mv_l = [sb("mv%d" % i, [128, 2]) for i in range(4)]
    rstd_l = [sb("rstd%d" % i, [128, 1]) for i in range(4)]
    nb_l = [sb("nb%d" % i, [128, 1]) for i in range(4)]
    mhalf = sb("mhalf", [128, 1])
    kfin = sb("kfin", [128, 128])
    kfin_t = sb("kfin_t", [128, 128])
    vfin = sb("vfin", [128, 128])
    ckt = ftA[1][:, 0:256].rearrange("p (a b) -> p a b", a=2)
    cvt = ftB[1][:, 0:128]

    psum_all = es.enter_context(nc.psum_tensor("ps_all", [128, 8, 512], F32))
    psum = [psum_all[:, i, :] for i in range(8)]
    bank_ctr = [0]

    def bank():
        b = bank_ctr[0] % 8
        bank_ctr[0] += 1
        return b

    def bank_pair():
        if bank_ctr[0] % 8 == 7:
            bank_ctr[0] += 1
        return bank(), bank()

    esem = {e: es.enter_context(nc.semaphore("s_" + e)) for e in Prog.ENGS}
    dsem_names = (["ring%d" % i for i in range(NS)] + ["hb%d" % s for s in range(4)] +
                  ["pb", "rope", "setup", "prep", "st_small", "ck", "cv", "car"])
    class _DS(dict):
        def __missing__(self, n):
            v = es.enter_context(nc.semaphore("d_" + n))
            self[n] = v
            return v
    dsems = _DS()
    for n in dsem_names:
        dsems[n]
    P.final_sems.add("setup")

    def dma(eng, out, in_, reads, writes, sem, nonc=False):
        if nonc:
            def fn(e, out=out, in_=in_):
                with nc.allow_non_contiguous_dma(reason="tiny strided state"):
                    return e.dma_start(out=out, in_=in_)
        else:
            def fn(e, out=out, in_=in_):
                return e.dma_start(out=out, in_=in_)
        return P.op(eng, fn, reads, writes, dsem=sem)

    def mm(out, lhsT, rhs, start, stop, reads, writes):
        return P.op("pe", lambda e: e.matmul(out, lhsT, rhs, start=start, stop=stop), reads, writes)

    def tp(out, in_, idn, reads, writes):
        return P.op("pe", lambda e: e.transpose(out, in_, idn), reads, writes)

    def act(out, in_, func, reads, writes, bias=None, scale=None):
        kw = {}
        if bias is not None:
            kw["bias"] = bias
        if scale is not None:
            kw["scale"] = scale
        return P.op("act", lambda e: e.activation(out, in_, func, **kw), reads, writes)

    def tt(eng, out, a, b, op, reads, writes):
        return P.op(eng, lambda e: e.tensor_tensor(out, a, b, op), reads, writes)

    def ts(eng, out, a, s1, s2, op0, op1, reads, writes):
        if op1 is None:
            return P.op(eng, lambda e: e.tensor_scalar(out, a, s1, None, op0), reads, writes)
        return P.op(eng, lambda e: e.tensor_scalar(out, a, s1, s2, op0, op1), reads, writes)

    def stt(out, a, s, b, op0, op1, reads, writes):
        return P.op("dve", lambda e: e.scalar_tensor_tensor(out, a, s, b, op0, op1), reads, writes)

    def cp(eng, out, in_, reads, writes):
        if eng == "act":
            return P.op("act", lambda e: e.copy(out, in_), reads, writes)
        return P.op(eng, lambda e: e.tensor_copy(out, in_), reads, writes)

    def mset(eng, out, val, writes):
        return P.op(eng, lambda e: e.memset(out, val), (), writes)

    prep_n = [0]
    prep_cur = [0]

    import os as _os
    KDBG = _os.environ.get("KDBG", "")
    STOP = int(_os.environ.get("KSTOP", "0"))

    def prep(out, in_):
        if "noprep" in KDBG:
            return
        def fn(e, out=out, in_=in_):
            with nc.allow_non_contiguous_dma(reason="one-time weight re-blocking"):
                return e.dma_start(out=out, in_=in_)
        prep_n[0] += 1
        o_ = P.op("pool", fn, (), ("wsdx%d" % prep_n[0],), dsem="prep%d" % prep_cur[0])
        P.last_w["wsd%d" % prep_cur[0]] = o_

    def wsv(bi):
        return ws_d[bi]

    def prep_block(bi):
        b = BLOCKS[bi]
        prep_cur[0] = bi
        k = b[0]
        dst = wsv(bi)
        if k == "up":
            f, blk = b[1], b[2]
            for gu, src in enumerate((wg_d[f], wu_d[f])):
                o = dst[:, gu * 2048:(gu + 1) * 2048].rearrange("p (k c) -> p k c", k=8)
                i = src[:, blk * 256:(blk + 1) * 256].rearrange("(k p) c -> p k c", p=128)
                prep(o, i)
        elif k == "dn":
            f, hf, g = b[1], b[2], b[3]
            nf = 8 if g < 2 else 6
            o = dst[:, 0:nf * 512].rearrange("p (f c) -> p f c", f=nf)
            i = wd_d[f][g * 8 * 128:(g * 8 + nf) * 128, hf * 512:(hf + 1) * 512].rearrange(
                "(f p) c -> p f c", p=128)
            prep(o, i)
        elif k == "in":
            j = b[1]
            src = win_d.rearrange("(k p) c -> p k c", p=128)
            if j in (0, 3, 4):
                o4 = dst.rearrange("p (k c) -> p k c", k=8)
            if j == 0:
                prep(o4, src[:, :, 0:512])
            elif j == 3:
                prep(o4, src[:, :, 768:1280])
            elif j == 4:
                prep(o4, src[:, :, 1280:1792])
            elif j == 5:
                o = dst[:, 0:1024].rearrange("p (k c) -> p k c", k=8)
                prep(o, src[:, :, 640:768])
            elif j == 2:
                o = dst[:, 0:1024].rearrange("p (k c) -> p k c", k=8)
                prep(o, src[:, :, 512:640])
        elif k == "oa":
            hf = b[1]
            o = dst[:, 0:2048].rearrange("p (s c) -> p s c", s=4)
            for pidx in range(4):
                for hh in range(2):
                    h = HEAD_PERM[2 * pidx + hh]
                    prep(o[hh * 64:(hh + 1) * 64, pidx, :], wout_d[h * 64:(h + 1) * 64, hf * 512:(hf + 1) * 512])
        elif k == "ol":
            hf = b[1]
            o = dst[:, 0:2048].rearrange("p (f c) -> p f c", f=4)
            i = wout_d[512:1024, hf * 512:(hf + 1) * 512].rearrange("(f p) c -> p f c", p=128)
            prep(o, i)
        elif k == "pg":
            hf = b[1]
            o = dst.rearrange("p (k c) -> p k c", k=8)
            i = wpg_d[:, hf * 512:(hf + 1) * 512].rearrange("(k p) c -> p k c", p=128)
            prep(o, i)
        elif k == "pp":
            hf = b[1]
            o = dst[:, 0:1024].rearrange("p (k c) -> p k c", k=2)
            i = wpp_d[:, hf * 512:(hf + 1) * 512].rearrange("(k p) c -> p k c", p=128)
            prep(o, i)


    setup_n = [0]

    def sdma(out, in_, nonc=False):
        setup_n[0] += 1
        o_ = dma("sp", out, in_, (), ("consts_%d" % setup_n[0],), "setup", nonc=nonc)
        P.last_w["consts"] = o_

    sdma(ident[:], ident_d)
    perm_f = ftA[0][:, 0:384].rearrange("p (a b) -> p a b", a=3)
    dma("sp", perm_f, perm_d, (), ("ftA0",), "permld")
    cp("dve", perm[:], perm_f, ("ftA0",), ("consts2",))
    for i in range(3):
        sdma(gbc[i][:], lng_d[i].partition_broadcast(128), nonc=True)
        sdma(bbc[i][:], lnb_d[i].partition_broadcast(128), nonc=True)
    for tap in range(4):
        sdma(cw[:, tap, :], convw_d[tap].rearrange("(c p) -> p c", p=128), nonc=True)
    sdma(cb[:], convb_d.rearrange("(c p) -> p c", p=128), nonc=True)
    sdma(hba[:], ba_d.rearrange("(c p) -> p c", p=128), nonc=True)
    sdma(hbx[:], bx_d.rearrange("(c p) -> p c", p=128), nonc=True)
    sdma(lam[:], lam_d.rearrange("(c p) -> p c", p=128), nonc=True)
    wn = 0
    for ax, src in enumerate((wa_d, wx_d)):
        P.final_sems.add("setup2_%d" % ax)
        stage = ftB[ax][:, :].rearrange("p (c d) -> p c d", c=4)
        mset("pool", ftB[ax][:, :], 0.0, ("ftB%d" % ax,))
        for cc in range(4):
            for hh in range(2):
                wn += 1
                o_ = dma("sp", stage[hh * 64:(hh + 1) * 64, cc, hh * 64:(hh + 1) * 64], src[cc * 2 + hh],
                         ("ftB%d" % ax,), ("wbdf_%d" % wn,), "setup2_%d" % ax, nonc=True)
                P.last_w["wbdf%d" % ax] = o_
        cp("pool", wbd[:, ax, :, :], stage, ("wbdf%d" % ax, "ftB%d" % ax), ("consts2", "ftB%d" % ax))
    mset("pool", ones_b[:], 1.0, ("consts2",))
    mset("pool", mhalf[:], -0.5, ("consts2",))
    sdma(sink64[:], sink_d.partition_broadcast(64), nonc=True)
    act(sink64[:], sink64[:], AF.Exp, ("consts",), ("esf",))
    for sl in range(8):
        h = HEAD_PERM[sl]
        cp("dve", esl[:, sl:sl + 1], sink64[:, h:h + 1], ("esf",), ("consts2",))
    mset("pool", Vb[:, :, :, 64:66], 1.0, ("Vb",))
    ts("dve", hba[:], hba[:], 0.5, None, ALU.mult, None, ("consts",), ("consts2",))
    ts("dve", hbx[:], hbx[:], 0.5, None, ALU.mult, None, ("consts",), ("consts2",))
    P.op("act", lambda e: e.activation(ltmp[:], lam[:], AF.Exp, scale=-1.0), ("consts",), ("lt",))
    lt2 = sb("lt2", [128, 4])
    lt3 = sb("lt3", [128, 4])
    lt4 = sb("lt4", [128, 4])
    ts("dve", lt2[:], ltmp[:], 2.0, None, ALU.add, None, ("lt",), ("lt2",))
    P.op("dve", lambda e: e.reciprocal(lt2[:], lt2[:]), ("lt2",), ("lt2",))
    tt("dve", lt2[:], lt2[:], ltmp[:], ALU.mult, ("lt2", "lt"), ("lt2",))
    tt("dve", lt3[:], lt2[:], lt2[:], ALU.mult, ("lt2",), ("lt3",))
    ts("dve", lt4[:], lt3[:], 1.0 / 13, 1.0 / 11, ALU.mult, ALU.add, ("lt3",), ("lt4",))
    for cst in (1.0 / 9, 1.0 / 7, 1.0 / 5, 1.0 / 3, 1.0):
        tt("dve", lt4[:], lt4[:], lt3[:], ALU.mult, ("lt4", "lt3"), ("lt4",))
        ts("dve", lt4[:], lt4[:], cst, None, ALU.add, None, ("lt4",), ("lt4",))
    tt("dve", lt4[:], lt4[:], lt2[:], ALU.mult, ("lt4", "lt2"), ("lt4",))
    ts("dve", cfull[:], lt4[:], -16.0, None, ALU.mult, None, ("lt4",), ("consts2",))
    ts("dve", chalf[:], lt4[:], -8.0, None, ALU.mult, None, ("lt4",), ("consts2",))

    CONST = ("consts", "consts2")

    ntile_total = ntiles_prompt + (1 if with_sample else 0)
    total_blocks = ntile_total * NBLK
    wstate = {"next": 0, "rel": set()}

    def w_pump():
        while wstate["next"] < total_blocks and (wstate["next"] < NS or (wstate["next"] - NS) in wstate["rel"]):
            i = wstate["next"]
            b = BLOCKS[i % NBLK]
            npart, nel = blk_extent(b)
            slot = i % NS
            if i < NBLK:
                prep_block(i)
            dma("sp", ring[slot][0:npart, 0:nel], ws_d[i % NBLK][0:npart, 0:nel],
                ("wsd%d" % (i % NBLK),), ("ring%d" % slot,), "ring%d" % slot)
            wstate["next"] += 1

    def w_acquire(g):
        w_pump()
        assert wstate["next"] > g, (g, wstate["next"])
        return ring[g % NS], "ring%d" % (g % NS)

    def w_release(g, total):
        wstate["rel"].add(g)
        w_pump()

    pending_T = []

    def flush_T():
        while pending_T:
            pending_T.pop(0)()

    def ffn(ti, which, T, nsub, nt, lnidx, g0):
        def up_chunk(j, rg, rname, jj, t0, t1):
            ss = range(t0 // 128, (t1 + 127) // 128)
            bG, bU = bank(), bank()
            W_ = t1 - t0
            for gu, bk in ((0, bG), (1, bU)):
                for k in range(8):
                    lhsT = rg[:, gu * 2048 + k * 256 + jj * 128: gu * 2048 + k * 256 + (jj + 1) * 128]
                    mm(psum[bk][:, 0:W_], lhsT, actT[:, k, t0:t1], k == 0, k == 7,
                       (rname,) + aT_names(nsub, (k,), ss), ("ps%d" % bk,))
            fa, fb = ftA[j % 2], ftB[j % 2]
            act(fa[:, 0:W_], psum[bG][:, 0:W_], AF.Tanh, ("ps%d" % bG,), ("ftA%d" % (j % 2),), scale=0.5)
            stt(fb[:, 0:W_], fa[:, 0:W_], 1.0, psum[bG][:, 0:W_], ALU.add, ALU.mult,
                ("ftA%d" % (j % 2), "ps%d" % bG), ("ftB%d" % (j % 2),))
            tt("dve", hid[:, j, t0:t1], fb[:, 0:W_], psum[bU][:, 0:W_], ALU.mult,
               ("ftB%d" % (j % 2), "ps%d" % bU), ("hid%d" % j,))

        NSPB = 3 if T == TILE else 0
        if not NSPB:
            flush_T()
        if NSPB:
            gsp = [g0 + BIDX[("up", which, b)] for b in range(NSPB)]
            rsp = [w_acquire(g) for g in gsp]
            for h_ in range(2):
                if h_ == 1:
                    flush_T()
                for b in range(NSPB):
                    for jj in range(2):
                        up_chunk(2 * b + jj, rsp[b][0], rsp[b][1], jj, h_ * 256, (h_ + 1) * 256)
            for g in gsp:
                w_release(g, total_blocks)
        for b in range(NSPB, 11):
            g = g0 + BIDX[("up", which, b)]
            rg, rname = w_acquire(g)
            for jj in range(2):
                up_chunk(2 * b + jj, rg, rname, jj, 0, T)
            w_release(g, total_blocks)
        for hf in range(2):
            gs = [g0 + BIDX[("dn", which, hf, gg)] for gg in range(3)]
            rgs = [w_acquire(g) for g in gs]
            for s in range(nsub):
                bk = bank()
                for f in range(NF):
                    rg, rname = rgs[f // 8]
                    mm(psum[bk][0:nt, :], hid[:, f, s * 128:s * 128 + nt],
                       rg[:, (f % 8) * 512:(f % 8 + 1) * 512], f == 0, f == NF - 1,
                       (rname, "hid%d" % f), ("ps%d" % bk,))
                hv = hbuf[s][0:nt, hf * 512:(hf + 1) * 512]
                stt(hv, hv, 4.0 * ALPHA, psum[bk][0:nt, :], ALU.mult, ALU.add,
                    ("hb%d" % s, "ps%d" % bk), ("hb%d" % s,))
                if hf == 1:
                    if s >= 2:
                        to_feature_major_s(s - 2, nt)
                    layer_norm(s, nt, lnidx, 16.0 * LN_EPS)
            for g in gs:
                w_release(g, total_blocks)
        for s in range(max(0, nsub - 2), nsub):
            pending_T.append(lambda s=s: to_feature_major_s(s, nt))

    def layer_norm(s, nt, lnidx, eps):
        hn = "hb%d" % s
        hv = hbuf[s]
        stats, mv, rstd, nb = stats_l[s], mv_l[s], rstd_l[s], nb_l[s]
        sn, mn, rn, nn = "stats%d" % s, "mv%d" % s, "rstd%d" % s, "nb%d" % s
        for c in range(2):
            P.op("dve", lambda e, c=c: e.bn_stats(stats[0:nt, c, :], hv[0:nt, c * 512:(c + 1) * 512]),
                 (hn,), (sn,))
        P.op("dve", lambda e: e.bn_aggr(mv[0:nt, :], stats[0:nt, :, :].rearrange("p a b -> p (a b)")),
             (sn,), (mn,))
        ts("pool", rstd[0:nt, :], mv[0:nt, 1:2], eps, None, ALU.add, None, (mn,), (rn,))
        tt("pool", rstd[0:nt, :], rstd[0:nt, :], mhalf[0:nt, :], ALU.pow, (rn,) + CONST, (rn,))
        stt(nb[0:nt, :], mv[0:nt, 0:1], -1.0, rstd[0:nt, :], ALU.mult, ALU.mult, (mn, rn), (nn,))
        act(hv[0:nt, :], hv[0:nt, :], AF.Identity, (hn, rn, nn), (hn,), bias=nb[0:nt, :], scale=rstd[0:nt, :])
        tt("dve", hv[0:nt, :], hv[0:nt, :], gbc[lnidx][0:nt, :], ALU.mult, (hn,) + CONST, (hn,))
        tt("pool", hv[0:nt, :], hv[0:nt, :], bbc[lnidx][0:nt, :], ALU.add, (hn,) + CONST, (hn,))

    def aT_names(nsub, ks=range(8), ss=None):
        ss = range(nsub) if ss is None else ss
        return tuple(sorted({"aT%d_%d" % (s_, k_ // 4) for s_ in ss for k_ in ks}))

    def to_feature_major_s(s, nt):
        for q in range(2):
            bk = bank()
            for kk in range(4):
                k = q * 4 + kk
                tp(psum[bk][:, kk * 128:kk * 128 + nt], hbuf[s][0:nt, k * 128:(k + 1) * 128],
                   ident[0:nt, 0:nt], ("hb%d" % s,) + CONST, ("ps%d" % bk,))
            src = psum[bk][:, :].rearrange("p (k t) -> p k t", k=4)[:, :, 0:nt]
            dst = actT[:, q * 4:(q + 1) * 4, s * 128:s * 128 + nt]
            if q == 0:
                cp("act", dst, src, ("ps%d" % bk,), ("aT%d_%d" % (s, q),))
            else:
                cp("dve", dst, src, ("ps%d" % bk,), ("aT%d_%d" % (s, q),))

    def to_feature_major(nsub, nt):
        for s in range(nsub):
            to_feature_major_s(s, nt)

    def tile_prog(ti, seq, tok0, T, pos0, first, last, nxt=None, preloaded=False):
        nsub = (T + 127) // 128
        nt = min(T, 128)
        g0 = ti * NBLK
        tag = "p" if seq == "prompt" else "s"

        if first:
            if seq == "prompt":
                mset("pool", xbuf[:, :, 0:4], 0.0, ["xb%d" % cc for cc in range(4)])
                mset("pool", hcar[:], 0.0, ("hcar",))
            else:
                for cc in range(4):
                    dma("sp", xbuf[:, cc, 1:4], sc_d[:, cc * 128:(cc + 1) * 128].rearrange("r p -> p r"),
                        (), ("xb%d" % cc,), "car%d" % cc, nonc=True)
                dma("sp", hcar[:], sh_d.rearrange("(c p) -> p c", p=128), (), ("hcar",), "carh", nonc=True)
                dma("sp", ckt[:, 0, :], ck_d, (), ("ftA1",), "ck")
                dma("sp", ckt[:, 1, 0:64], ck_d[:, 64:128], (), ("ftA1",), "ck", nonc=True)
                dma("sp", ckt[:, 1, 64:128], ck_d[:, 0:64], (), ("ftA1",), "ck", nonc=True)
                dma("sp", cvt, cv_d, (), ("ftB1",), "cv")
                for var in range(2):
                    bk = bank()
                    tp(psum[bk][:, 0:128], ckt[:, var, :], ident[:], ("ftA1",) + CONST, ("ps%d" % bk,))
                    cp("act", kT[:, var, 0:128], psum[bk][:, 0:128], ("ps%d" % bk,), ("kT",))
                cp("act", Vb[:, 0, :, 0:64], cvt.rearrange("p (a b) -> p a b", a=2), ("ftB1",), ("Vb",))

        if not preloaded:
            for s in range(nsub):
                dma("sp", hbuf[s][0:nt, :], x_d[tok0 + s * 128: tok0 + s * 128 + nt, :], (), ("hb%d" % s,), "hb%d" % s)
        dma("sp", pbuf[0:nt, 0:nsub, :],
            p_d[tok0:tok0 + T, :].rearrange("(s p) c -> p s c", p=nt), (), ("pbuf",), "pb")
        dma("sp", rope_c[:, 0:T], rope_d[0, :, pos0:pos0 + T], (), ("rope",), "rope")
        dma("sp", rope_s[:, 0:T], rope_d[1, :, pos0:pos0 + T], (), ("rope",), "rope")

        if STOP == 1:
            return
        for s in range(nsub):
            if nsub > 2 and s >= 2:
                pending_T.append(lambda s=s: to_feature_major_s(s, nt))
            else:
                to_feature_major_s(s, nt)
        ffn(ti, 1, T, nsub, nt, 0, g0)

        if STOP == 2:
            return
        def inproj_chunk(blk, ci, t0=0, t1=None):
            t1 = T if t1 is None else t1
            ss = range(t0 // 128, (t1 + 127) // 128)
            rg, rname = w_acquire(g0 + BIDX[("in", blk)])
            bk = bank()
            for k in range(8):
                cw_ = 128 if blk == 2 else 512
                mm(psum[bk][:, 0:t1 - t0], rg[:, k * cw_ + ci * 128:k * cw_ + (ci + 1) * 128], actT[:, k, t0:t1],
                   k == 0, k == 7, (rname,) + aT_names(nsub, (k,), ss), ("ps%d" % bk,))
            return bk

        halves = ((0, 256), (256, 512)) if T == TILE else ((0, T),)
        items = [(hi_, t0, t1, qc) for hi_, (t0, t1) in enumerate(halves) for qc in range(4)]
        pend = []

        def q_finish(it, bq, ri):
            hi_, t0, t1, qc = it
            W_ = t1 - t0
            bs = bank()
            mm(psum[bs][:, 0:W_], perm[:, 0, :], qb[ri % 2][:, 0:W_], True, True, ("qb%d" % (ri % 2),) + CONST,
               ("ps%d" % bs,))
            fa, fb = ftA[ri % 2], ftB[ri % 2]
            tt("dve", fa[:, 0:W_], psum[bq][:, 0:W_], rope_c[:, t0:t1], ALU.mult,
               ("ps%d" % bq, "rope"), ("ftA%d" % (ri % 2),))
            tt("dve", fb[:, 0:W_], psum[bs][:, 0:W_], rope_s[:, t0:t1], ALU.mult,
               ("ps%d" % bs, "rope"), ("ftB%d" % (ri % 2),))
            tt("pool", qT[:, qc, t0:t1], fa[:, 0:W_], fb[:, 0:W_], ALU.add,
               ("ftA%d" % (ri % 2), "ftB%d" % (ri % 2)), ("qT",))

        for ri, it in enumerate(items):
            hi_, t0, t1, qc = it
            if hi_ == len(halves) - 1 and qc == 0:
                flush_T()
            bq = inproj_chunk(0, qc, t0, t1)
            cp("act", qb[ri % 2][:, 0:t1 - t0], psum[bq][:, 0:t1 - t0], ("ps%d" % bq,), ("qb%d" % (ri % 2),))
            if pend:
                q_finish(*pend.pop())
            pend.append((it, bq, ri))
        q_finish(*pend.pop())
        w_release(g0 + BIDX[("in", 0)], total_blocks)
        if STOP == 21:
            return
        bkk = inproj_chunk(2, 0)
        cp("act", qb[0][:, 0:T], psum[bkk][:, 0:T], ("ps%d" % bkk,), ("qb0",))
        bks, bkd, bkds = bank(), bank(), bank()
        for pi_, bb in ((0, bks), (1, bkd), (2, bkds)):
            mm(psum[bb][:, 0:T], perm[:, pi_, :], qb[0][:, 0:T], True, True, ("qb0",) + CONST, ("ps%d" % bb,))
        for var, (bq, bs) in enumerate(((bkk, bks), (bkd, bkds))):
            fa, fb = ftA[var], ftB[var]
            tt("dve", fa[:, 0:T], psum[bq][:, 0:T], rope_c[:, 0:T], ALU.mult,
               ("ps%d" % bq, "rope"), ("ftA%d" % var,))
            tt("dve", fb[:, 0:T], psum[bs][:, 0:T], rope_s[:, 0:T], ALU.mult,
               ("ps%d" % bs, "rope"), ("ftB%d" % var,))
            tt("pool", kT[:, var, 128:128 + T], fa[:, 0:T], fb[:, 0:T], ALU.add,
               ("ftA%d" % var, "ftB%d" % var), ("kT",))
            if last and var == 0:
                nl = min(T, 128)
                tt("pool", kfin[:, 0:nl], fa[:, T - nl:T], fb[:, T - nl:T], ALU.add,
                   ("ftA0", "ftB0"), ("kfin",))
        w_release(g0 + BIDX[("in", 2)], total_blocks)
        if STOP == 22:
            return
        rg, rname = w_acquire(g0 + BIDX[("in", 5)])
        for s in range(nsub):
            bk = bank()
            for k in range(8):
                mm(psum[bk][0:nt, 0:128], actT[:, k, s * 128:s * 128 + nt], rg[:, k * 128:(k + 1) * 128],
                   k == 0, k == 7, (rname,) + aT_names(nsub, (k,), (s,)), ("ps%d" % bk,))
            cp("act", Vb[0:nt, 1 + s, :, 0:64], psum[bk][0:nt, 0:128].rearrange("p (a b) -> p a b", a=2),
               ("ps%d" % bk,), ("Vb",))
            if last and s == nsub - 1:
                cp("dve", vfin[0:nt, :], psum[bk][0:nt, 0:128], ("ps%d" % bk,), ("vfin",))
        w_release(g0 + BIDX[("in", 5)], total_blocks)

        if STOP == 3:
            return
        nchunk = T // 64

        NPT = len(PT)

        def attn_blks(c):
            if c % 2 == 0:
                blks = [(c // 2, 0, 128), (c // 2 + 1, 0, 64)]
            else:
                blks = [((c - 1) // 2, 64, 128), ((c + 1) // 2, 0, 128)]
            if first and seq == "prompt":
                blks = [b_ for b_ in blks if b_[0] >= 1]
            return blks

        astate = {}

        def attn_A(i):
            c, v = divmod(i, 2)
            blks = attn_blks(c)
            pi = i % NPT
            bSs = bank_pair()
            ptn = "PT%d" % pi
            for bi_, (slot, lo, hi) in enumerate(blks):
                for par in range(2):
                    var = 0 if v == par else 1
                    bS = bSs[par]
                    if lo == 0:
                        lhsT = kT[par * 64:(par + 1) * 64, var, slot * 128: slot * 128 + hi]
                        out = psum[bS][0:hi, bi_ * 128:(bi_ + 1) * 128]
                    else:
                        lhsT = kT[par * 64:(par + 1) * 64, var, slot * 128: slot * 128 + 128]
                        out = psum[bS][0:128, bi_ * 128:(bi_ + 1) * 128]
                    rhs = qT[par * 64:(par + 1) * 64, 2 * v:2 * v + 2, c * 64:(c + 1) * 64]
                    mm(out, lhsT, rhs, True, True, ("kT", "qT"), ("ps%d" % bS,))
            for bi_, (slot, lo, hi) in enumerate(blks):
                act(PT[pi][lo:hi, bi_, :].rearrange("p (a c) -> p a c", a=2),
                    psum_all[lo:hi, bSs[0]:bSs[0] + 2, bi_ * 128:(bi_ + 1) * 128], AF.Exp,
                    ("ps%d" % bSs[0], "ps%d" % bSs[1]), (ptn,), scale=0.125)

        def attn_B(i):
            c, v = divmod(i, 2)
            blks = attn_blks(c)
            pi = i % NPT
            ptn = "PT%d" % pi
            ai = i % 3
            bO = bank()
            for j in range(4):
                for bi_, (slot, lo, hi) in enumerate(blks):
                    mm(psum[bO][0:64, j * 68:j * 68 + 65], PT[pi][lo:hi, bi_, j * 64:(j + 1) * 64],
                       Vb[lo:hi, slot, v, 0:65], bi_ == 0, bi_ == len(blks) - 1, ("Vb", ptn), ("ps%d" % bO,))
            rn = "rec%d" % ai
            o4 = psum[bO][0:64, 0:272].rearrange("p (j d) -> p j d", j=4)
            tt("dve", rec[ai][:, :], o4[:, :, 64], esl[:, 4 * v:4 * v + 4], ALU.add,
               ("ps%d" % bO,) + CONST, (rn,))
            P.op("dve", lambda e, ai=ai: e.reciprocal(rec[ai][:, :], rec[ai][:, :]), (rn,), (rn,))
            an = "atm%d" % ai
            tt("dve", atm[ai][:, :].rearrange("p (j d) -> p j d", j=4), o4[:, :, 0:64],
               rec[ai][:, :].unsqueeze(2).broadcast_to([64, 4, 64]), ALU.mult, ("ps%d" % bO, rn), (an,))

        def attn_C(i):
            c, v = divmod(i, 2)
            ai = i % 3
            an = "atm%d" % ai
            bT = bank()
            for pr_ in range(2):
                tp(psum[bT][:, pr_ * 64:(pr_ + 1) * 64], atm[ai][:, pr_ * 128:(pr_ + 1) * 128], ident[0:64, 0:64],
                   (an,) + CONST, ("ps%d" % bT,))
            cp("act", attnT[:, 2 * v:2 * v + 2, c * 64:(c + 1) * 64],
               psum[bT][:, 0:128].rearrange("p (a t) -> p a t", a=2), ("ps%d" % bT,), ("attnT",))

        nitem = 2 * nchunk
        attn_steps = []
        for t_ in range(nitem + 3):
            def step(t_=t_):
                if t_ < nitem:
                    attn_A(t_)
                if 0 <= t_ - 2 < nitem:
                    attn_B(t_ - 2)
                if 0 <= t_ - 3 < nitem:
                    attn_C(t_ - 3)
            attn_steps.append(step)

        def lru_a1(cc, j):
            xn = "xb%d" % cc
            bx = inproj_chunk(3, cc)
            cp("act", xbuf[:, cc, 4:4 + T], psum[bx][:, 0:T], ("ps%d" % bx,), (xn,))
            xc = LXC[j]
            ts("dve", xc[:, 0:T], xbuf[:, cc, 1:1 + T], cw[:, 0, cc:cc + 1], cb[:, cc:cc + 1], ALU.mult, ALU.add,
               (xn,) + CONST, ("L_xc%d" % j,))
            for tap in range(1, 4):
                stt(xc[:, 0:T], xbuf[:, cc, 1 + tap:1 + tap + T], cw[:, tap, cc:cc + 1], xc[:, 0:T],
                    ALU.mult, ALU.add, (xn, "L_xc%d" % j) + CONST, ("L_xc%d" % j,))
            if last:
                dma("sp", outs_state[tag]["c"][:, cc * 128:(cc + 1) * 128].rearrange("r p -> p r"),
                    xbuf[:, cc, 1 + T:4 + T], (xn,), (), "st_c%d" % cc, nonc=True)
            cp("pool", xbuf[:, cc, 1:4], xbuf[:, cc, 1 + T:4 + T], (xn,), (xn,))
            cp("act", LXCB[j][:, 0:T], xc[:, 0:T], ("L_xc%d" % j,), ("L_xcb%d" % j,))

        def lru_a2(cc, j):
            xc = LXC[j]
            bA, bI = bank(), bank()
            mm(psum[bA][:, 0:T], wbd[:, 0, cc, :], LXCB[j][:, 0:T], True, True, ("L_xcb%d" % j,) + CONST, ("ps%d" % bA,))
            mm(psum[bI][:, 0:T], wbd[:, 1, cc, :], LXCB[j][:, 0:T], True, True, ("L_xcb%d" % j,) + CONST, ("ps%d" % bI,))
            act(L_tr[:, 0:T], psum[bA][:, 0:T], AF.Tanh, ("ps%d" % bA,) + CONST, ("L_tr",),
                bias=hba[:, cc:cc + 1], scale=0.5)
            act(L_ti[:, 0:T], psum[bI][:, 0:T], AF.Tanh, ("ps%d" % bI,) + CONST, ("L_ti",),
                bias=hbx[:, cc:cc + 1], scale=0.5)
            act(LA[:, j, 0:T], L_tr[:, 0:T], AF.Exp, ("L_tr",) + CONST, ("LA%d" % j,),
                bias=chalf[:, cc:cc + 1], scale=chalf[:, cc:cc + 1])
            act(LS[:, j, 0:T], L_tr[:, 0:T], AF.Exp, ("L_tr",) + CONST, ("LS%d" % j,),
                bias=cfull[:, cc:cc + 1], scale=cfull[:, cc:cc + 1])
            ts("pool", LS[:, j, 0:T], LS[:, j, 0:T], -1.0, 1.0, ALU.mult, ALU.add, ("LS%d" % j,), ("LS%d" % j,))
            stt(LU[:, j, 0:T], L_ti[:, 0:T], 1.0, xc[:, 0:T], ALU.add, ALU.mult, ("L_ti", "L_xc%d" % j), ("LU%d" % j,))

        def lru_sqrt():
            act(LS[:, :, 0:T], LS[:, :, 0:T], AF.Sqrt, ("LS0", "LS1"), ("LS0", "LS1"))

        def lru_c1(cc, j):
            stt(LU[:, j, 0:T], LU[:, j, 0:T], 0.5, LS[:, j, 0:T], ALU.mult, ALU.mult,
                ("LU%d" % j, "LS%d" % j), ("LU%d" % j,))
            stt(LU[:, j, 0:1], LA[:, j, 0:1], hcar[:, cc:cc + 1], LU[:, j, 0:1], ALU.mult, ALU.add,
                ("LA%d" % j, "LU%d" % j, "hcar"), ("LU%d" % j,))
            P.op("dve", lambda e: e.tensor_tensor_scan(LHS[j][:, 0:T], LA[:, j, 0:T], LU[:, j, 0:T],
                                                       0.0, ALU.mult, ALU.add),
                 ("LA%d" % j, "LU%d" % j), ("L_hs%d" % j,))
            cp("pool", hcar[:, cc:cc + 1], LHS[j][:, T - 1:T], ("L_hs%d" % j,), ("hcar",))

        def lru_c2(cc, j):
            bg = inproj_chunk(4, cc)
            cp("act", L_gb[:, 0:T], psum[bg][:, 0:T], ("ps%d" % bg,), ("L_gb",))
            act(L_g[:, 0:T], L_gb[:, 0:T], AF.Square, ("L_gb",), ("L_g",))
            ts("pool", L_g[:, 0:T], L_g[:, 0:T], 0.044715, 1.0, ALU.mult, ALU.add, ("L_g",), ("L_g",))
            tt("dve", L_g[:, 0:T], L_g[:, 0:T], L_gb[:, 0:T], ALU.mult, ("L_g", "L_gb"), ("L_g",))
            act(L_g2[:, 0:T], L_g[:, 0:T], AF.Tanh, ("L_g",), ("L_g2",), scale=0.7978845608028654)
            stt(L_g2[:, 0:T], L_g2[:, 0:T], 1.0, L_gb[:, 0:T], ALU.add, ALU.mult,
                ("L_g2", "L_gb"), ("L_g2",))
            stt(lruT[:, cc, 0:T], L_g2[:, 0:T], 0.5, LHS[j][:, 0:T], ALU.mult, ALU.mult,
                ("L_g2", "L_hs%d" % j), ("lruT",))

        lru_steps = []
        for pr in range(2):
            for j in range(2):
                lru_steps.append(lambda pr=pr, j=j: lru_a1(2 * pr + j, j))
            for j in range(2):
                lru_steps.append(lambda pr=pr, j=j: lru_a2(2 * pr + j, j))
            lru_steps.append(lru_sqrt)
            for j in range(2):
                lru_steps.append(lambda pr=pr, j=j: lru_c1(2 * pr + j, j))
            for j in range(2):
                lru_steps.append(lambda pr=pr, j=j: lru_c2(2 * pr + j, j))
        for i_ in range(max(len(attn_steps), len(lru_steps))):
            if i_ < len(attn_steps):
                attn_steps[i_]()
            if i_ < len(lru_steps):
                lru_steps[i_]()
        w_release(g0 + BIDX[("in", 3)], total_blocks)
        w_release(g0 + BIDX[("in", 4)], total_blocks)
        if last:
            dma("sp", outs_state[tag]["h"].rearrange("(c p) -> p c", p=128), hcar[:], ("hcar",), (),
                "st_h", nonc=True)

        if STOP == 5:
            return
        if last:
            nl = min(T, 128)
            bk = bank()
            tp(psum[bk][0:nl, 0:128], kfin[:, 0:nl], ident[:], ("kfin",) + CONST, ("ps%d" % bk,))
            cp("act", kfin_t[0:nl, :], psum[bk][0:nl, 0:128], ("ps%d" % bk,), ("kfin_t",))
            dma("sp", outs_state[tag]["k"][128 - nl:128, :], kfin_t[0:nl, :], ("kfin_t",), (), "st_k")
            dma("sp", outs_state[tag]["v"][128 - nl:128, :], vfin[0:nl, :], ("vfin",), (), "st_v")
            if nl < 128:
                dma("sp", outs_state[tag]["k"][0:128 - nl, :], ck_d[nl:128, :], (), (), "st_k2")
                dma("sp", outs_state[tag]["v"][0:128 - nl, :], cv_d[nl:128, :], (), (), "st_v2")
        else:
            cp("pool", kT[:, :, 0:128], kT[:, :, T:T + 128], ("kT",), ("kT",))
            cp("pool", Vb[:, 0, :, :], Vb[:, nsub, :, :], ("Vb",), ("Vb",))

        if STOP == 6:
            return
        for hf in range(2):
            ga = g0 + BIDX[("oa", hf)]
            gl = g0 + BIDX[("ol", hf)]
            ra, ran = w_acquire(ga)
            rl, rln = w_acquire(gl)
            for s in range(nsub):
                bk = bank()
                for sl in range(4):
                    mm(psum[bk][0:nt, :], attnT[:, sl, s * 128:s * 128 + nt], ra[:, sl * 512:(sl + 1) * 512],
                       sl == 0, False, (ran, "attnT"), ("ps%d" % bk,))
                for cc in range(4):
                    mm(psum[bk][0:nt, :], lruT[:, cc, s * 128:s * 128 + nt], rl[:, cc * 512:(cc + 1) * 512],
                       False, cc == 3, (rln, "lruT"), ("ps%d" % bk,))
                hv = hbuf[s][0:nt, hf * 512:(hf + 1) * 512]
                stt(hv, hv, ALPHA, psum[bk][0:nt, :], ALU.mult, ALU.add, ("hb%d" % s, "ps%d" % bk), ("hb%d" % s,))
                if hf == 1:
                    if s >= 2:
                        to_feature_major_s(s - 2, nt)
                    layer_norm(s, nt, 1, LN_EPS)
            w_release(ga, total_blocks)
            w_release(gl, total_blocks)
        for s in range(max(0, nsub - 2), nsub):
            pending_T.append(lambda s=s: to_feature_major_s(s, nt))

        if STOP == 7:
            return
        for s in range(nsub):
            bk = bank()
            for j in range(2):
                tp(psum[bk][:, j * 128:j * 128 + nt], pbuf[0:nt, s, j * 128:(j + 1) * 128], ident[0:nt, 0:nt],
                   ("pbuf",) + CONST, ("ps%d" % bk,))
            cp("act", pT[:, :, s * 128:s * 128 + nt],
               psum[bk][:, 0:256].rearrange("p (k t) -> p k t", k=2)[:, :, 0:nt], ("ps%d" % bk,), ("pT",))
        ffn(ti, 2, T, nsub, nt, 2, g0)

        if STOP == 8:
            return
        if nsub <= 2:
            flush_T()
        for hf in range(2):
            gg = g0 + BIDX[("pg", hf)]
            gp = g0 + BIDX[("pp", hf)]
            rg_, rgn = w_acquire(gg)
            rp_, rpn = w_acquire(gp)
            for s in range(nsub):
                if s == 2:
                    flush_T()
                bG, bP = bank(), bank()
                for k in range(8):
                    mm(psum[bG][0:nt, :], actT[:, k, s * 128:s * 128 + nt], rg_[:, k * 512:(k + 1) * 512],
                       k == 0, k == 7, (rgn,) + aT_names(nsub, (k,), (s,)), ("ps%d" % bG,))
                for k in range(2):
                    mm(psum[bP][0:nt, :], pT[:, k, s * 128:s * 128 + nt], rp_[:, k * 512:(k + 1) * 512],
                       k == 0, k == 1, (rpn, "pT"), ("ps%d" % bP,))
                fa, fb = ftA[s % 2], ftB[s % 2]
                act(fa[0:nt, :], psum[bG][0:nt, :], AF.Tanh, ("ps%d" % bG,), ("ftA%d" % (s % 2),), scale=0.5)
                stt(fb[0:nt, :], fa[0:nt, :], 1.0, psum[bP][0:nt, :], ALU.add, ALU.mult,
                    ("ftA%d" % (s % 2), "ps%d" % bP), ("ftB%d" % (s % 2),))
                hv = hbuf[s][0:nt, hf * 512:(hf + 1) * 512]
                stt(hv, fb[0:nt, :], 0.5, hv, ALU.mult, ALU.add, ("ftB%d" % (s % 2), "hb%d" % s), ("hb%d" % s,))
                if hf == 1:
                    dma("sp", y_d[tok0 + s * 128: tok0 + s * 128 + nt, :], hbuf[s][0:nt, :], ("hb%d" % s,), (),
                        "hb%d" % s)
                    if nxt is not None and s < nxt[2]:
                        ntok0, nnt = nxt[0], nxt[1]
                        dma("pool", hbuf[s][0:nnt, :], x_d[ntok0 + s * 128: ntok0 + s * 128 + nnt, :], (),
                            ("hb%d" % s,), "hbx%d" % s)
            w_release(gg, total_blocks)
            w_release(gp, total_blocks)

    ti = 0
    if "preponly" in KDBG:
        ntiles_prompt = 0
        with_sample = False
    if STOP > 0:
        with_sample = False
    for t in range(ntiles_prompt):
        if t + 1 < ntiles_prompt:
            nxt = ((t + 1) * TILE, 128, 4)
        elif with_sample:
            nxt = (ntiles_prompt * TILE, TS, 1)
        else:
            nxt = None
        tile_prog(ti, "prompt", t * TILE, TILE, t * TILE, t == 0, t == ntiles_prompt - 1, nxt=nxt, preloaded=(t > 0))
        ti += 1
    if with_sample:
        tile_prog(ti, "sample", ntiles_prompt * TILE, TS, SEQ, True, True, preloaded=(ntiles_prompt > 0))
        ti += 1

    with nc.Block() as block:
        P.emit(nc, block, esem, dsems)
    es.close()
    return nc, P


def rope_tables():
    half = 8
    inv = np.power(np.float32(ROPE_THETA), -np.arange(half, dtype=np.float32) * np.float32(2.0 / 16)).astype(np.float32)
    pos = np.concatenate([np.arange(SEQ), PAST_LEN + np.arange(TS)]).astype(np.float32)
    ang = (pos[None, :] * inv[:, None]).astype(np.float32)
    cos = np.cos(ang).astype(np.float32)
    sin = np.sin(ang).astype(np.float32)
    tab = np.zeros((2, 128, SEQ + TS), np.float32)
    tab[0] = 1.0
    for hh in range(2):
        b = hh * 64
        tab[0, b:b + 8] = cos
        tab[0, b + 8:b + 16] = cos
        tab[1, b:b + 8] = -sin
        tab[1, b + 8:b + 16] = sin
    return tab


def perm_tables():
    pm = np.zeros((128, 3, 128), np.float32)

    def partner(d):
        dd = d % 64
        if dd < 8:
            return d + 8
        if dd < 16:
            return d - 8
        return None

    for m in range(128):
        p_ = partner(m)
        if p_ is not None:
            pm[p_, 0, m] = 1.0
        sw = (m + 64) % 128
        pm[sw, 1, m] = 1.0
        p2 = partner(sw)
        if p2 is not None:
            pm[p2, 2, m] = 1.0
    return pm


_CACHE = {}


def kernel(x_prompt, x_sample, p_prompt, p_sample, cache_k, cache_v, state_conv, state_h,
           ffn1_wg, ffn1_wu, ffn1_wd, ln1_g, ln1_b, w_in, attn_sinks, conv_w, conv_b,
           lru_wa, lru_ba, lru_wx, lru_bx, lru_lambda, w_out, ln2_g, ln2_b,
           ffn2_wg, ffn2_wu, ffn2_wd, ln3_g, ln3_b, w_ple, w_ple_gate, _ntiles=SEQ // TILE):
    f = lambda a: np.ascontiguousarray(np.asarray(a, dtype=np.float32))
    n = 8
    ntp = _ntiles
    if "nc" not in _CACHE or _CACHE.get("ntp") != ntp:
        _CACHE["nc"] = build(ntp, True)[0]
        _CACHE["ntp"] = ntp
    nc = _CACHE["nc"]
    rope = rope_tables()
    ident = np.eye(128, dtype=np.float32)
    perm = perm_tables()
    shared = {
        "ffn1_wg": f(ffn1_wg[0]), "ffn1_wu": f(ffn1_wu[0]), "ffn1_wd": f(ffn1_wd[0]),
        "ffn2_wg": f(ffn2_wg[0]), "ffn2_wu": f(ffn2_wu[0]), "ffn2_wd": f(ffn2_wd[0]),
        "w_in": f(w_in[0]), "w_out": f(w_out[0]), "w_ple_gate": f(w_ple_gate[0]), "w_ple": f(w_ple[0]),
        "ln1_g": f(ln1_g[0]), "ln1_b": f(ln1_b[0]), "ln2_g": f(ln2_g[0]), "ln2_b": f(ln2_b[0]),
        "ln3_g": f(ln3_g[0]), "ln3_b": f(ln3_b[0]), "attn_sinks": f(attn_sinks[0]),
        "conv_w": f(conv_w[0]), "conv_b": f(conv_b[0]), "lru_wa": f(lru_wa[0]), "lru_ba": f(lru_ba[0]),
        "lru_wx": f(lru_wx[0]), "lru_bx": f(lru_bx[0]), "lru_lambda": f(lru_lambda[0]),
        "rope": rope, "ident": ident, "perm": perm,
    }
    L = ntp * TILE
    in_maps = []
    for b in range(n):
        m = dict(shared)
        m["x"] = np.concatenate([f(x_prompt[b, :L]), f(x_sample[b])], axis=0)
        m["p"] = np.concatenate([f(p_prompt[0, b, :L]), f(p_sample[0, b])], axis=0)
        m["cache_k"] = f(cache_k[0, b]).reshape(128, 128)
        m["cache_v"] = f(cache_v[0, b]).reshape(128, 128)
        m["state_conv"] = f(state_conv[0, b])
        m["state_h"] = f(state_h[0, b])
        in_maps.append(m)
    res = run_bass_kernel_spmd(nc, in_maps, core_ids=list(range(n)))
    R = res.results
    yp = np.stack([R[b]["y"][:L] for b in range(n)])
    ys = np.stack([R[b]["y"][L:] for b in range(n)])

    def st(name, shape):
        return np.stack([np.asarray(R[b][name]).reshape(shape) for b in range(n)])[None]

    return (yp.astype(np.float32), ys.astype(np.float32),
            st("newk_p", (128, 2, 64)), st("newv_p", (128, 2, 64)), st("conv_p", (3, 512)), st("h_p", (512,)),
            st("newk_s", (128, 2, 64)), st("newv_s", (128, 2, 64)), st("conv_s", (3, 512)), st("h_s", (512,)))
```

```python
import numpy as np
import ml_dtypes
from contextlib import ExitStack

import concourse.bass as bass
import concourse.mybir as mybir
from concourse.bass_utils import run_bass_kernel_spmd

F32 = mybir.dt.float32
BF16 = mybir.dt.bfloat16
AF = mybir.ActivationFunctionType
ALU = mybir.AluOpType

D = 1024
SEQ = 8192
TS = 64
DFF = 2816
NF = DFF // 128
INC = 1792
PLE = 256
WINDOW = 128
ALPHA = 2.0 ** 0.25
LN_EPS = 1e-5
ROPE_THETA = 500000.0
PAST_LEN = 4096
TILE = 512
NS = 6
SLOT = 4096
NBLK = 47

HEAD_PERM = [0, 2, 1, 3, 4, 6, 5, 7]


class Op:
    __slots__ = ("eng", "fn", "waits", "signal", "dsem", "dval", "idx", "cnt")

    def __init__(self, eng, fn):
        self.eng = eng
        self.fn = fn
        self.waits = []
        self.signal = False
        self.dsem = None
        self.dval = 0
        self.idx = 0
        self.cnt = 0


class Prog:
    ENGS = ("pe", "act", "dve", "pool", "sp")

    def __init__(self):
        self.streams = {e: [] for e in self.ENGS}
        self.last_w = {}
        self.readers = {}
        self.known = {e: {} for e in self.ENGS}
        self.dma_cnt = {}
        self.final_sems = set()

    def _dep(self, op, d):
        if d is None or d is op:
            return
        if d.dsem is not None:
            key = ("d", d.dsem)
            val = d.dval
            if d.dsem in self.final_sems:
                val = -1
                if self.known[op.eng].get(key, 0) == -1:
                    return
                self.known[op.eng][key] = -1
                op.waits.append((key, d, True))
                return
            if self.known[op.eng].get(key, 0) >= val:
                return
            self.known[op.eng][key] = val
            op.waits.append((key, d, False))
        else:
            if d.eng == op.eng and op.eng == "pe" and op.dsem is None:
                return
            key = ("e", d.eng)
            if self.known[op.eng].get(key, -1) >= d.idx:
                return
            self.known[op.eng][key] = d.idx
            d.signal = True
            op.waits.append((key, d, False))

    def op(self, eng, fn, reads=(), writes=(), dsem=None):
        o = Op(eng, fn)
        st = self.streams[eng]
        o.idx = len(st)
        if dsem is not None:
            o.dsem = dsem
            self.dma_cnt[dsem] = self.dma_cnt.get(dsem, 0) + 16
            o.dval = self.dma_cnt[dsem]
        deps = []
        for n in reads:
            deps.append(self.last_w.get(n))
            if n.startswith("ps"):
                deps.extend(r for r in self.readers.get(n, ()) if r.eng != eng)
        for n in writes:
            deps.append(self.last_w.get(n))
            deps.extend(self.readers.get(n, ()))
        best = {}
        for d in deps:
            if d is None or d is o:
                continue
            if d.dsem is not None:
                key = ("d", d.dsem)
                if key not in best or best[key].dval < d.dval:
                    best[key] = d
            else:
                key = ("e", d.eng)
                if key not in best or best[key].idx < d.idx:
                    best[key] = d
        for d in best.values():
            self._dep(o, d)
        for n in reads:
            self.readers.setdefault(n, []).append(o)
        for n in writes:
            self.last_w[n] = o
            self.readers[n] = []
        st.append(o)
        return o

    def emit(self, nc, block, esem, dsems):
        for e in self.ENGS:
            c = 0
            for o in self.streams[e]:
                if o.dsem is None and o.signal:
                    c += 1
                    o.cnt = c
        prog = self

        def run(engname):
            def body(eng):
                for o in prog.streams[engname]:
                    for key, d, fin in o.waits:
                        if key[0] == "d":
                            v = prog.dma_cnt[d.dsem] if fin else d.dval
                            eng.wait_ge(dsems[d.dsem], v)
                        else:
                            eng.wait_ge(esem[d.eng], d.cnt)
                    ins = o.fn(eng)
                    if o.dsem is not None:
                        ins.then_inc(dsems[o.dsem], 16)
                    elif o.signal:
                        ins.then_inc(esem[engname], 1)
                if engname == "sp":
                    for k, v in prog.dma_cnt.items():
                        eng.wait_ge(dsems[k], v)
            return body

        block.tensor(run("pe"))
        block.scalar(run("act"))
        block.vector(run("dve"))
        block.gpsimd(run("pool"))
        block.sync(run("sp"))


def block_table():
    t = []
    for ffn in (1, 2):
        pass
    blocks = []
    for b in range(11):
        blocks.append(("up", 1, b))
    for hf in range(2):
        for g in range(3):
            blocks.append(("dn", 1, hf, g))
    for i in (0, 2, 3, 4, 5):
        blocks.append(("in", i))
    for hf in range(2):
        blocks.append(("oa", hf))
        blocks.append(("ol", hf))
    for b in range(11):
        blocks.append(("up", 2, b))
    for hf in range(2):
        for g in range(3):
            blocks.append(("dn", 2, hf, g))
    for hf in range(2):
        blocks.append(("pg", hf))
        blocks.append(("pp", hf))
    assert len(blocks) == NBLK
    return blocks


BLOCKS = block_table()
BIDX = {b: i for i, b in enumerate(BLOCKS)}


def blk_extent(b):
    k = b[0]
    if k == "up":
        return 128, 4096
    if k == "dn":
        nf = 8 if b[3] < 2 else 6
        return 128, nf * 512
    if k == "in":
        return (128, 4096) if b[1] in (0, 3, 4) else (128, 1024)
    if k == "oa":
        return 128, 2048
    if k == "ol":
        return 128, 2048
    if k == "pg":
        return 128, 4096
    if k == "pp":
        return 128, 1024
    raise ValueError(b)


def build(ntiles_prompt=SEQ // TILE, with_sample=True):
    NTOK = ntiles_prompt * TILE + (TS if with_sample else 0)
    NPOS = SEQ + TS
    nc = bass.Bass("TRN2", target_bir_lowering=False)
    P = Prog()
    es = ExitStack()

    def din(name, shape, dt=F32):
        return nc.dram_tensor(name, list(shape), dt, kind="ExternalInput").ap()

    def dout(name, shape, dt=F32):
        return nc.dram_tensor(name, list(shape), dt, kind="ExternalOutput").ap()

    x_d = din("x", [NTOK, D])
    p_d = din("p", [NTOK, PLE])
    ck_d = din("cache_k", [128, 128])
    cv_d = din("cache_v", [128, 128])
    sc_d = din("state_conv", [3, 512])
    sh_d = din("state_h", [512])
    wg_d = {1: din("ffn1_wg", [D, DFF]), 2: din("ffn2_wg", [D, DFF])}
    wu_d = {1: din("ffn1_wu", [D, DFF]), 2: din("ffn2_wu", [D, DFF])}
    wd_d = {1: din("ffn1_wd", [DFF, D]), 2: din("ffn2_wd", [DFF, D])}
    win_d = din("w_in", [D, INC])
    wout_d = din("w_out", [D, D])
    wpg_d = din("w_ple_gate", [D, D])
    wpp_d = din("w_ple", [PLE, D])
    lng_d = [din("ln%d_g" % i, [D]) for i in (1, 2, 3)]
    lnb_d = [din("ln%d_b" % i, [D]) for i in (1, 2, 3)]
    sink_d = din("attn_sinks", [8])
    convw_d = din("conv_w", [4, 512])
    convb_d = din("conv_b", [512])
    wa_d = din("lru_wa", [8, 64, 64])
    ba_d = din("lru_ba", [512])
    wx_d = din("lru_wx", [8, 64, 64])
    bx_d = din("lru_bx", [512])
    lam_d = din("lru_lambda", [512])
    rope_d = din("rope", [2, 128, NPOS])
    ident_d = din("ident", [128, 128])
    perm_d = din("perm", [128, 3, 128])

    y_d = dout("y", [NTOK, D])
    outs_state = {}
    for tag in ("p", "s"):
        outs_state[tag] = dict(
            k=dout("newk_" + tag, [128, 128]), v=dout("newv_" + tag, [128, 128]),
            c=dout("conv_" + tag, [3, 512]), h=dout("h_" + tag, [512]))

    ws_d = nc.dram_tensor("wstream", [NBLK, 128, SLOT], BF16, kind="Internal").ap()

    def sb(name, shape, dt=F32):
        return es.enter_context(nc.sbuf_tensor(name, list(shape), dt))

    hbuf = [sb("hbuf%d" % s, [128, D]) for s in range(4)]
    actT = sb("actT", [128, 8, TILE], BF16)
    hid = sb("hid", [128, NF, TILE], BF16)
    ring = [sb("ring%d" % i, [128, SLOT], BF16) for i in range(NS)]
    qT = sb("qT", [128, 4, TILE], BF16)
    kT = sb("kT", [128, 2, 128 + TILE], BF16)
    Vb = sb("Vb", [128, 5, 2, 66], BF16)
    xbuf = sb("xbuf", [128, 4, 4 + TILE])
    attnT = sb("attnT", [128, 4, TILE], BF16)
    atm = [sb("atm%d" % i, [64, 256]) for i in range(3)]
    esl = sb("esl", [64, 8])
    sink64 = sb("sink64", [64, 8])
    lruT = sb("lruT", [128, 4, TILE], BF16)
    PT = [sb("PT%d" % i, [128, 2, 256], BF16) for i in range(4)]
    rec = [sb("rec%d" % i, [64, 4]) for i in range(3)]
    rope_c = sb("rope_c", [128, TILE])
    rope_s = sb("rope_s", [128, TILE])
    gbc = [sb("gbc%d" % i, [128, D]) for i in range(3)]
    bbc = [sb("bbc%d" % i, [128, D]) for i in range(3)]
    pbuf = sb("pbuf", [128, 4, PLE])
    pT = sb("pT", [128, 2, TILE], BF16)
    ftA = [sb("ftA%d" % i, [128, TILE]) for i in range(2)]
    ftB = [sb("ftB%d" % i, [128, TILE]) for i in range(2)]
    ident = sb("ident_sb", [128, 128])
    perm = sb("perm_b", [128, 3, 128], BF16)
    qb = [sb("qb%d" % i, [128, TILE], BF16) for i in range(2)]
    ones_b = sb("ones_b", [128, 64], BF16)
    cw = sb("cw", [128, 4, 4])
    cb = sb("cb", [128, 4])
    hba = sb("hba", [128, 4])
    hbx = sb("hbx", [128, 4])
    lam = sb("lam", [128, 4])
    chalf = sb("chalf", [128, 4])
    cfull = sb("cfull", [128, 4])
    ltmp = sb("ltmp", [128, 4])
    wbd = sb("wbd", [128, 2, 4, 128], BF16)
    hcar = sb("hcar", [128, 4])
    LXC = [sb("L_xc%d" % i, [128, TILE]) for i in range(2)]
    LXCB = [sb("L_xcb%d" % i, [128, TILE], BF16) for i in range(2)]
    L_tr = sb("L_tr", [128, TILE])
    LA = sb("LA", [128, 2, TILE])
    LS = sb("LS", [128, 2, TILE])
    LU = sb("LU", [128, 2, TILE])
    L_ti = sb("L_ti", [128, TILE])
    LHS = [sb("L_hs%d" % i, [128, TILE]) for i in range(2)]
    L_g = sb("L_g", [128, TILE])
    L_g2 = sb("L_g2", [128, TILE])
    stats_l = [sb("stats%d" % i, [128, 2, 6]) for i in range(4)]
    mv_l = [sb("mv%d" % i, [128, 2]) for i in range(4)]
    rstd_l = [sb("rstd%d" % i, [128, 1]) for i in range(4)]
    nb_l = [sb("nb%d" % i, [128, 1]) for i in range(4)]
    mhalf = sb("mhalf", [128, 1])
    kfin = sb("kfin", [128, 128])
    kfin_t = sb("kfin_t", [128, 128])
    vfin = sb("vfin", [128, 128])
    ckt = sb("ckt", [128, 2, 128])
    cvt = sb("cvt", [128, 128])

    psum_all = es.enter_context(nc.psum_tensor("ps_all", [128, 8, 512], F32))
    psum = [psum_all[:, i, :] for i in range(8)]
    bank_ctr = [0]

    def bank():
        b = bank_ctr[0] % 8
        bank_ctr[0] += 1
        return b

    def bank_pair():
        if bank_ctr[0] % 8 == 7:
            bank_ctr[0] += 1
        return bank(), bank()

    esem = {e: es.enter_context(nc.semaphore("s_" + e)) for e in Prog.ENGS}
    dsem_names = (["ring%d" % i for i in range(NS)] + ["hb%d" % s for s in range(4)] +
                  ["pb", "rope", "setup", "prep", "st_small", "ck", "cv", "car"])
    class _DS(dict):
        def __missing__(self, n):
            v = es.enter_context(nc.semaphore("d_" + n))
            self[n] = v
            return v
    dsems = _DS()
    for n in dsem_names:
        dsems[n]
    P.final_sems.add("setup")

    def dma(eng, out, in_, reads, writes, sem, nonc=False):
        if nonc:
            def fn(e, out=out, in_=in_):
                with nc.allow_non_contiguous_dma(reason="tiny strided state"):
                    return e.dma_start(out=out, in_=in_)
        else:
            def fn(e, out=out, in_=in_):
                return e.dma_start(out=out, in_=in_)
        return P.op(eng, fn, reads, writes, dsem=sem)

    def mm(out, lhsT, rhs, start, stop, reads, writes):
        return P.op("pe", lambda e: e.matmul(out, lhsT, rhs, start=start, stop=stop), reads, writes)

    def tp(out, in_, idn, reads, writes):
        return P.op("pe", lambda e: e.transpose(out, in_, idn), reads, writes)

    def act(out, in_, func, reads, writes, bias=None, scale=None):
        kw = {}
        if bias is not None:
            kw["bias"] = bias
        if scale is not None:
            kw["scale"] = scale
        return P.op("act", lambda e: e.activation(out, in_, func, **kw), reads, writes)

    def tt(eng, out, a, b, op, reads, writes):
        return P.op(eng, lambda e: e.tensor_tensor(out, a, b, op), reads, writes)

    def ts(eng, out, a, s1, s2, op0, op1, reads, writes):
        if op1 is None:
            return P.op(eng, lambda e: e.tensor_scalar(out, a, s1, None, op0), reads, writes)
        return P.op(eng, lambda e: e.tensor_scalar(out, a, s1, s2, op0, op1), reads, writes)

    def stt(out, a, s, b, op0, op1, reads, writes):
        return P.op("dve", lambda e: e.scalar_tensor_tensor(out, a, s, b, op0, op1), reads, writes)

    def cp(eng, out, in_, reads, writes):
        if eng == "act":
            return P.op("act", lambda e: e.copy(out, in_), reads, writes)
        return P.op(eng, lambda e: e.tensor_copy(out, in_), reads, writes)

    def mset(eng, out, val, writes):
        return P.op(eng, lambda e: e.memset(out, val), (), writes)

    prep_n = [0]
    prep_cur = [0]

    import os as _os
    KDBG = _os.environ.get("KDBG", "")
    STOP = int(_os.environ.get("KSTOP", "0"))

    def prep(out, in_):
        if "noprep" in KDBG:
            return
        def fn(e, out=out, in_=in_):
            with nc.allow_non_contiguous_dma(reason="one-time weight re-blocking"):
                return e.dma_start(out=out, in_=in_)
        prep_n[0] += 1
        o_ = P.op("pool", fn, (), ("wsdx%d" % prep_n[0],), dsem="prep%d" % prep_cur[0])
        P.last_w["wsd%d" % prep_cur[0]] = o_

    def wsv(bi):
        return ws_d[bi]

    def prep_block(bi):
        b = BLOCKS[bi]
        prep_cur[0] = bi
        k = b[0]
        dst = wsv(bi)
        if k == "up":
            f, blk = b[1], b[2]
            for gu, src in enumerate((wg_d[f], wu_d[f])):
                o = dst[:, gu * 2048:(gu + 1) * 2048].rearrange("p (k c) -> p k c", k=8)
                i = src[:, blk * 256:(blk + 1) * 256].rearrange("(k p) c -> p k c", p=128)
                prep(o, i)
        elif k == "dn":
            f, hf, g = b[1], b[2], b[3]
            nf = 8 if g < 2 else 6
            o = dst[:, 0:nf * 512].rearrange("p (f c) -> p f c", f=nf)
            i = wd_d[f][g * 8 * 128:(g * 8 + nf) * 128, hf * 512:(hf + 1) * 512].rearrange(
                "(f p) c -> p f c", p=128)
            prep(o, i)
        elif k == "in":
            j = b[1]
            src = win_d.rearrange("(k p) c -> p k c", p=128)
            if j in (0, 3, 4):
                o4 = dst.rearrange("p (k c) -> p k c", k=8)
            if j == 0:
                prep(o4, src[:, :, 0:512])
            elif j == 3:
                prep(o4, src[:, :, 768:1280])
            elif j == 4:
                prep(o4, src[:, :, 1280:1792])
            elif j == 5:
                o = dst[:, 0:1024].rearrange("p (k c) -> p k c", k=8)
                prep(o, src[:, :, 640:768])
            elif j == 2:
                o = dst[:, 0:1024].rearrange("p (k c) -> p k c", k=8)
                prep(o, src[:, :, 512:640])
        elif k == "oa":
            hf = b[1]
            o = dst[:, 0:2048].rearrange("p (s c) -> p s c", s=4)
            for pidx in range(4):
                for hh in range(2):
                    h = HEAD_PERM[2 * pidx + hh]
                    prep(o[hh * 64:(hh + 1) * 64, pidx, :], wout_d[h * 64:(h + 1) * 64, hf * 512:(hf + 1) * 512])
        elif k == "ol":
            hf = b[1]
            o = dst[:, 0:2048].rearrange("p (f c) -> p f c", f=4)
            i = wout_d[512:1024, hf * 512:(hf + 1) * 512].rearrange("(f p) c -> p f c", p=128)
            prep(o, i)
        elif k == "pg":
            hf = b[1]
            o = dst.rearrange("p (k c) -> p k c", k=8)
            i = wpg_d[:, hf * 512:(hf + 1) * 512].rearrange("(k p) c -> p k c", p=128)
            prep(o, i)
        elif k == "pp":
            hf = b[1]
            o = dst[:, 0:1024].rearrange("p (k c) -> p k c", k=2)
            i = wpp_d[:, hf * 512:(hf + 1) * 512].rearrange("(k p) c -> p k c", p=128)
            prep(o, i)


    setup_n = [0]

    def sdma(out, in_, nonc=False):
        setup_n[0] += 1
        o_ = dma("sp", out, in_, (), ("consts_%d" % setup_n[0],), "setup", nonc=nonc)
        P.last_w["consts"] = o_

    sdma(ident[:], ident_d)
    perm_f = ftA[0][:, 0:384].rearrange("p (a b) -> p a b", a=3)
    dma("sp", perm_f, perm_d, (), ("ftA0",), "permld")
    cp("dve", perm[:], perm_f, ("ftA0",), ("consts2",))
    for i in range(3):
        sdma(gbc[i][:], lng_d[i].partition_broadcast(128), nonc=True)
        sdma(bbc[i][:], lnb_d[i].partition_broadcast(128), nonc=True)
    for tap in range(4):
        sdma(cw[:, tap, :], convw_d[tap].rearrange("(c p) -> p c", p=128), nonc=True)
    sdma(cb[:], convb_d.rearrange("(c p) -> p c", p=128), nonc=True)
    sdma(hba[:], ba_d.rearrange("(c p) -> p c", p=128), nonc=True)
    sdma(hbx[:], bx_d.rearrange("(c p) -> p c", p=128), nonc=True)
    sdma(lam[:], lam_d.rearrange("(c p) -> p c", p=128), nonc=True)
    wn = 0
    for ax, src in enumerate((wa_d, wx_d)):
        P.final_sems.add("setup2_%d" % ax)
        stage = ftB[ax][:, :].rearrange("p (c d) -> p c d", c=4)
        mset("pool", ftB[ax][:, :], 0.0, ("ftB%d" % ax,))
        for cc in range(4):
            for hh in range(2):
                wn += 1
                o_ = dma("sp", stage[hh * 64:(hh + 1) * 64, cc, hh * 64:(hh + 1) * 64], src[cc * 2 + hh],
                         ("ftB%d" % ax,), ("wbdf_%d" % wn,), "setup2_%d" % ax, nonc=True)
                P.last_w["wbdf%d" % ax] = o_
        cp("pool", wbd[:, ax, :, :], stage, ("wbdf%d" % ax, "ftB%d" % ax), ("consts2", "ftB%d" % ax))
    mset("pool", ones_b[:], 1.0, ("consts2",))
    mset("pool", mhalf[:], -0.5, ("consts2",))
    sdma(sink64[:], sink_d.partition_broadcast(64), nonc=True)
    act(sink64[:], sink64[:], AF.Exp, ("consts",), ("esf",))
    for sl in range(8):
        h = HEAD_PERM[sl]
        cp("dve", esl[:, sl:sl + 1], sink64[:, h:h + 1], ("esf",), ("consts2",))
    mset("pool", Vb[:, :, :, 64:66], 1.0, ("Vb",))
    ts("dve", hba[:], hba[:], 0.5, None, ALU.mult, None, ("consts",), ("consts2",))
    ts("dve", hbx[:], hbx[:], 0.5, None, ALU.mult, None, ("consts",), ("consts2",))
    P.op("act", lambda e: e.activation(ltmp[:], lam[:], AF.Exp, scale=-1.0), ("consts",), ("lt",))
    lt2 = sb("lt2", [128, 4])
    lt3 = sb("lt3", [128, 4])
    lt4 = sb("lt4", [128, 4])
    ts("dve", lt2[:], ltmp[:], 2.0, None, ALU.add, None, ("lt",), ("lt2",))
    P.op("dve", lambda e: e.reciprocal(lt2[:], lt2[:]), ("lt2",), ("lt2",))
    tt("dve", lt2[:], lt2[:], ltmp[:], ALU.mult, ("lt2", "lt"), ("lt2",))
    tt("dve", lt3[:], lt2[:], lt2[:], ALU.mult, ("lt2",), ("lt3",))
    ts("dve", lt4[:], lt3[:], 1.0 / 13, 1.0 / 11, ALU.mult, ALU.add, ("lt3",), ("lt4",))
    for cst in (1.0 / 9, 1.0 / 7, 1.0 / 5, 1.0 / 3, 1.0):
        tt("dve", lt4[:], lt4[:], lt3[:], ALU.mult, ("lt4", "lt3"), ("lt4",))
        ts("dve", lt4[:], lt4[:], cst, None, ALU.add, None, ("lt4",), ("lt4",))
    tt("dve", lt4[:], lt4[:], lt2[:], ALU.mult, ("lt4", "lt2"), ("lt4",))
    ts("dve", cfull[:], lt4[:], -16.0, None, ALU.mult, None, ("lt4",), ("consts2",))
    ts("dve", chalf[:], lt4[:], -8.0, None, ALU.mult, None, ("lt4",), ("consts2",))

    CONST = ("consts", "consts2")

    ntile_total = ntiles_prompt + (1 if with_sample else 0)
    total_blocks = ntile_total * NBLK
    wstate = {"next": 0, "rel": set()}

    def w_pump():
        while wstate["next"] < total_blocks and (wstate["next"] < NS or (wstate["next"] - NS) in wstate["rel"]):
            i = wstate["next"]
            b = BLOCKS[i % NBLK]
            npart, nel = blk_extent(b)
            slot = i % NS
            if i < NBLK:
                prep_block(i)
            dma("sp", ring[slot][0:npart, 0:nel], ws_d[i % NBLK][0:npart, 0:nel],
                ("wsd%d" % (i % NBLK),), ("ring%d" % slot,), "ring%d" % slot)
            wstate["next"] += 1

    def w_acquire(g):
        w_pump()
        assert wstate["next"] > g, (g, wstate["next"])
        return ring[g % NS], "ring%d" % (g % NS)

    def w_release(g, total):
        wstate["rel"].add(g)
        w_pump()

    pending_T = []

    def flush_T():
        while pending_T:
            pending_T.pop(0)()

    def ffn(ti, which, T, nsub, nt, lnidx, g0):
        def up_chunk(j, rg, rname, jj, t0, t1):
            ss = range(t0 // 128, (t1 + 127) // 128)
            bG, bU = bank(), bank()
            W_ = t1 - t0
            for gu, bk in ((0, bG), (1, bU)):
                for k in range(8):
                    lhsT = rg[:, gu * 2048 + k * 256 + jj * 128: gu * 2048 + k * 256 + (jj + 1) * 128]
                    mm(psum[bk][:, 0:W_], lhsT, actT[:, k, t0:t1], k == 0, k == 7,
                       (rname,) + aT_names(nsub, (k,), ss), ("ps%d" % bk,))
            fa, fb = ftA[j % 2], ftB[j % 2]
            act(fa[:, 0:W_], psum[bG][:, 0:W_], AF.Tanh, ("ps%d" % bG,), ("ftA%d" % (j % 2),), scale=0.5)
            stt(fb[:, 0:W_], fa[:, 0:W_], 1.0, psum[bG][:, 0:W_], ALU.add, ALU.mult,
                ("ftA%d" % (j % 2), "ps%d" % bG), ("ftB%d" % (j % 2),))
            tt("dve", hid[:, j, t0:t1], fb[:, 0:W_], psum[bU][:, 0:W_], ALU.mult,
               ("ftB%d" % (j % 2), "ps%d" % bU), ("hid%d" % j,))

        NSPB = 3 if T == TILE else 0
        if not NSPB:
            flush_T()
        if NSPB:
            gsp = [g0 + BIDX[("up", which, b)] for b in range(NSPB)]
            rsp = [w_acquire(g) for g in gsp]
            for h_ in range(2):
                if h_ == 1:
                    flush_T()
                for b in range(NSPB):
                    for jj in range(2):
                        up_chunk(2 * b + jj, rsp[b][0], rsp[b][1], jj, h_ * 256, (h_ + 1) * 256)
            for g in gsp:
                w_release(g, total_blocks)
        for b in range(NSPB, 11):
            g = g0 + BIDX[("up", which, b)]
            rg, rname = w_acquire(g)
            for jj in range(2):
                up_chunk(2 * b + jj, rg, rname, jj, 0, T)
            w_release(g, total_blocks)
        for hf in range(2):
            gs = [g0 + BIDX[("dn", which, hf, gg)] for gg in range(3)]
            rgs = [w_acquire(g) for g in gs]
            for s in range(nsub):
                bk = bank()
                for f in range(NF):
                    rg, rname = rgs[f // 8]
                    mm(psum[bk][0:nt, :], hid[:, f, s * 128:s * 128 + nt],
                       rg[:, (f % 8) * 512:(f % 8 + 1) * 512], f == 0, f == NF - 1,
                       (rname, "hid%d" % f), ("ps%d" % bk,))
                hv = hbuf[s][0:nt, hf * 512:(hf + 1) * 512]
                stt(hv, hv, 4.0 * ALPHA, psum[bk][0:nt, :], ALU.mult, ALU.add,
                    ("hb%d" % s, "ps%d" % bk), ("hb%d" % s,))
                if hf == 1:
                    if s >= 2:
                        to_feature_major_s(s - 2, nt)
                    layer_norm(s, nt, lnidx, 16.0 * LN_EPS)
            for g in gs:
                w_release(g, total_blocks)
        for s in range(max(0, nsub - 2), nsub):
            pending_T.append(lambda s=s: to_feature_major_s(s, nt))

    def layer_norm(s, nt, lnidx, eps):
        hn = "hb%d" % s
        hv = hbuf[s]
        stats, mv, rstd, nb = stats_l[s], mv_l[s], rstd_l[s], nb_l[s]
        sn, mn, rn, nn = "stats%d" % s, "mv%d" % s, "rstd%d" % s, "nb%d" % s
        for c in range(2):
            P.op("dve", lambda e, c=c: e.bn_stats(stats[0:nt, c, :], hv[0:nt, c * 512:(c + 1) * 512]),
                 (hn,), (sn,))
        P.op("dve", lambda e: e.bn_aggr(mv[0:nt, :], stats[0:nt, :, :].rearrange("p a b -> p (a b)")),
             (sn,), (mn,))
        ts("pool", rstd[0:nt, :], mv[0:nt, 1:2], eps, None, ALU.add, None, (mn,), (rn,))
        tt("pool", rstd[0:nt, :], rstd[0:nt, :], mhalf[0:nt, :], ALU.pow, (rn,) + CONST, (rn,))
        stt(nb[0:nt, :], mv[0:nt, 0:1], -1.0, rstd[0:nt, :], ALU.mult, ALU.mult, (mn, rn), (nn,))
        act(hv[0:nt, :], hv[0:nt, :], AF.Identity, (hn, rn, nn), (hn,), bias=nb[0:nt, :], scale=rstd[0:nt, :])
        tt("dve", hv[0:nt, :], hv[0:nt, :], gbc[lnidx][0:nt, :], ALU.mult, (hn,) + CONST, (hn,))
        tt("pool", hv[0:nt, :], hv[0:nt, :], bbc[lnidx][0:nt, :], ALU.add, (hn,) + CONST, (hn,))

    def aT_names(nsub, ks=range(8), ss=None):
        ss = range(nsub) if ss is None else ss
        return tuple(sorted({"aT%d_%d" % (s_, k_ // 4) for s_ in ss for k_ in ks}))

    def to_feature_major_s(s, nt):
        for q in range(2):
            bk = bank()
            for kk in range(4):
                k = q * 4 + kk
                tp(psum[bk][:, kk * 128:kk * 128 + nt], hbuf[s][0:nt, k * 128:(k + 1) * 128],
                   ident[0:nt, 0:nt], ("hb%d" % s,) + CONST, ("ps%d" % bk,))
            src = psum[bk][:, :].rearrange("p (k t) -> p k t", k=4)[:, :, 0:nt]
            dst = actT[:, q * 4:(q + 1) * 4, s * 128:s * 128 + nt]
            if q == 0:
                cp("act", dst, src, ("ps%d" % bk,), ("aT%d_%d" % (s, q),))
            else:
                cp("dve", dst, src, ("ps%d" % bk,), ("aT%d_%d" % (s, q),))

    def to_feature_major(nsub, nt):
        for s in range(nsub):
            to_feature_major_s(s, nt)

    def tile_prog(ti, seq, tok0, T, pos0, first, last, nxt=None, preloaded=False):
        nsub = (T + 127) // 128
        nt = min(T, 128)
        g0 = ti * NBLK
        tag = "p" if seq == "prompt" else "s"

        if first:
            if seq == "prompt":
                mset("pool", xbuf[:, :, 0:4], 0.0, ["xb%d" % cc for cc in range(4)])
                mset("pool", hcar[:], 0.0, ("hcar",))
            else:
                for cc in range(4):
                    dma("sp", xbuf[:, cc, 1:4], sc_d[:, cc * 128:(cc + 1) * 128].rearrange("r p -> p r"),
                        (), ("xb%d" % cc,), "car%d" % cc, nonc=True)
                dma("sp", hcar[:], sh_d.rearrange("(c p) -> p c", p=128), (), ("hcar",), "carh", nonc=True)
                dma("sp", ckt[:, 0, :], ck_d, (), ("ckt",), "ck")
                dma("sp", ckt[:, 1, 0:64], ck_d[:, 64:128], (), ("ckt",), "ck", nonc=True)
                dma("sp", ckt[:, 1, 64:128], ck_d[:, 0:64], (), ("ckt",), "ck", nonc=True)
                dma("sp", cvt[:], cv_d, (), ("cvt",), "cv")
                for var in range(2):
                    bk = bank()
                    tp(psum[bk][:, 0:128], ckt[:, var, :], ident[:], ("ckt",) + CONST, ("ps%d" % bk,))
                    cp("act", kT[:, var, 0:128], psum[bk][:, 0:128], ("ps%d" % bk,), ("kT",))
                cp("act", Vb[:, 0, :, 0:64], cvt[:].rearrange("p (a b) -> p a b", a=2), ("cvt",), ("Vb",))

        if not preloaded:
            for s in range(nsub):
                dma("sp", hbuf[s][0:nt, :], x_d[tok0 + s * 128: tok0 + s * 128 + nt, :], (), ("hb%d" % s,), "hb%d" % s)
        dma("sp", pbuf[0:nt, 0:nsub, :],
            p_d[tok0:tok0 + T, :].rearrange("(s p) c -> p s c", p=nt), (), ("pbuf",), "pb")
        dma("sp", rope_c[:, 0:T], rope_d[0, :, pos0:pos0 + T], (), ("rope",), "rope")
        dma("sp", rope_s[:, 0:T], rope_d[1, :, pos0:pos0 + T], (), ("rope",), "rope")

        if STOP == 1:
            return
        for s in range(nsub):
            if nsub > 2 and s >= 2:
                pending_T.append(lambda s=s: to_feature_major_s(s, nt))
            else:
                to_feature_major_s(s, nt)
        ffn(ti, 1, T, nsub, nt, 0, g0)

        if STOP == 2:
            return
        def inproj_chunk(blk, ci, t0=0, t1=None):
            t1 = T if t1 is None else t1
            ss = range(t0 // 128, (t1 + 127) // 128)
            rg, rname = w_acquire(g0 + BIDX[("in", blk)])
            bk = bank()
            for k in range(8):
                cw_ = 128 if blk == 2 else 512
                mm(psum[bk][:, 0:t1 - t0], rg[:, k * cw_ + ci * 128:k * cw_ + (ci + 1) * 128], actT[:, k, t0:t1],
                   k == 0, k == 7, (rname,) + aT_names(nsub, (k,), ss), ("ps%d" % bk,))
            return bk

        halves = ((0, 256), (256, 512)) if T == TILE else ((0, T),)
        items = [(hi_, t0, t1, qc) for hi_, (t0, t1) in enumerate(halves) for qc in range(4)]
        pend = []

        def q_finish(it, bq, ri):
            hi_, t0, t1, qc = it
            W_ = t1 - t0
            bs = bank()
            mm(psum[bs][:, 0:W_], perm[:, 0, :], qb[ri % 2][:, 0:W_], True, True, ("qb%d" % (ri % 2),) + CONST,
               ("ps%d" % bs,))
            fa, fb = ftA[ri % 2], ftB[ri % 2]
            tt("dve", fa[:, 0:W_], psum[bq][:, 0:W_], rope_c[:, t0:t1], ALU.mult,
               ("ps%d" % bq, "rope"), ("ftA%d" % (ri % 2),))
            tt("dve", fb[:, 0:W_], psum[bs][:, 0:W_], rope_s[:, t0:t1], ALU.mult,
               ("ps%d" % bs, "rope"), ("ftB%d" % (ri % 2),))
            tt("pool", qT[:, qc, t0:t1], fa[:, 0:W_], fb[:, 0:W_], ALU.add,
               ("ftA%d" % (ri % 2), "ftB%d" % (ri % 2)), ("qT%d" % hi_,))

        def q_start(ri, it):
            hi_, t0, t1, qc = it
            bq = inproj_chunk(0, qc, t0, t1)
            cp("act", qb[ri % 2][:, 0:t1 - t0], psum[bq][:, 0:t1 - t0], ("ps%d" % bq,), ("qb%d" % (ri % 2),))
            return (it, bq, ri)

        first_items = [(ri, it) for ri, it in enumerate(items) if it[0] == 0]
        late_items = [(ri, it) for ri, it in enumerate(items) if it[0] != 0]
        if len(halves) == 1:
            flush_T()
        for ri, it in first_items:
            st_ = q_start(ri, it)
            if pend:
                q_finish(*pend.pop())
            pend.append(st_)
        q_finish(*pend.pop())
        flush_T()
        q_late_steps = []
        for a_ in range(0, len(late_items), 2):
            def qstep(a_=a_):
                sts = [q_start(ri, it) for ri, it in late_items[a_:a_ + 2]]
                for st_ in sts:
                    q_finish(*st_)
            q_late_steps.append(qstep)
        if STOP == 21:
            return
        bkk = inproj_chunk(2, 0)
        cp("act", qb[0][:, 0:T], psum[bkk][:, 0:T], ("ps%d" % bkk,), ("qb0",))
        bks, bkd, bkds = bank(), bank(), bank()
        for pi_, bb in ((0, bks), (1, bkd), (2, bkds)):
            mm(psum[bb][:, 0:T], perm[:, pi_, :], qb[0][:, 0:T], True, True, ("qb0",) + CONST, ("ps%d" % bb,))
        for var, (bq, bs) in enumerate(((bkk, bks), (bkd, bkds))):
            fa, fb = ftA[var], ftB[var]
            tt("dve", fa[:, 0:T], psum[bq][:, 0:T], rope_c[:, 0:T], ALU.mult,
               ("ps%d" % bq, "rope"), ("ftA%d" % var,))
            tt("dve", fb[:, 0:T], psum[bs][:, 0:T], rope_s[:, 0:T], ALU.mult,
               ("ps%d" % bs, "rope"), ("ftB%d" % var,))
            tt("pool", kT[:, var, 128:128 + T], fa[:, 0:T], fb[:, 0:T], ALU.add,
               ("ftA%d" % var, "ftB%d" % var), ("kT",))
            if last and var == 0:
                nl = min(T, 128)
                tt("pool", kfin[:, 0:nl], fa[:, T - nl:T], fb[:, T - nl:T], ALU.add,
                   ("ftA0", "ftB0"), ("kfin",))
        w_release(g0 + BIDX[("in", 2)], total_blocks)
        if STOP == 22:
            return
        rg, rname = w_acquire(g0 + BIDX[("in", 5)])
        for s in range(nsub):
            bk = bank()
            for k in range(8):
                mm(psum[bk][0:nt, 0:128], actT[:, k, s * 128:s * 128 + nt], rg[:, k * 128:(k + 1) * 128],
                   k == 0, k == 7, (rname,) + aT_names(nsub, (k,), (s,)), ("ps%d" % bk,))
            cp("act", Vb[0:nt, 1 + s, :, 0:64], psum[bk][0:nt, 0:128].rearrange("p (a b) -> p a b", a=2),
               ("ps%d" % bk,), ("Vb",))
            if last and s == nsub - 1:
                cp("dve", vfin[0:nt, :], psum[bk][0:nt, 0:128], ("ps%d" % bk,), ("vfin",))
        w_release(g0 + BIDX[("in", 5)], total_blocks)

        if STOP == 3:
            return
        nchunk = T // 64

        NPT = len(PT)

        def attn_blks(c):
            if c % 2 == 0:
                blks = [(c // 2, 0, 128), (c // 2 + 1, 0, 64)]
            else:
                blks = [((c - 1) // 2, 64, 128), ((c + 1) // 2, 0, 128)]
            if first and seq == "prompt":
                blks = [b_ for b_ in blks if b_[0] >= 1]
            return blks

        astate = {}

        def attn_A(i):
            c, v = divmod(i, 2)
            blks = attn_blks(c)
            pi = i % NPT
            bSs = bank_pair()
            ptn = "PT%d" % pi
            for bi_, (slot, lo, hi) in enumerate(blks):
                for par in range(2):
                    var = 0 if v == par else 1
                    bS = bSs[par]
                    if lo == 0:
                        lhsT = kT[par * 64:(par + 1) * 64, var, slot * 128: slot * 128 + hi]
                        out = psum[bS][0:hi, bi_ * 128:(bi_ + 1) * 128]
                    else:
                        lhsT = kT[par * 64:(par + 1) * 64, var, slot * 128: slot * 128 + 128]
                        out = psum[bS][0:128, bi_ * 128:(bi_ + 1) * 128]
                    rhs = qT[par * 64:(par + 1) * 64, 2 * v:2 * v + 2, c * 64:(c + 1) * 64]
                    mm(out, lhsT, rhs, True, True, ("kT", "qT%d" % ((c * 64) // 256 if T == TILE else 0)), ("ps%d" % bS,))
            for bi_, (slot, lo, hi) in enumerate(blks):
                act(PT[pi][lo:hi, bi_, :].rearrange("p (a c) -> p a c", a=2),
                    psum_all[lo:hi, bSs[0]:bSs[0] + 2, bi_ * 128:(bi_ + 1) * 128], AF.Exp,
                    ("ps%d" % bSs[0], "ps%d" % bSs[1]), (ptn,), scale=0.125)

        def attn_B(i):
            c, v = divmod(i, 2)
            blks = attn_blks(c)
            pi = i % NPT
            ptn = "PT%d" % pi
            ai = i % 3
            bO = bank()
            for j in range(4):
                for bi_, (slot, lo, hi) in enumerate(blks):
                    mm(psum[bO][0:64, j * 68:j * 68 + 65], PT[pi][lo:hi, bi_, j * 64:(j + 1) * 64],
                       Vb[lo:hi, slot, v, 0:65], bi_ == 0, bi_ == len(blks) - 1, ("Vb", ptn), ("ps%d" % bO,))
            rn = "rec%d" % ai
            o4 = psum[bO][0:64, 0:272].rearrange("p (j d) -> p j d", j=4)
            tt("dve", rec[ai][:, :], o4[:, :, 64], esl[:, 4 * v:4 * v + 4], ALU.add,
               ("ps%d" % bO,) + CONST, (rn,))
            P.op("dve", lambda e, ai=ai: e.reciprocal(rec[ai][:, :], rec[ai][:, :]), (rn,), (rn,))
            an = "atm%d" % ai
            tt("dve", atm[ai][:, :].rearrange("p (j d) -> p j d", j=4), o4[:, :, 0:64],
               rec[ai][:, :].unsqueeze(2).broadcast_to([64, 4, 64]), ALU.mult, ("ps%d" % bO, rn), (an,))

        def attn_C(i):
            c, v = divmod(i, 2)
            ai = i % 3
            an = "atm%d" % ai
            bT = bank()
            for pr_ in range(2):
                tp(psum[bT][:, pr_ * 64:(pr_ + 1) * 64], atm[ai][:, pr_ * 128:(pr_ + 1) * 128], ident[0:64, 0:64],
                   (an,) + CONST, ("ps%d" % bT,))
            cp("act", attnT[:, 2 * v:2 * v + 2, c * 64:(c + 1) * 64],
               psum[bT][:, 0:128].rearrange("p (a t) -> p a t", a=2), ("ps%d" % bT,), ("attnT",))

        nitem = 2 * nchunk
        attn_steps = []
        for t_ in range(nitem + 3):
            def step(t_=t_):
                if t_ < nitem:
                    attn_A(t_)
                if 0 <= t_ - 2 < nitem:
                    attn_B(t_ - 2)
                if 0 <= t_ - 3 < nitem:
                    attn_C(t_ - 3)
            attn_steps.append(step)

        def lru_a1(cc, j):
            xn = "xb%d" % cc
            bx = inproj_chunk(3, cc)
            cp("act", xbuf[:, cc, 4:4 + T], psum[bx][:, 0:T], ("ps%d" % bx,), (xn,))
            xc = LXC[j]
            ts("dve", xc[:, 0:T], xbuf[:, cc, 1:1 + T], cw[:, 0, cc:cc + 1], cb[:, cc:cc + 1], ALU.mult, ALU.add,
               (xn,) + CONST, ("L_xc%d" % j,))
            for tap in range(1, 4):
                stt(xc[:, 0:T], xbuf[:, cc, 1 + tap:1 + tap + T], cw[:, tap, cc:cc + 1], xc[:, 0:T],
                    ALU.mult, ALU.add, (xn, "L_xc%d" % j) + CONST, ("L_xc%d" % j,))
            if last:
                dma("sp", outs_state[tag]["c"][:, cc * 128:(cc + 1) * 128].rearrange("r p -> p r"),
                    xbuf[:, cc, 1 + T:4 + T], (xn,), (), "st_c%d" % cc, nonc=True)
            cp("pool", xbuf[:, cc, 1:4], xbuf[:, cc, 1 + T:4 + T], (xn,), (xn,))
            cp("act", LXCB[j][:, 0:T], xc[:, 0:T], ("L_xc%d" % j,), ("L_xcb%d" % j,))

        def lru_a2(cc, j):
            xc = LXC[j]
            bA, bI = bank(), bank()
            mm(psum[bA][:, 0:T], wbd[:, 0, cc, :], LXCB[j][:, 0:T], True, True, ("L_xcb%d" % j,) + CONST, ("ps%d" % bA,))
            mm(psum[bI][:, 0:T], wbd[:, 1, cc, :], LXCB[j][:, 0:T], True, True, ("L_xcb%d" % j,) + CONST, ("ps%d" % bI,))
            act(L_tr[:, 0:T], psum[bA][:, 0:T], AF.Tanh, ("ps%d" % bA,) + CONST, ("L_tr",),
                bias=hba[:, cc:cc + 1], scale=0.5)
            act(L_ti[:, 0:T], psum[bI][:, 0:T], AF.Tanh, ("ps%d" % bI,) + CONST, ("L_ti",),
                bias=hbx[:, cc:cc + 1], scale=0.5)
            act(LA[:, j, 0:T], L_tr[:, 0:T], AF.Exp, ("L_tr",) + CONST, ("LA%d" % j,),
                bias=chalf[:, cc:cc + 1], scale=chalf[:, cc:cc + 1])
            act(LS[:, j, 0:T], L_tr[:, 0:T], AF.Exp, ("L_tr",) + CONST, ("LS%d" % j,),
                bias=cfull[:, cc:cc + 1], scale=cfull[:, cc:cc + 1])
            ts("pool", LS[:, j, 0:T], LS[:, j, 0:T], -1.0, 1.0, ALU.mult, ALU.add, ("LS%d" % j,), ("LS%d" % j,))
            stt(LU[:, j, 0:T], L_ti[:, 0:T], 1.0, xc[:, 0:T], ALU.add, ALU.mult, ("L_ti", "L_xc%d" % j), ("LU%d" % j,))

        def lru_sqrt():
            act(LS[:, :, 0:T], LS[:, :, 0:T], AF.Sqrt, ("LS0", "LS1"), ("LS0", "LS1"))

        def lru_c1(cc, j):
            stt(LU[:, j, 0:T], LU[:, j, 0:T], 0.5, LS[:, j, 0:T], ALU.mult, ALU.mult,
                ("LU%d" % j, "LS%d" % j), ("LU%d" % j,))
            stt(LU[:, j, 0:1], LA[:, j, 0:1], hcar[:, cc:cc + 1], LU[:, j, 0:1], ALU.mult, ALU.add,
                ("LA%d" % j, "LU%d" % j, "hcar"), ("LU%d" % j,))
            P.op("dve", lambda e: e.tensor_tensor_scan(LHS[j][:, 0:T], LA[:, j, 0:T], LU[:, j, 0:T],
                                                       0.0, ALU.mult, ALU.add),
                 ("LA%d" % j, "LU%d" % j), ("L_hs%d" % j,))
            cp("pool", hcar[:, cc:cc + 1], LHS[j][:, T - 1:T], ("L_hs%d" % j,), ("hcar",))

        def lru_c2(cc, j):
            bg = inproj_chunk(4, cc)
            act(L_g[:, 0:T], psum[bg][:, 0:T], AF.Square, ("ps%d" % bg,), ("L_g",))
            ts("pool", L_g[:, 0:T], L_g[:, 0:T], 0.044715, 1.0, ALU.mult, ALU.add, ("L_g",), ("L_g",))
            tt("dve", L_g[:, 0:T], L_g[:, 0:T], psum[bg][:, 0:T], ALU.mult, ("L_g", "ps%d" % bg), ("L_g",))
            act(L_g2[:, 0:T], L_g[:, 0:T], AF.Tanh, ("L_g",), ("L_g2",), scale=0.7978845608028654)
            stt(L_g2[:, 0:T], L_g2[:, 0:T], 1.0, psum[bg][:, 0:T], ALU.add, ALU.mult,
                ("L_g2", "ps%d" % bg), ("L_g2",))
            stt(lruT[:, cc, 0:T], L_g2[:, 0:T], 0.5, LHS[j][:, 0:T], ALU.mult, ALU.mult,
                ("L_g2", "L_hs%d" % j), ("lruT",))

        lru_steps = []
        for pr in range(2):
            for j in range(2):
                lru_steps.append(lambda pr=pr, j=j: lru_a1(2 * pr + j, j))
            for j in range(2):
                lru_steps.append(lambda pr=pr, j=j: lru_a2(2 * pr + j, j))
            lru_steps.append(lru_sqrt)
            for j in range(2):
                lru_steps.append(lambda pr=pr, j=j: lru_c1(2 * pr + j, j))
            for j in range(2):
                lru_steps.append(lambda pr=pr, j=j: lru_c2(2 * pr + j, j))
        for i_ in range(max(len(attn_steps), len(lru_steps), len(q_late_steps))):
            if i_ < len(attn_steps):
                attn_steps[i_]()
            if i_ < len(q_late_steps):
                q_late_steps[i_]()
                if i_ == len(q_late_steps) - 1:
                    w_release(g0 + BIDX[("in", 0)], total_blocks)
            if i_ < len(lru_steps):
                lru_steps[i_]()
        if not q_late_steps:
            w_release(g0 + BIDX[("in", 0)], total_blocks)
        w_release(g0 + BIDX[("in", 3)], total_blocks)
        w_release(g0 + BIDX[("in", 4)], total_blocks)
        if last:
            dma("sp", outs_state[tag]["h"].rearrange("(c p) -> p c", p=128), hcar[:], ("hcar",), (),
                "st_h", nonc=True)

        if STOP == 5:
            return
        if last:
            nl = min(T, 128)
            bk = bank()
            tp(psum[bk][0:nl, 0:128], kfin[:, 0:nl], ident[:], ("kfin",) + CONST, ("ps%d" % bk,))
            cp("act", kfin_t[0:nl, :], psum[bk][0:nl, 0:128], ("ps%d" % bk,), ("kfin_t",))
            dma("sp", outs_state[tag]["k"][128 - nl:128, :], kfin_t[0:nl, :], ("kfin_t",), (), "st_k")
            dma("sp", outs_state[tag]["v"][128 - nl:128, :], vfin[0:nl, :], ("vfin",), (), "st_v")
            if nl < 128:
                dma("sp", outs_state[tag]["k"][0:128 - nl, :], ck_d[nl:128, :], (), (), "st_k2")
                dma("sp", outs_state[tag]["v"][0:128 - nl, :], cv_d[nl:128, :], (), (), "st_v2")
        else:
            cp("pool", kT[:, :, 0:128], kT[:, :, T:T + 128], ("kT",), ("kT",))
            cp("pool", Vb[:, 0, :, :], Vb[:, nsub, :, :], ("Vb",), ("Vb",))

        if STOP == 6:
            return
        for hf in range(2):
            ga = g0 + BIDX[("oa", hf)]
            gl = g0 + BIDX[("ol", hf)]
            ra, ran = w_acquire(ga)
            rl, rln = w_acquire(gl)
            for s in range(nsub):
                bk = bank()
                for sl in range(4):
                    mm(psum[bk][0:nt, :], attnT[:, sl, s * 128:s * 128 + nt], ra[:, sl * 512:(sl + 1) * 512],
                       sl == 0, False, (ran, "attnT"), ("ps%d" % bk,))
                for cc in range(4):
                    mm(psum[bk][0:nt, :], lruT[:, cc, s * 128:s * 128 + nt], rl[:, cc * 512:(cc + 1) * 512],
                       False, cc == 3, (rln, "lruT"), ("ps%d" % bk,))
                hv = hbuf[s][0:nt, hf * 512:(hf + 1) * 512]
                stt(hv, hv, ALPHA, psum[bk][0:nt, :], ALU.mult, ALU.add, ("hb%d" % s, "ps%d" % bk), ("hb%d" % s,))
                if hf == 1:
                    if s >= 2:
                        to_feature_major_s(s - 2, nt)
                    layer_norm(s, nt, 1, LN_EPS)
            w_release(ga, total_blocks)
            w_release(gl, total_blocks)
        for s in range(max(0, nsub - 2), nsub):
            pending_T.append(lambda s=s: to_feature_major_s(s, nt))

        if STOP == 7:
            return
        for s in range(nsub):
            bk = bank()
            for j in range(2):
                tp(psum[bk][:, j * 128:j * 128 + nt], pbuf[0:nt, s, j * 128:(j + 1) * 128], ident[0:nt, 0:nt],
                   ("pbuf",) + CONST, ("ps%d" % bk,))
            cp("act", pT[:, :, s * 128:s * 128 + nt],
               psum[bk][:, 0:256].rearrange("p (k t) -> p k t", k=2)[:, :, 0:nt], ("ps%d" % bk,), ("pT",))
        ffn(ti, 2, T, nsub, nt, 2, g0)

        if STOP == 8:
            return
        if nsub <= 2:
            flush_T()
        for hf in range(2):
            gg = g0 + BIDX[("pg", hf)]
            gp = g0 + BIDX[("pp", hf)]
            rg_, rgn = w_acquire(gg)
            rp_, rpn = w_acquire(gp)
            for s in range(nsub):
                if s == 2:
                    flush_T()
                bG, bP = bank(), bank()
                for k in range(8):
                    mm(psum[bG][0:nt, :], actT[:, k, s * 128:s * 128 + nt], rg_[:, k * 512:(k + 1) * 512],
                       k == 0, k == 7, (rgn,) + aT_names(nsub, (k,), (s,)), ("ps%d" % bG,))
                for k in range(2):
                    mm(psum[bP][0:nt, :], pT[:, k, s * 128:s * 128 + nt], rp_[:, k * 512:(k + 1) * 512],
                       k == 0, k == 1, (rpn, "pT"), ("ps%d" % bP,))
                fa, fb = ftA[s % 2], ftB[s % 2]
                act(fa[0:nt, :], psum[bG][0:nt, :], AF.Tanh, ("ps%d" % bG,), ("ftA%d" % (s % 2),), scale=0.5)
                stt(fb[0:nt, :], fa[0:nt, :], 1.0, psum[bP][0:nt, :], ALU.add, ALU.mult,
                    ("ftA%d" % (s % 2), "ps%d" % bP), ("ftB%d" % (s % 2),))
                hv = hbuf[s][0:nt, hf * 512:(hf + 1) * 512]
                stt(hv, fb[0:nt, :], 0.5, hv, ALU.mult, ALU.add, ("ftB%d" % (s % 2), "hb%d" % s), ("hb%d" % s,))
                if hf == 1:
                    dma("sp", y_d[tok0 + s * 128: tok0 + s * 128 + nt, :], hbuf[s][0:nt, :], ("hb%d" % s,), (),
                        "hb%d" % s)
                    if nxt is not None and s < nxt[2]:
                        ntok0, nnt = nxt[0], nxt[1]
                        dma("pool", hbuf[s][0:nnt, :], x_d[ntok0 + s * 128: ntok0 + s * 128 + nnt, :], (),
                            ("hb%d" % s,), "hbx%d" % s)
            w_release(gg, total_blocks)
            w_release(gp, total_blocks)

    ti = 0
    if "preponly" in KDBG:
        ntiles_prompt = 0
        with_sample = False
    if STOP > 0:
        with_sample = False
    for t in range(ntiles_prompt):
        if t + 1 < ntiles_prompt:
            nxt = ((t + 1) * TILE, 128, 4)
        elif with_sample:
            nxt = (ntiles_prompt * TILE, TS, 1)
        else:
            nxt = None
        tile_prog(ti, "prompt", t * TILE, TILE, t * TILE, t == 0, t == ntiles_prompt - 1, nxt=nxt, preloaded=(t > 0))
        ti += 1
    if with_sample:
        tile_prog(ti, "sample", ntiles_prompt * TILE, TS, SEQ, True, True, preloaded=(ntiles_prompt > 0))
        ti += 1

    with nc.Block() as block:
        P.emit(nc, block, esem, dsems)
    es.close()
    return nc, P


def rope_tables():
    half = 8
    inv = np.power(np.float32(ROPE_THETA), -np.arange(half, dtype=np.float32) * np.float32(2.0 / 16)).astype(np.float32)
    pos = np.concatenate([np.arange(SEQ), PAST_LEN + np.arange(TS)]).astype(np.float32)
    ang = (pos[None, :] * inv[:, None]).astype(np.float32)
    cos = np.cos(ang).astype(np.float32)
    sin = np.sin(ang).astype(np.float32)
    tab = np.zeros((2, 128, SEQ + TS), np.float32)
    tab[0] = 1.0
    for hh in range(2):
        b = hh * 64
        tab[0, b:b + 8] = cos
        tab[0, b + 8:b + 16] = cos
        tab[1, b:b + 8] = -sin
        tab[1, b + 8:b + 16] = sin
    return tab


def perm_tables():
    pm = np.zeros((128, 3, 128), np.float32)

    def partner(d):
        dd = d % 64
        if dd < 8:
            return d + 8
        if dd < 16:
            return d - 8
        return None

    for m in range(128):
        p_ = partner(m)
        if p_ is not None:
            pm[p_, 0, m] = 1.0
        sw = (m + 64) % 128
        pm[sw, 1, m] = 1.0
        p2 = partner(sw)
        if p2 is not None:
            pm[p2, 2, m] = 1.0
    return pm


_CACHE = {}


def kernel(x_prompt, x_sample, p_prompt, p_sample, cache_k, cache_v, state_conv, state_h,
           ffn1_wg, ffn1_wu, ffn1_wd, ln1_g, ln1_b, w_in, attn_sinks, conv_w, conv_b,
           lru_wa, lru_ba, lru_wx, lru_bx, lru_lambda, w_out, ln2_g, ln2_b,
           ffn2_wg, ffn2_wu, ffn2_wd, ln3_g, ln3_b, w_ple, w_ple_gate, _ntiles=SEQ // TILE):
    f = lambda a: np.ascontiguousarray(np.asarray(a, dtype=np.float32))
    n = 8
    ntp = _ntiles
    if "nc" not in _CACHE or _CACHE.get("ntp") != ntp:
        _CACHE["nc"] = build(ntp, True)[0]
        _CACHE["ntp"] = ntp
    nc = _CACHE["nc"]
    rope = rope_tables()
    ident = np.eye(128, dtype=np.float32)
    perm = perm_tables()
    shared = {
        "ffn1_wg": f(ffn1_wg[0]), "ffn1_wu": f(ffn1_wu[0]), "ffn1_wd": f(ffn1_wd[0]),
        "ffn2_wg": f(ffn2_wg[0]), "ffn2_wu": f(ffn2_wu[0]), "ffn2_wd": f(ffn2_wd[0]),
        "w_in": f(w_in[0]), "w_out": f(w_out[0]), "w_ple_gate": f(w_ple_gate[0]), "w_ple": f(w_ple[0]),
        "ln1_g": f(ln1_g[0]), "ln1_b": f(ln1_b[0]), "ln2_g": f(ln2_g[0]), "ln2_b": f(ln2_b[0]),
        "ln3_g": f(ln3_g[0]), "ln3_b": f(ln3_b[0]), "attn_sinks": f(attn_sinks[0]),
        "conv_w": f(conv_w[0]), "conv_b": f(conv_b[0]), "lru_wa": f(lru_wa[0]), "lru_ba": f(lru_ba[0]),
        "lru_wx": f(lru_wx[0]), "lru_bx": f(lru_bx[0]), "lru_lambda": f(lru_lambda[0]),
        "rope": rope, "ident": ident, "perm": perm,
    }
    L = ntp * TILE
    in_maps = []
    for b in range(n):
        m = dict(shared)
        m["x"] = np.concatenate([f(x_prompt[b, :L]), f(x_sample[b])], axis=0)
        m["p"] = np.concatenate([f(p_prompt[0, b, :L]), f(p_sample[0, b])], axis=0)
        m["cache_k"] = f(cache_k[0, b]).reshape(128, 128)
        m["cache_v"] = f(cache_v[0, b]).reshape(128, 128)
        m["state_conv"] = f(state_conv[0, b])
        m["state_h"] = f(state_h[0, b])
        in_maps.append(m)
    res = run_bass_kernel_spmd(nc, in_maps, core_ids=list(range(n)))
    R = res.results
    yp = np.stack([R[b]["y"][:L] for b in range(n)])
    ys = np.stack([R[b]["y"][L:] for b in range(n)])

    def st(name, shape):
        return np.stack([np.asarray(R[b][name]).reshape(shape) for b in range(n)])[None]

    return (yp.astype(np.float32), ys.astype(np.float32),
            st("newk_p", (128, 2, 64)), st("newv_p", (128, 2, 64)), st("conv_p", (3, 512)), st("h_p", (512,)),
            st("newk_s", (128, 2, 64)), st("newv_s", (128, 2, 64)), st("conv_s", (3, 512)), st("h_s", (512,)))
```

```python
import numpy as np
import ml_dtypes
from contextlib import ExitStack

import concourse.bass as bass
import concourse.mybir as mybir
from concourse.bass_utils import run_bass_kernel_spmd

F32 = mybir.dt.float32
BF16 = mybir.dt.bfloat16
AF = mybir.ActivationFunctionType
ALU = mybir.AluOpType

D = 1024
SEQ = 8192
TS = 64
DFF = 2816
NF = DFF // 128
INC = 1792
PLE = 256
WINDOW = 128
ALPHA = 2.0 ** 0.25
LN_EPS = 1e-5
ROPE_THETA = 500000.0
PAST_LEN = 4096
TILE = 512
NS = 6
SLOT = 4096
NBLK = 47

HEAD_PERM = [0, 2, 1, 3, 4, 6, 5, 7]


class Op:
    __slots__ = ("eng", "fn", "waits", "signal", "dsem", "dval", "idx", "cnt")

    def __init__(self, eng, fn):
        self.eng = eng
        self.fn = fn
        self.waits = []
        self.signal = False
        self.dsem = None
        self.dval = 0
        self.idx = 0
        self.cnt = 0


class Prog:
    ENGS = ("pe", "act", "dve", "pool", "sp")

    def __init__(self):
        self.streams = {e: [] for e in self.ENGS}
        self.last_w = {}
        self.readers = {}
        self.known = {e: {} for e in self.ENGS}
        self.dma_cnt = {}
        self.final_sems = set()

    def _dep(self, op, d):
        if d is None or d is op:
            return
        if d.dsem is not None:
            key = ("d", d.dsem)
            val = d.dval
            if d.dsem in self.final_sems:
                val = -1
                if self.known[op.eng].get(key, 0) == -1:
                    return
                self.known[op.eng][key] = -1
                op.waits.append((key, d, True))
                return
            if self.known[op.eng].get(key, 0) >= val:
                return
            self.known[op.eng][key] = val
            op.waits.append((key, d, False))
        else:
            if d.eng == op.eng and op.eng == "pe" and op.dsem is None:
                return
            key = ("e", d.eng)
            if self.known[op.eng].get(key, -1) >= d.idx:
                return
            self.known[op.eng][key] = d.idx
            d.signal = True
            op.waits.append((key, d, False))

    def op(self, eng, fn, reads=(), writes=(), dsem=None):
        o = Op(eng, fn)
        st = self.streams[eng]
        o.idx = len(st)
        if dsem is not None:
            o.dsem = dsem
            self.dma_cnt[dsem] = self.dma_cnt.get(dsem, 0) + 16
            o.dval = self.dma_cnt[dsem]
        deps = []
        for n in reads:
            deps.append(self.last_w.get(n))
            if n.startswith("ps"):
                deps.extend(r for r in self.readers.get(n, ()) if r.eng != eng)
        for n in writes:
            deps.append(self.last_w.get(n))
            deps.extend(self.readers.get(n, ()))
        best = {}
        for d in deps:
            if d is None or d is o:
                continue
            if d.dsem is not None:
                key = ("d", d.dsem)
                if key not in best or best[key].dval < d.dval:
                    best[key] = d
            else:
                key = ("e", d.eng)
                if key not in best or best[key].idx < d.idx:
                    best[key] = d
        for d in best.values():
            self._dep(o, d)
        for n in reads:
            self.readers.setdefault(n, []).append(o)
        for n in writes:
            self.last_w[n] = o
            self.readers[n] = []
        st.append(o)
        return o

    def emit(self, nc, block, esem, dsems):
        for e in self.ENGS:
            c = 0
            for o in self.streams[e]:
                if o.dsem is None and o.signal:
                    c += 1
                    o.cnt = c
        prog = self

        def run(engname):
            def body(eng):
                for o in prog.streams[engname]:
                    for key, d, fin in o.waits:
                        if key[0] == "d":
                            v = prog.dma_cnt[d.dsem] if fin else d.dval
                            eng.wait_ge(dsems[d.dsem], v)
                        else:
                            eng.wait_ge(esem[d.eng], d.cnt)
                    ins = o.fn(eng)
                    if o.dsem is not None:
                        ins.then_inc(dsems[o.dsem], 16)
                    elif o.signal:
                        ins.then_inc(esem[engname], 1)
                if engname == "sp":
                    for k, v in prog.dma_cnt.items():
                        eng.wait_ge(dsems[k], v)
            return body

        block.tensor(run("pe"))
        block.scalar(run("act"))
        block.vector(run("dve"))
        block.gpsimd(run("pool"))
        block.sync(run("sp"))


def block_table():
    t = []
    for ffn in (1, 2):
        pass
    blocks = []
    for b in range(11):
        blocks.append(("up", 1, b))
    for hf in range(2):
        for g in range(3):
            blocks.append(("dn", 1, hf, g))
    for i in (0, 2, 3, 4, 5):
        blocks.append(("in", i))
    for hf in range(2):
        blocks.append(("oa", hf))
        blocks.append(("ol", hf))
    for b in range(11):
        blocks.append(("up", 2, b))
    for hf in range(2):
        for g in range(3):
            blocks.append(("dn", 2, hf, g))
    for hf in range(2):
        blocks.append(("pg", hf))
        blocks.append(("pp", hf))
    assert len(blocks) == NBLK
    return blocks


BLOCKS = block_table()
BIDX = {b: i for i, b in enumerate(BLOCKS)}


def blk_extent(b):
    k = b[0]
    if k == "up":
        return 128, 4096
    if k == "dn":
        nf = 8 if b[3] < 2 else 6
        return 128, nf * 512
    if k == "in":
        return (128, 4096) if b[1] in (0, 3, 4) else (128, 1024)
    if k == "oa":
        return 128, 2048
    if k == "ol":
        return 128, 2048
    if k == "pg":
        return 128, 4096
    if k == "pp":
        return 128, 1024
    raise ValueError(b)


def build(ntiles_prompt=SEQ // TILE, with_sample=True):
    NTOK = ntiles_prompt * TILE + (TS if with_sample else 0)
    NPOS = SEQ + TS
    nc = bass.Bass("TRN2", target_bir_lowering=False)
    P = Prog()
    es = ExitStack()

    def din(name, shape, dt=F32):
        return nc.dram_tensor(name, list(shape), dt, kind="ExternalInput").ap()

    def dout(name, shape, dt=F32):
        return nc.dram_tensor(name, list(shape), dt, kind="ExternalOutput").ap()

    x_d = din("x", [NTOK, D])
    p_d = din("p", [NTOK, PLE])
    ck_d = din("cache_k", [128, 128])
    cv_d = din("cache_v", [128, 128])
    sc_d = din("state_conv", [3, 512])
    sh_d = din("state_h", [512])
    wg_d = {1: din("ffn1_wg", [D, DFF]), 2: din("ffn2_wg", [D, DFF])}
    wu_d = {1: din("ffn1_wu", [D, DFF]), 2: din("ffn2_wu", [D, DFF])}
    wd_d = {1: din("ffn1_wd", [DFF, D]), 2: din("ffn2_wd", [DFF, D])}
    win_d = din("w_in", [D, INC])
    wout_d = din("w_out", [D, D])
    wpg_d = din("w_ple_gate", [D, D])
    wpp_d = din("w_ple", [PLE, D])
    lng_d = [din("ln%d_g" % i, [D]) for i in (1, 2, 3)]
    lnb_d = [din("ln%d_b" % i, [D]) for i in (1, 2, 3)]
    sink_d = din("attn_sinks", [8])
    convw_d = din("conv_w", [4, 512])
    convb_d = din("conv_b", [512])
    wa_d = din("lru_wa", [8, 64, 64])
    ba_d = din("lru_ba", [512])
    wx_d = din("lru_wx", [8, 64, 64])
    bx_d = din("lru_bx", [512])
    lam_d = din("lru_lambda", [512])
    rope_d = din("rope", [2, 128, NPOS])
    ident_d = din("ident", [128, 128])
    perm_d = din("perm", [128, 3, 128])

    y_d = dout("y", [NTOK, D])
    outs_state = {}
    for tag in ("p", "s"):
        outs_state[tag] = dict(
            k=dout("newk_" + tag, [128, 128]), v=dout("newv_" + tag, [128, 128]),
            c=dout("conv_" + tag, [3, 512]), h=dout("h_" + tag, [512]))

    ws_d = nc.dram_tensor("wstream", [NBLK, 128, SLOT], BF16, kind="Internal").ap()

    def sb(name, shape, dt=F32):
        return es.enter_context(nc.sbuf_tensor(name, list(shape), dt))

    hbuf = [sb("hbuf%d" % s, [128, D]) for s in range(4)]
    actT = sb("actT", [128, 8, TILE], BF16)
    hid = sb("hid", [128, NF, TILE], BF16)
    ring = [sb("ring%d" % i, [128, SLOT], BF16) for i in range(NS)]
    qT = sb("qT", [128, 4, TILE], BF16)
    kT = sb("kT", [128, 2, 128 + TILE], BF16)
    Vb = sb("Vb", [128, 5, 2, 66], BF16)
    xbuf = sb("xbuf", [128, 4, 4 + TILE])
    attnT = sb("attnT", [128, 4, TILE], BF16)
    atm = [sb("atm%d" % i, [64, 256]) for i in range(3)]
    esl = sb("esl", [64, 8])
    sink64 = sb("sink64", [64, 8])
    lruT = sb("lruT", [128, 4, TILE], BF16)
    PT = [sb("PT%d" % i, [128, 2, 256], BF16) for i in range(4)]
    rec = [sb("rec%d" % i, [64, 4]) for i in range(3)]
    rope_c = sb("rope_c", [128, TILE])
    rope_s = sb("rope_s", [128, TILE])
    gbc = [sb("gbc%d" % i, [128, D]) for i in range(3)]
    bbc = [sb("bbc%d" % i, [128, D]) for i in range(3)]
    pbuf = sb("pbuf", [128, 4, PLE])
    pT = sb("pT", [128, 2, TILE], BF16)
    ftA = [sb("ftA%d" % i, [128, TILE]) for i in range(2)]
    ftB = [sb("ftB%d" % i, [128, TILE]) for i in range(2)]
    ident = sb("ident_sb", [128, 128])
    perm = sb("perm_b", [128, 3, 128], BF16)
    qb = [sb("qb%d" % i, [128, TILE], BF16) for i in range(2)]
    ones_b = sb("ones_b", [128, 64], BF16)
    cw = sb("cw", [128, 4, 4])
    cb = sb("cb", [128, 4])
    hba = sb("hba", [128, 4])
    hbx = sb("hbx", [128, 4])
    lam = sb("lam", [128, 4])
    chalf = sb("chalf", [128, 4])
    cfull = sb("cfull", [128, 4])
    ltmp = sb("ltmp", [128, 4])
    wbd = sb("wbd", [128, 2, 4, 128], BF16)
    hcar = sb("hcar", [128, 4])
    LXC = [sb("L_xc%d" % i, [128, TILE]) for i in range(2)]
    LXCB = [sb("L_xcb%d" % i, [128, TILE], BF16) for i in range(2)]
    L_tr = sb("L_tr", [128, TILE])
    LA = sb("LA", [128, 2, TILE])
    LS = sb("LS", [128, 2, TILE])
    LU = sb("LU", [128, 2, TILE])
    L_ti = sb("L_ti", [128, TILE])
    LHS = [sb("L_hs%d" % i, [128, TILE]) for i in range(2)]
    L_g = sb("L_g", [128, TILE])
    L_g2 = sb("L_g2", [128, TILE])
    stats_l = [sb("stats%d" % i, [128, 2, 6]) for i in range(4)]
    mv_l = [sb("mv%d" % i, [128, 2]) for i in range(4)]
    rstd_l = [sb("rstd%d" % i, [128, 1]) for i in range(4)]
    nb_l = [sb("nb%d" % i, [128, 1]) for i in range(4)]
    mhalf = sb("mhalf", [128, 1])
    kfin = sb("kfin", [128, 128])
    kfin_t = sb("kfin_t", [128, 128])
    vfin = sb("vfin", [128, 128])
    ckt = sb("ckt", [128, 2, 128])
    cvt = sb("cvt", [128, 128])

    psum_all = es.enter_context(nc.psum_tensor("ps_all", [128, 8, 512], F32))
    psum = [psum_all[:, i, :] for i in range(8)]
    bank_ctr = [0]

    def bank():
        b = bank_ctr[0] % 8
        bank_ctr[0] += 1
        return b

    def bank_pair():
        if bank_ctr[0] % 8 == 7:
            bank_ctr[0] += 1
        return bank(), bank()

    esem = {e: es.enter_context(nc.semaphore("s_" + e)) for e in Prog.ENGS}
    dsem_names = (["ring%d" % i for i in range(NS)] + ["hb%d" % s for s in range(4)] +
                  ["pb", "rope", "setup", "prep", "st_small", "ck", "cv", "car"])
    class _DS(dict):
        def __missing__(self, n):
            v = es.enter_context(nc.semaphore("d_" + n))
            self[n] = v
            return v
    dsems = _DS()
    for n in dsem_names:
        dsems[n]
    P.final_sems.add("setup")

    def dma(eng, out, in_, reads, writes, sem, nonc=False):
        if nonc:
            def fn(e, out=out, in_=in_):
                with nc.allow_non_contiguous_dma(reason="tiny strided state"):
                    return e.dma_start(out=out, in_=in_)
        else:
            def fn(e, out=out, in_=in_):
                return e.dma_start(out=out, in_=in_)
        return P.op(eng, fn, reads, writes, dsem=sem)

    def mm(out, lhsT, rhs, start, stop, reads, writes):
        return P.op("pe", lambda e: e.matmul(out, lhsT, rhs, start=start, stop=stop), reads, writes)

    def tp(out, in_, idn, reads, writes):
        return P.op("pe", lambda e: e.transpose(out, in_, idn), reads, writes)

    def act(out, in_, func, reads, writes, bias=None, scale=None):
        kw = {}
        if bias is not None:
            kw["bias"] = bias
        if scale is not None:
            kw["scale"] = scale
        return P.op("act", lambda e: e.activation(out, in_, func, **kw), reads, writes)

    def tt(eng, out, a, b, op, reads, writes):
        return P.op(eng, lambda e: e.tensor_tensor(out, a, b, op), reads, writes)

    def ts(eng, out, a, s1, s2, op0, op1, reads, writes):
        if op1 is None:
            return P.op(eng, lambda e: e.tensor_scalar(out, a, s1, None, op0), reads, writes)
        return P.op(eng, lambda e: e.tensor_scalar(out, a, s1, s2, op0, op1), reads, writes)

    def stt(out, a, s, b, op0, op1, reads, writes):
        return P.op("dve", lambda e: e.scalar_tensor_tensor(out, a, s, b, op0, op1), reads, writes)

    def cp(eng, out, in_, reads, writes):
        if eng == "act":
            return P.op("act", lambda e: e.copy(out, in_), reads, writes)
        return P.op(eng, lambda e: e.tensor_copy(out, in_), reads, writes)

    def mset(eng, out, val, writes):
        return P.op(eng, lambda e: e.memset(out, val), (), writes)

    prep_n = [0]
    prep_cur = [0]

    import os as _os
    KDBG = _os.environ.get("KDBG", "")
    STOP = int(_os.environ.get("KSTOP", "0"))

    def prep(out, in_):
        if "noprep" in KDBG:
            return
        def fn(e, out=out, in_=in_):
            with nc.allow_non_contiguous_dma(reason="one-time weight re-blocking"):
                return e.dma_start(out=out, in_=in_)
        prep_n[0] += 1
        o_ = P.op("pool", fn, (), ("wsdx%d" % prep_n[0],), dsem="prep%d" % prep_cur[0])
        P.last_w["wsd%d" % prep_cur[0]] = o_

    def wsv(bi):
        return ws_d[bi]

    def prep_block(bi):
        b = BLOCKS[bi]
        prep_cur[0] = bi
        k = b[0]
        dst = wsv(bi)
        if k == "up":
            f, blk = b[1], b[2]
            for gu, src in enumerate((wg_d[f], wu_d[f])):
                o = dst[:, gu * 2048:(gu + 1) * 2048].rearrange("p (k c) -> p k c", k=8)
                i = src[:, blk * 256:(blk + 1) * 256].rearrange("(k p) c -> p k c", p=128)
                prep(o, i)
        elif k == "dn":
            f, hf, g = b[1], b[2], b[3]
            nf = 8 if g < 2 else 6
            o = dst[:, 0:nf * 512].rearrange("p (f c) -> p f c", f=nf)
            i = wd_d[f][g * 8 * 128:(g * 8 + nf) * 128, hf * 512:(hf + 1) * 512].rearrange(
                "(f p) c -> p f c", p=128)
            prep(o, i)
        elif k == "in":
            j = b[1]
            src = win_d.rearrange("(k p) c -> p k c", p=128)
            if j in (0, 3, 4):
                o4 = dst.rearrange("p (k c) -> p k c", k=8)
            if j == 0:
                prep(o4, src[:, :, 0:512])
            elif j == 3:
                prep(o4, src[:, :, 768:1280])
            elif j == 4:
                prep(o4, src[:, :, 1280:1792])
            elif j == 5:
                o = dst[:, 0:1024].rearrange("p (k c) -> p k c", k=8)
                prep(o, src[:, :, 640:768])
            elif j == 2:
                o = dst[:, 0:1024].rearrange("p (k c) -> p k c", k=8)
                prep(o, src[:, :, 512:640])
        elif k == "oa":
            hf = b[1]
            o = dst[:, 0:2048].rearrange("p (s c) -> p s c", s=4)
            for pidx in range(4):
                for hh in range(2):
                    h = HEAD_PERM[2 * pidx + hh]
                    prep(o[hh * 64:(hh + 1) * 64, pidx, :], wout_d[h * 64:(h + 1) * 64, hf * 512:(hf + 1) * 512])
        elif k == "ol":
            hf = b[1]
            o = dst[:, 0:2048].rearrange("p (f c) -> p f c", f=4)
            i = wout_d[512:1024, hf * 512:(hf + 1) * 512].rearrange("(f p) c -> p f c", p=128)
            prep(o, i)
        elif k == "pg":
            hf = b[1]
            o = dst.rearrange("p (k c) -> p k c", k=8)
            i = wpg_d[:, hf * 512:(hf + 1) * 512].rearrange("(k p) c -> p k c", p=128)
            prep(o, i)
        elif k == "pp":
            hf = b[1]
            o = dst[:, 0:1024].rearrange("p (k c) -> p k c", k=2)
            i = wpp_d[:, hf * 512:(hf + 1) * 512].rearrange("(k p) c -> p k c", p=128)
            prep(o, i)


    setup_n = [0]

    def sdma(out, in_, nonc=False):
        setup_n[0] += 1
        o_ = dma("sp", out, in_, (), ("consts_%d" % setup_n[0],), "setup", nonc=nonc)
        P.last_w["consts"] = o_

    sdma(ident[:], ident_d)
    perm_f = ftA[0][:, 0:384].rearrange("p (a b) -> p a b", a=3)
    dma("sp", perm_f, perm_d, (), ("ftA0",), "permld")
    cp("dve", perm[:], perm_f, ("ftA0",), ("consts2",))
    for i in range(3):
        sdma(gbc[i][:], lng_d[i].partition_broadcast(128), nonc=True)
        sdma(bbc[i][:], lnb_d[i].partition_broadcast(128), nonc=True)
    for tap in range(4):
        sdma(cw[:, tap, :], convw_d[tap].rearrange("(c p) -> p c", p=128), nonc=True)
    sdma(cb[:], convb_d.rearrange("(c p) -> p c", p=128), nonc=True)
    sdma(hba[:], ba_d.rearrange("(c p) -> p c", p=128), nonc=True)
    sdma(hbx[:], bx_d.rearrange("(c p) -> p c", p=128), nonc=True)
    sdma(lam[:], lam_d.rearrange("(c p) -> p c", p=128), nonc=True)
    wn = 0
    for ax, src in enumerate((wa_d, wx_d)):
        P.final_sems.add("setup2_%d" % ax)
        stage = ftB[ax][:, :].rearrange("p (c d) -> p c d", c=4)
        mset("pool", ftB[ax][:, :], 0.0, ("ftB%d" % ax,))
        for cc in range(4):
            for hh in range(2):
                wn += 1
                o_ = dma("sp", stage[hh * 64:(hh + 1) * 64, cc, hh * 64:(hh + 1) * 64], src[cc * 2 + hh],
                         ("ftB%d" % ax,), ("wbdf_%d" % wn,), "setup2_%d" % ax, nonc=True)
                P.last_w["wbdf%d" % ax] = o_
        cp("pool", wbd[:, ax, :, :], stage, ("wbdf%d" % ax, "ftB%d" % ax), ("consts2", "ftB%d" % ax))
    mset("pool", ones_b[:], 1.0, ("consts2",))
    mset("pool", mhalf[:], -0.5, ("consts2",))
    sdma(sink64[:], sink_d.partition_broadcast(64), nonc=True)
    act(sink64[:], sink64[:], AF.Exp, ("consts",), ("esf",))
    for sl in range(8):
        h = HEAD_PERM[sl]
        cp("dve", esl[:, sl:sl + 1], sink64[:, h:h + 1], ("esf",), ("consts2",))
    mset("pool", Vb[:, :, :, 64:66], 1.0, ("Vb",))
    ts("dve", hba[:], hba[:], 0.5, None, ALU.mult, None, ("consts",), ("consts2",))
    ts("dve", hbx[:], hbx[:], 0.5, None, ALU.mult, None, ("consts",), ("consts2",))
    P.op("act", lambda e: e.activation(ltmp[:], lam[:], AF.Exp, scale=-1.0), ("consts",), ("lt",))
    lt2 = sb("lt2", [128, 4])
    lt3 = sb("lt3", [128, 4])
    lt4 = sb("lt4", [128, 4])
    ts("dve", lt2[:], ltmp[:], 2.0, None, ALU.add, None, ("lt",), ("lt2",))
    P.op("dve", lambda e: e.reciprocal(lt2[:], lt2[:]), ("lt2",), ("lt2",))
    tt("dve", lt2[:], lt2[:], ltmp[:], ALU.mult, ("lt2", "lt"), ("lt2",))
    tt("dve", lt3[:], lt2[:], lt2[:], ALU.mult, ("lt2",), ("lt3",))
    ts("dve", lt4[:], lt3[:], 1.0 / 13, 1.0 / 11, ALU.mult, ALU.add, ("lt3",), ("lt4",))
    for cst in (1.0 / 9, 1.0 / 7, 1.0 / 5, 1.0 / 3, 1.0):
        tt("dve", lt4[:], lt4[:], lt3[:], ALU.mult, ("lt4", "lt3"), ("lt4",))
        ts("dve", lt4[:], lt4[:], cst, None, ALU.add, None, ("lt4",), ("lt4",))
    tt("dve", lt4[:], lt4[:], lt2[:], ALU.mult, ("lt4", "lt2"), ("lt4",))
    ts("dve", cfull[:], lt4[:], -16.0, None, ALU.mult, None, ("lt4",), ("consts2",))
    ts("dve", chalf[:], lt4[:], -8.0, None, ALU.mult, None, ("lt4",), ("consts2",))

    CONST = ("consts", "consts2")

    ntile_total = ntiles_prompt + (1 if with_sample else 0)
    total_blocks = ntile_total * NBLK
    wstate = {"next": 0, "rel": set(), "prep": 0}
    PREP_AHEAD = 14

    def w_pump():
        while wstate["next"] < total_blocks and (wstate["next"] < NS or (wstate["next"] - NS) in wstate["rel"]):
            i = wstate["next"]
            b = BLOCKS[i % NBLK]
            npart, nel = blk_extent(b)
            slot = i % NS
            while wstate["prep"] < NBLK and wstate["prep"] <= i + PREP_AHEAD:
                prep_block(wstate["prep"])
                wstate["prep"] += 1
            dma("sp", ring[slot][0:npart, 0:nel], ws_d[i % NBLK][0:npart, 0:nel],
                ("wsd%d" % (i % NBLK),), ("ring%d" % slot,), "ring%d" % slot)
            wstate["next"] += 1

    def w_acquire(g):
        w_pump()
        assert wstate["next"] > g, (g, wstate["next"])
        return ring[g % NS], "ring%d" % (g % NS)

    def w_release(g, total):
        wstate["rel"].add(g)
        w_pump()

    pending_T = []

    def flush_T():
        while pending_T:
            pending_T.pop(0)()

    def ffn(ti, which, T, nsub, nt, lnidx, g0):
        def up_chunk(j, rg, rname, jj, t0, t1):
            ss = range(t0 // 128, (t1 + 127) // 128)
            bG, bU = bank(), bank()
            W_ = t1 - t0
            for gu, bk in ((0, bG), (1, bU)):
                for k in range(8):
                    lhsT = rg[:, gu * 2048 + k * 256 + jj * 128: gu * 2048 + k * 256 + (jj + 1) * 128]
                    mm(psum[bk][:, 0:W_], lhsT, actT[:, k, t0:t1], k == 0, k == 7,
                       (rname,) + aT_names(nsub, (k,), ss), ("ps%d" % bk,))
            fa, fb = ftA[j % 2], ftB[j % 2]
            act(fa[:, 0:W_], psum[bG][:, 0:W_], AF.Tanh, ("ps%d" % bG,), ("ftA%d" % (j % 2),), scale=0.5)
            stt(fb[:, 0:W_], fa[:, 0:W_], 1.0, psum[bG][:, 0:W_], ALU.add, ALU.mult,
                ("ftA%d" % (j % 2), "ps%d" % bG), ("ftB%d" % (j % 2),))
            tt("dve", hid[:, j, t0:t1], fb[:, 0:W_], psum[bU][:, 0:W_], ALU.mult,
               ("ftB%d" % (j % 2), "ps%d" % bU), ("hid%d" % j,))

        NSPB = 3 if T == TILE else 0
        if not NSPB:
            flush_T()
        if NSPB:
            gsp = [g0 + BIDX[("up", which, b)] for b in range(NSPB)]
            rsp = [w_acquire(g) for g in gsp]
            for h_ in range(2):
                if h_ == 1:
                    flush_T()
                for b in range(NSPB):
                    for jj in range(2):
                        up_chunk(2 * b + jj, rsp[b][0], rsp[b][1], jj, h_ * 256, (h_ + 1) * 256)
            for g in gsp:
                w_release(g, total_blocks)
        for b in range(NSPB, 11):
            g = g0 + BIDX[("up", which, b)]
            rg, rname = w_acquire(g)
            for jj in range(2):
                up_chunk(2 * b + jj, rg, rname, jj, 0, T)
            w_release(g, total_blocks)
        for hf in range(2):
            gs = [g0 + BIDX[("dn", which, hf, gg)] for gg in range(3)]
            rgs = [w_acquire(g) for g in gs]
            for s in range(nsub):
                bk = bank()
                for f in range(NF):
                    rg, rname = rgs[f // 8]
                    mm(psum[bk][0:nt, :], hid[:, f, s * 128:s * 128 + nt],
                       rg[:, (f % 8) * 512:(f % 8 + 1) * 512], f == 0, f == NF - 1,
                       (rname, "hid%d" % f), ("ps%d" % bk,))
                hv = hbuf[s][0:nt, hf * 512:(hf + 1) * 512]
                stt(hv, hv, 4.0 * ALPHA, psum[bk][0:nt, :], ALU.mult, ALU.add,
                    ("hb%d" % s, "ps%d" % bk), ("hb%d" % s,))
                if hf == 1:
                    if s >= 2:
                        to_feature_major_s(s - 2, nt)
                    layer_norm(s, nt, lnidx, 16.0 * LN_EPS)
            for g in gs:
                w_release(g, total_blocks)
        for s in range(max(0, nsub - 2), nsub):
            pending_T.append(lambda s=s: to_feature_major_s(s, nt))

    def layer_norm(s, nt, lnidx, eps):
        hn = "hb%d" % s
        hv = hbuf[s]
        stats, mv, rstd, nb = stats_l[s], mv_l[s], rstd_l[s], nb_l[s]
        sn, mn, rn, nn = "stats%d" % s, "mv%d" % s, "rstd%d" % s, "nb%d" % s
        for c in range(2):
            P.op("dve", lambda e, c=c: e.bn_stats(stats[0:nt, c, :], hv[0:nt, c * 512:(c + 1) * 512]),
                 (hn,), (sn,))
        P.op("dve", lambda e: e.bn_aggr(mv[0:nt, :], stats[0:nt, :, :].rearrange("p a b -> p (a b)")),
             (sn,), (mn,))
        ts("pool", rstd[0:nt, :], mv[0:nt, 1:2], eps, None, ALU.add, None, (mn,), (rn,))
        tt("pool", rstd[0:nt, :], rstd[0:nt, :], mhalf[0:nt, :], ALU.pow, (rn,) + CONST, (rn,))
        stt(nb[0:nt, :], mv[0:nt, 0:1], -1.0, rstd[0:nt, :], ALU.mult, ALU.mult, (mn, rn), (nn,))
        act(hv[0:nt, :], hv[0:nt, :], AF.Identity, (hn, rn, nn), (hn,), bias=nb[0:nt, :], scale=rstd[0:nt, :])
        tt("dve", hv[0:nt, :], hv[0:nt, :], gbc[lnidx][0:nt, :], ALU.mult, (hn,) + CONST, (hn,))
        tt("pool", hv[0:nt, :], hv[0:nt, :], bbc[lnidx][0:nt, :], ALU.add, (hn,) + CONST, (hn,))

    def aT_names(nsub, ks=range(8), ss=None):
        ss = range(nsub) if ss is None else ss
        return tuple(sorted({"aT%d_%d" % (s_, k_ // 4) for s_ in ss for k_ in ks}))

    def to_feature_major_s(s, nt):
        for q in range(2):
            bk = bank()
            for kk in range(4):
                k = q * 4 + kk
                tp(psum[bk][:, kk * 128:kk * 128 + nt], hbuf[s][0:nt, k * 128:(k + 1) * 128],
                   ident[0:nt, 0:nt], ("hb%d" % s,) + CONST, ("ps%d" % bk,))
            src = psum[bk][:, :].rearrange("p (k t) -> p k t", k=4)[:, :, 0:nt]
            dst = actT[:, q * 4:(q + 1) * 4, s * 128:s * 128 + nt]
            if q == 0:
                cp("act", dst, src, ("ps%d" % bk,), ("aT%d_%d" % (s, q),))
            else:
                cp("dve", dst, src, ("ps%d" % bk,), ("aT%d_%d" % (s, q),))

    def to_feature_major(nsub, nt):
        for s in range(nsub):
            to_feature_major_s(s, nt)

    def tile_prog(ti, seq, tok0, T, pos0, first, last, nxt=None, preloaded=False):
        nsub = (T + 127) // 128
        nt = min(T, 128)
        g0 = ti * NBLK
        tag = "p" if seq == "prompt" else "s"

        if first:
            if seq == "prompt":
                mset("pool", xbuf[:, :, 0:4], 0.0, ["xb%d" % cc for cc in range(4)])
                mset("pool", hcar[:], 0.0, ("hcar",))
            else:
                for cc in range(4):
                    dma("sp", xbuf[:, cc, 1:4], sc_d[:, cc * 128:(cc + 1) * 128].rearrange("r p -> p r"),
                        (), ("xb%d" % cc,), "car%d" % cc, nonc=True)
                dma("sp", hcar[:], sh_d.rearrange("(c p) -> p c", p=128), (), ("hcar",), "carh", nonc=True)
                dma("sp", ckt[:, 0, :], ck_d, (), ("ckt",), "ck")
                dma("sp", ckt[:, 1, 0:64], ck_d[:, 64:128], (), ("ckt",), "ck", nonc=True)
                dma("sp", ckt[:, 1, 64:128], ck_d[:, 0:64], (), ("ckt",), "ck", nonc=True)
                dma("sp", cvt[:], cv_d, (), ("cvt",), "cv")
                for var in range(2):
                    bk = bank()
                    tp(psum[bk][:, 0:128], ckt[:, var, :], ident[:], ("ckt",) + CONST, ("ps%d" % bk,))
                    cp("act", kT[:, var, 0:128], psum[bk][:, 0:128], ("ps%d" % bk,), ("kT",))
                cp("act", Vb[:, 0, :, 0:64], cvt[:].rearrange("p (a b) -> p a b", a=2), ("cvt",), ("Vb",))

        if not preloaded:
            for s in range(nsub):
                dma("sp", hbuf[s][0:nt, :], x_d[tok0 + s * 128: tok0 + s * 128 + nt, :], (), ("hb%d" % s,), "hb%d" % s)
        dma("sp", pbuf[0:nt, 0:nsub, :],
            p_d[tok0:tok0 + T, :].rearrange("(s p) c -> p s c", p=nt), (), ("pbuf",), "pb")
        dma("sp", rope_c[:, 0:T], rope_d[0, :, pos0:pos0 + T], (), ("rope",), "rope")
        dma("sp", rope_s[:, 0:T], rope_d[1, :, pos0:pos0 + T], (), ("rope",), "rope")

        if STOP == 1:
            return
        for s in range(nsub):
            if nsub > 2 and s >= 2:
                pending_T.append(lambda s=s: to_feature_major_s(s, nt))
            else:
                to_feature_major_s(s, nt)
        ffn(ti, 1, T, nsub, nt, 0, g0)

        if STOP == 2:
            return
        def inproj_chunk(blk, ci, t0=0, t1=None):
            t1 = T if t1 is None else t1
            ss = range(t0 // 128, (t1 + 127) // 128)
            rg, rname = w_acquire(g0 + BIDX[("in", blk)])
            bk = bank()
            for k in range(8):
                cw_ = 128 if blk == 2 else 512
                mm(psum[bk][:, 0:t1 - t0], rg[:, k * cw_ + ci * 128:k * cw_ + (ci + 1) * 128], actT[:, k, t0:t1],
                   k == 0, k == 7, (rname,) + aT_names(nsub, (k,), ss), ("ps%d" % bk,))
            return bk

        halves = ((0, 256), (256, 512)) if T == TILE else ((0, T),)
        items = [(hi_, t0, t1, qc) for hi_, (t0, t1) in enumerate(halves) for qc in range(4)]
        pend = []

        def q_finish(it, bq, ri):
            hi_, t0, t1, qc = it
            W_ = t1 - t0
            bs = bank()
            mm(psum[bs][:, 0:W_], perm[:, 0, :], qb[ri % 2][:, 0:W_], True, True, ("qb%d" % (ri % 2),) + CONST,
               ("ps%d" % bs,))
            fa, fb = ftA[ri % 2], ftB[ri % 2]
            tt("dve", fa[:, 0:W_], psum[bq][:, 0:W_], rope_c[:, t0:t1], ALU.mult,
               ("ps%d" % bq, "rope"), ("ftA%d" % (ri % 2),))
            tt("dve", fb[:, 0:W_], psum[bs][:, 0:W_], rope_s[:, t0:t1], ALU.mult,
               ("ps%d" % bs, "rope"), ("ftB%d" % (ri % 2),))
            tt("pool", qT[:, qc, t0:t1], fa[:, 0:W_], fb[:, 0:W_], ALU.add,
               ("ftA%d" % (ri % 2), "ftB%d" % (ri % 2)), ("qT",))

        for ri, it in enumerate(items):
            hi_, t0, t1, qc = it
            if hi_ == len(halves) - 1 and qc == 0:
                flush_T()
            bq = inproj_chunk(0, qc, t0, t1)
            cp("act", qb[ri % 2][:, 0:t1 - t0], psum[bq][:, 0:t1 - t0], ("ps%d" % bq,), ("qb%d" % (ri % 2),))
            if pend:
                q_finish(*pend.pop())
            pend.append((it, bq, ri))
        q_finish(*pend.pop())
        w_release(g0 + BIDX[("in", 0)], total_blocks)
        if STOP == 21:
            return
        bkk = inproj_chunk(2, 0)
        cp("act", qb[0][:, 0:T], psum[bkk][:, 0:T], ("ps%d" % bkk,), ("qb0",))
        bks, bkd, bkds = bank(), bank(), bank()
        for pi_, bb in ((0, bks), (1, bkd), (2, bkds)):
            mm(psum[bb][:, 0:T], perm[:, pi_, :], qb[0][:, 0:T], True, True, ("qb0",) + CONST, ("ps%d" % bb,))
        for var, (bq, bs) in enumerate(((bkk, bks), (bkd, bkds))):
            fa, fb = ftA[var], ftB[var]
            tt("dve", fa[:, 0:T], psum[bq][:, 0:T], rope_c[:, 0:T], ALU.mult,
               ("ps%d" % bq, "rope"), ("ftA%d" % var,))
            tt("dve", fb[:, 0:T], psum[bs][:, 0:T], rope_s[:, 0:T], ALU.mult,
               ("ps%d" % bs, "rope"), ("ftB%d" % var,))
            tt("pool", kT[:, var, 128:128 + T], fa[:, 0:T], fb[:, 0:T], ALU.add,
               ("ftA%d" % var, "ftB%d" % var), ("kT",))
            if last and var == 0:
                nl = min(T, 128)
                tt("pool", kfin[:, 0:nl], fa[:, T - nl:T], fb[:, T - nl:T], ALU.add,
                   ("ftA0", "ftB0"), ("kfin",))
        w_release(g0 + BIDX[("in", 2)], total_blocks)
        if STOP == 22:
            return
        rg, rname = w_acquire(g0 + BIDX[("in", 5)])
        for s in range(nsub):
            bk = bank()
            for k in range(8):
                mm(psum[bk][0:nt, 0:128], actT[:, k, s * 128:s * 128 + nt], rg[:, k * 128:(k + 1) * 128],
                   k == 0, k == 7, (rname,) + aT_names(nsub, (k,), (s,)), ("ps%d" % bk,))
            cp("act", Vb[0:nt, 1 + s, :, 0:64], psum[bk][0:nt, 0:128].rearrange("p (a b) -> p a b", a=2),
               ("ps%d" % bk,), ("Vb",))
            if last and s == nsub - 1:
                cp("dve", vfin[0:nt, :], psum[bk][0:nt, 0:128], ("ps%d" % bk,), ("vfin",))
        w_release(g0 + BIDX[("in", 5)], total_blocks)

        if STOP == 3:
            return
        nchunk = T // 64

        NPT = len(PT)

        def attn_blks(c):
            if c % 2 == 0:
                blks = [(c // 2, 0, 128), (c // 2 + 1, 0, 64)]
            else:
                blks = [((c - 1) // 2, 64, 128), ((c + 1) // 2, 0, 128)]
            if first and seq == "prompt":
                blks = [b_ for b_ in blks if b_[0] >= 1]
            return blks

        astate = {}

        def attn_A(i):
            c, v = divmod(i, 2)
            blks = attn_blks(c)
            pi = i % NPT
            bSs = bank_pair()
            ptn = "PT%d" % pi
            for bi_, (slot, lo, hi) in enumerate(blks):
                for par in range(2):
                    var = 0 if v == par else 1
                    bS = bSs[par]
                    if lo == 0:
                        lhsT = kT[par * 64:(par + 1) * 64, var, slot * 128: slot * 128 + hi]
                        out = psum[bS][0:hi, bi_ * 128:(bi_ + 1) * 128]
                    else:
                        lhsT = kT[par * 64:(par + 1) * 64, var, slot * 128: slot * 128 + 128]
                        out = psum[bS][0:128, bi_ * 128:(bi_ + 1) * 128]
                    rhs = qT[par * 64:(par + 1) * 64, 2 * v:2 * v + 2, c * 64:(c + 1) * 64]
                    mm(out, lhsT, rhs, True, True, ("kT", "qT"), ("ps%d" % bS,))
            for bi_, (slot, lo, hi) in enumerate(blks):
                act(PT[pi][lo:hi, bi_, :].rearrange("p (a c) -> p a c", a=2),
                    psum_all[lo:hi, bSs[0]:bSs[0] + 2, bi_ * 128:(bi_ + 1) * 128], AF.Exp,
                    ("ps%d" % bSs[0], "ps%d" % bSs[1]), (ptn,), scale=0.125)

        def attn_B(i):
            c, v = divmod(i, 2)
            blks = attn_blks(c)
            pi = i % NPT
            ptn = "PT%d" % pi
            ai = i % 3
            bO = bank()
            for j in range(4):
                for bi_, (slot, lo, hi) in enumerate(blks):
                    mm(psum[bO][0:64, j * 68:j * 68 + 65], PT[pi][lo:hi, bi_, j * 64:(j + 1) * 64],
                       Vb[lo:hi, slot, v, 0:65], bi_ == 0, bi_ == len(blks) - 1, ("Vb", ptn), ("ps%d" % bO,))
            rn = "rec%d" % ai
            o4 = psum[bO][0:64, 0:272].rearrange("p (j d) -> p j d", j=4)
            tt("dve", rec[ai][:, :], o4[:, :, 64], esl[:, 4 * v:4 * v + 4], ALU.add,
               ("ps%d" % bO,) + CONST, (rn,))
            P.op("dve", lambda e, ai=ai: e.reciprocal(rec[ai][:, :], rec[ai][:, :]), (rn,), (rn,))
            an = "atm%d" % ai
            tt("dve", atm[ai][:, :].rearrange("p (j d) -> p j d", j=4), o4[:, :, 0:64],
               rec[ai][:, :].unsqueeze(2).broadcast_to([64, 4, 64]), ALU.mult, ("ps%d" % bO, rn), (an,))

        def attn_C(i):
            c, v = divmod(i, 2)
            ai = i % 3
            an = "atm%d" % ai
            bT = bank()
            for pr_ in range(2):
                tp(psum[bT][:, pr_ * 64:(pr_ + 1) * 64], atm[ai][:, pr_ * 128:(pr_ + 1) * 128], ident[0:64, 0:64],
                   (an,) + CONST, ("ps%d" % bT,))
            cp("act", attnT[:, 2 * v:2 * v + 2, c * 64:(c + 1) * 64],
               psum[bT][:, 0:128].rearrange("p (a t) -> p a t", a=2), ("ps%d" % bT,), ("attnT",))

        nitem = 2 * nchunk
        attn_steps = []
        for t_ in range(nitem + 3):
            def step(t_=t_):
                if t_ < nitem:
                    attn_A(t_)
                if 0 <= t_ - 2 < nitem:
                    attn_B(t_ - 2)
                if 0 <= t_ - 3 < nitem:
                    attn_C(t_ - 3)
            attn_steps.append(step)

        def lru_a1(cc, j):
            xn = "xb%d" % cc
            bx = inproj_chunk(3, cc)
            cp("act", xbuf[:, cc, 4:4 + T], psum[bx][:, 0:T], ("ps%d" % bx,), (xn,))
            xc = LXC[j]
            ts("dve", xc[:, 0:T], xbuf[:, cc, 1:1 + T], cw[:, 0, cc:cc + 1], cb[:, cc:cc + 1], ALU.mult, ALU.add,
               (xn,) + CONST, ("L_xc%d" % j,))
            for tap in range(1, 4):
                stt(xc[:, 0:T], xbuf[:, cc, 1 + tap:1 + tap + T], cw[:, tap, cc:cc + 1], xc[:, 0:T],
                    ALU.mult, ALU.add, (xn, "L_xc%d" % j) + CONST, ("L_xc%d" % j,))
            if last:
                dma("sp", outs_state[tag]["c"][:, cc * 128:(cc + 1) * 128].rearrange("r p -> p r"),
                    xbuf[:, cc, 1 + T:4 + T], (xn,), (), "st_c%d" % cc, nonc=True)
            cp("pool", xbuf[:, cc, 1:4], xbuf[:, cc, 1 + T:4 + T], (xn,), (xn,))
            cp("act", LXCB[j][:, 0:T], xc[:, 0:T], ("L_xc%d" % j,), ("L_xcb%d" % j,))

        def lru_a2(cc, j):
            xc = LXC[j]
            bA, bI = bank(), bank()
            mm(psum[bA][:, 0:T], wbd[:, 0, cc, :], LXCB[j][:, 0:T], True, True, ("L_xcb%d" % j,) + CONST, ("ps%d" % bA,))
            mm(psum[bI][:, 0:T], wbd[:, 1, cc, :], LXCB[j][:, 0:T], True, True, ("L_xcb%d" % j,) + CONST, ("ps%d" % bI,))
            act(L_tr[:, 0:T], psum[bA][:, 0:T], AF.Tanh, ("ps%d" % bA,) + CONST, ("L_tr",),
                bias=hba[:, cc:cc + 1], scale=0.5)
            act(L_ti[:, 0:T], psum[bI][:, 0:T], AF.Tanh, ("ps%d" % bI,) + CONST, ("L_ti",),
                bias=hbx[:, cc:cc + 1], scale=0.5)
            act(LA[:, j, 0:T], L_tr[:, 0:T], AF.Exp, ("L_tr",) + CONST, ("LA%d" % j,),
                bias=chalf[:, cc:cc + 1], scale=chalf[:, cc:cc + 1])
            act(LS[:, j, 0:T], L_tr[:, 0:T], AF.Exp, ("L_tr",) + CONST, ("LS%d" % j,),
                bias=cfull[:, cc:cc + 1], scale=cfull[:, cc:cc + 1])
            ts("pool", LS[:, j, 0:T], LS[:, j, 0:T], -1.0, 1.0, ALU.mult, ALU.add, ("LS%d" % j,), ("LS%d" % j,))
            stt(LU[:, j, 0:T], L_ti[:, 0:T], 1.0, xc[:, 0:T], ALU.add, ALU.mult, ("L_ti", "L_xc%d" % j), ("LU%d" % j,))

        def lru_sqrt():
            act(LS[:, :, 0:T], LS[:, :, 0:T], AF.Sqrt, ("LS0", "LS1"), ("LS0", "LS1"))

        def lru_c1(cc, j):
            stt(LU[:, j, 0:T], LU[:, j, 0:T], 0.5, LS[:, j, 0:T], ALU.mult, ALU.mult,
                ("LU%d" % j, "LS%d" % j), ("LU%d" % j,))
            stt(LU[:, j, 0:1], LA[:, j, 0:1], hcar[:, cc:cc + 1], LU[:, j, 0:1], ALU.mult, ALU.add,
                ("LA%d" % j, "LU%d" % j, "hcar"), ("LU%d" % j,))
            P.op("dve", lambda e: e.tensor_tensor_scan(LHS[j][:, 0:T], LA[:, j, 0:T], LU[:, j, 0:T],
                                                       0.0, ALU.mult, ALU.add),
                 ("LA%d" % j, "LU%d" % j), ("L_hs%d" % j,))
            cp("pool", hcar[:, cc:cc + 1], LHS[j][:, T - 1:T], ("L_hs%d" % j,), ("hcar",))

        def lru_c2(cc, j):
            bg = inproj_chunk(4, cc)
            act(L_g[:, 0:T], psum[bg][:, 0:T], AF.Square, ("ps%d" % bg,), ("L_g",))
            ts("pool", L_g[:, 0:T], L_g[:, 0:T], 0.044715, 1.0, ALU.mult, ALU.add, ("L_g",), ("L_g",))
            tt("dve", L_g[:, 0:T], L_g[:, 0:T], psum[bg][:, 0:T], ALU.mult, ("L_g", "ps%d" % bg), ("L_g",))
            act(L_g2[:, 0:T], L_g[:, 0:T], AF.Tanh, ("L_g",), ("L_g2",), scale=0.7978845608028654)
            stt(L_g2[:, 0:T], L_g2[:, 0:T], 1.0, psum[bg][:, 0:T], ALU.add, ALU.mult,
                ("L_g2", "ps%d" % bg), ("L_g2",))
            stt(lruT[:, cc, 0:T], L_g2[:, 0:T], 0.5, LHS[j][:, 0:T], ALU.mult, ALU.mult,
                ("L_g2", "L_hs%d" % j), ("lruT",))

        lru_steps = []
        for pr in range(2):
            for j in range(2):
                lru_steps.append(lambda pr=pr, j=j: lru_a1(2 * pr + j, j))
            for j in range(2):
                lru_steps.append(lambda pr=pr, j=j: lru_a2(2 * pr + j, j))
            lru_steps.append(lru_sqrt)
            for j in range(2):
                lru_steps.append(lambda pr=pr, j=j: lru_c1(2 * pr + j, j))
            for j in range(2):
                lru_steps.append(lambda pr=pr, j=j: lru_c2(2 * pr + j, j))
        for i_ in range(max(len(attn_steps), len(lru_steps))):
            if i_ < len(attn_steps):
                attn_steps[i_]()
            if i_ < len(lru_steps):
                lru_steps[i_]()
        w_release(g0 + BIDX[("in", 3)], total_blocks)
        w_release(g0 + BIDX[("in", 4)], total_blocks)
        if last:
            dma("sp", outs_state[tag]["h"].rearrange("(c p) -> p c", p=128), hcar[:], ("hcar",), (),
                "st_h", nonc=True)

        if STOP == 5:
            return
        if last:
            nl = min(T, 128)
            bk = bank()
            tp(psum[bk][0:nl, 0:128], kfin[:, 0:nl], ident[:], ("kfin",) + CONST, ("ps%d" % bk,))
            cp("act", kfin_t[0:nl, :], psum[bk][0:nl, 0:128], ("ps%d" % bk,), ("kfin_t",))
            dma("sp", outs_state[tag]["k"][128 - nl:128, :], kfin_t[0:nl, :], ("kfin_t",), (), "st_k")
            dma("sp", outs_state[tag]["v"][128 - nl:128, :], vfin[0:nl, :], ("vfin",), (), "st_v")
            if nl < 128:
                dma("sp", outs_state[tag]["k"][0:128 - nl, :], ck_d[nl:128, :], (), (), "st_k2")
                dma("sp", outs_state[tag]["v"][0:128 - nl, :], cv_d[nl:128, :], (), (), "st_v2")
        else:
            cp("pool", kT[:, :, 0:128], kT[:, :, T:T + 128], ("kT",), ("kT",))
            cp("pool", Vb[:, 0, :, :], Vb[:, nsub, :, :], ("Vb",), ("Vb",))

        if STOP == 6:
            return
        for hf in range(2):
            ga = g0 + BIDX[("oa", hf)]
            gl = g0 + BIDX[("ol", hf)]
            ra, ran = w_acquire(ga)
            rl, rln = w_acquire(gl)
            for s in range(nsub):
                bk = bank()
                for sl in range(4):
                    mm(psum[bk][0:nt, :], attnT[:, sl, s * 128:s * 128 + nt], ra[:, sl * 512:(sl + 1) * 512],
                       sl == 0, False, (ran, "attnT"), ("ps%d" % bk,))
                for cc in range(4):
                    mm(psum[bk][0:nt, :], lruT[:, cc, s * 128:s * 128 + nt], rl[:, cc * 512:(cc + 1) * 512],
                       False, cc == 3, (rln, "lruT"), ("ps%d" % bk,))
                hv = hbuf[s][0:nt, hf * 512:(hf + 1) * 512]
                stt(hv, hv, ALPHA, psum[bk][0:nt, :], ALU.mult, ALU.add, ("hb%d" % s, "ps%d" % bk), ("hb%d" % s,))
                if hf == 1:
                    if s >= 2:
                        to_feature_major_s(s - 2, nt)
                    layer_norm(s, nt, 1, LN_EPS)
            w_release(ga, total_blocks)
            w_release(gl, total_blocks)
        for s in range(max(0, nsub - 2), nsub):
            pending_T.append(lambda s=s: to_feature_major_s(s, nt))

        if STOP == 7:
            return
        for s in range(nsub):
            bk = bank()
            for j in range(2):
                tp(psum[bk][:, j * 128:j * 128 + nt], pbuf[0:nt, s, j * 128:(j + 1) * 128], ident[0:nt, 0:nt],
                   ("pbuf",) + CONST, ("ps%d" % bk,))
            cp("act", pT[:, :, s * 128:s * 128 + nt],
               psum[bk][:, 0:256].rearrange("p (k t) -> p k t", k=2)[:, :, 0:nt], ("ps%d" % bk,), ("pT",))
        ffn(ti, 2, T, nsub, nt, 2, g0)

        if STOP == 8:
            return
        if nsub <= 2:
            flush_T()
        for hf in range(2):
            gg = g0 + BIDX[("pg", hf)]
            gp = g0 + BIDX[("pp", hf)]
            rg_, rgn = w_acquire(gg)
            rp_, rpn = w_acquire(gp)
            for s in range(nsub):
                if s == 2:
                    flush_T()
                bG, bP = bank(), bank()
                for k in range(8):
                    mm(psum[bG][0:nt, :], actT[:, k, s * 128:s * 128 + nt], rg_[:, k * 512:(k + 1) * 512],
                       k == 0, k == 7, (rgn,) + aT_names(nsub, (k,), (s,)), ("ps%d" % bG,))
                for k in range(2):
                    mm(psum[bP][0:nt, :], pT[:, k, s * 128:s * 128 + nt], rp_[:, k * 512:(k + 1) * 512],
                       k == 0, k == 1, (rpn, "pT"), ("ps%d" % bP,))
                fa, fb = ftA[s % 2], ftB[s % 2]
                act(fa[0:nt, :], psum[bG][0:nt, :], AF.Tanh, ("ps%d" % bG,), ("ftA%d" % (s % 2),), scale=0.5)
                stt(fb[0:nt, :], fa[0:nt, :], 1.0, psum[bP][0:nt, :], ALU.add, ALU.mult,
                    ("ftA%d" % (s % 2), "ps%d" % bP), ("ftB%d" % (s % 2),))
                hv = hbuf[s][0:nt, hf * 512:(hf + 1) * 512]
                stt(hv, fb[0:nt, :], 0.5, hv, ALU.mult, ALU.add, ("ftB%d" % (s % 2), "hb%d" % s), ("hb%d" % s,))
                if hf == 1:
                    dma("sp", y_d[tok0 + s * 128: tok0 + s * 128 + nt, :], hbuf[s][0:nt, :], ("hb%d" % s,), (),
                        "hb%d" % s)
                    if nxt is not None and s < nxt[2]:
                        ntok0, nnt = nxt[0], nxt[1]
                        dma("pool", hbuf[s][0:nnt, :], x_d[ntok0 + s * 128: ntok0 + s * 128 + nnt, :], (),
                            ("hb%d" % s,), "hbx%d" % s)
            w_release(gg, total_blocks)
            w_release(gp, total_blocks)

    ti = 0
    if "preponly" in KDBG:
        ntiles_prompt = 0
        with_sample = False
    if STOP > 0:
        with_sample = False
    for t in range(ntiles_prompt):
        if t + 1 < ntiles_prompt:
            nxt = ((t + 1) * TILE, 128, 4)
        elif with_sample:
            nxt = (ntiles_prompt * TILE, TS, 1)
        else:
            nxt = None
        tile_prog(ti, "prompt", t * TILE, TILE, t * TILE, t == 0, t == ntiles_prompt - 1, nxt=nxt, preloaded=(t > 0))
        ti += 1
    if with_sample:
        tile_prog(ti, "sample", ntiles_prompt * TILE, TS, SEQ, True, True, preloaded=(ntiles_prompt > 0))
        ti += 1

    with nc.Block() as block:
        P.emit(nc, block, esem, dsems)
    es.close()
    return nc, P


def rope_tables():
    half = 8
    inv = np.power(np.float32(ROPE_THETA), -np.arange(half, dtype=np.float32) * np.float32(2.0 / 16)).astype(np.float32)
    pos = np.concatenate([np.arange(SEQ), PAST_LEN + np.arange(TS)]).astype(np.float32)
    ang = (pos[None, :] * inv[:, None]).astype(np.float32)
    cos = np.cos(ang).astype(np.float32)
    sin = np.sin(ang).astype(np.float32)
    tab = np.zeros((2, 128, SEQ + TS), np.float32)
    tab[0] = 1.0
    for hh in range(2):
        b = hh * 64
        tab[0, b:b + 8] = cos
        tab[0, b + 8:b + 16] = cos
        tab[1, b:b + 8] = -sin
        tab[1, b + 8:b + 16] = sin
    return tab


def perm_tables():
    pm = np.zeros((128, 3, 128), np.float32)

    def partner(d):
        dd = d % 64
        if dd < 8:
            return d + 8
        if dd < 16:
            return d - 8
        return None

    for m in range(128):
        p_ = partner(m)
        if p_ is not None:
            pm[p_, 0, m] = 1.0
        sw = (m + 64) % 128
        pm[sw, 1, m] = 1.0
        p2 = partner(sw)
        if p2 is not None:
            pm[p2, 2, m] = 1.0
    return pm


_CACHE = {}


def kernel(x_prompt, x_sample, p_prompt, p_sample, cache_k, cache_v, state_conv, state_h,
           ffn1_wg, ffn1_wu, ffn1_wd, ln1_g, ln1_b, w_in, attn_sinks, conv_w, conv_b,
           lru_wa, lru_ba, lru_wx, lru_bx, lru_lambda, w_out, ln2_g, ln2_b,
           ffn2_wg, ffn2_wu, ffn2_wd, ln3_g, ln3_b, w_ple, w_ple_gate, _ntiles=SEQ // TILE):
    f = lambda a: np.ascontiguousarray(np.asarray(a, dtype=np.float32))
    n = 8
    ntp = _ntiles
    if "nc" not in _CACHE or _CACHE.get("ntp") != ntp:
        _CACHE["nc"] = build(ntp, True)[0]
        _CACHE["ntp"] = ntp
    nc = _CACHE["nc"]
    rope = rope_tables()
    ident = np.eye(128, dtype=np.float32)
    perm = perm_tables()
    shared = {
        "ffn1_wg": f(ffn1_wg[0]), "ffn1_wu": f(ffn1_wu[0]), "ffn1_wd": f(ffn1_wd[0]),
        "ffn2_wg": f(ffn2_wg[0]), "ffn2_wu": f(ffn2_wu[0]), "ffn2_wd": f(ffn2_wd[0]),
        "w_in": f(w_in[0]), "w_out": f(w_out[0]), "w_ple_gate": f(w_ple_gate[0]), "w_ple": f(w_ple[0]),
        "ln1_g": f(ln1_g[0]), "ln1_b": f(ln1_b[0]), "ln2_g": f(ln2_g[0]), "ln2_b": f(ln2_b[0]),
        "ln3_g": f(ln3_g[0]), "ln3_b": f(ln3_b[0]), "attn_sinks": f(attn_sinks[0]),
        "conv_w": f(conv_w[0]), "conv_b": f(conv_b[0]), "lru_wa": f(lru_wa[0]), "lru_ba": f(lru_ba[0]),
        "lru_wx": f(lru_wx[0]), "lru_bx": f(lru_bx[0]), "lru_lambda": f(lru_lambda[0]),
        "rope": rope, "ident": ident, "perm": perm,
    }
    L = ntp * TILE
    in_maps = []
    for b in range(n):
        m = dict(shared)
        m["x"] = np.concatenate([f(x_prompt[b, :L]), f(x_sample[b])], axis=0)
        m["p"] = np.concatenate([f(p_prompt[0, b, :L]), f(p_sample[0, b])], axis=0)
        m["cache_k"] = f(cache_k[0, b]).reshape(128, 128)
        m["cache_v"] = f(cache_v[0, b]).reshape(128, 128)
        m["state_conv"] = f(state_conv[0, b])
        m["state_h"] = f(state_h[0, b])
        in_maps.append(m)
    res = run_bass_kernel_spmd(nc, in_maps, core_ids=list(range(n)))
    R = res.results
    yp = np.stack([R[b]["y"][:L] for b in range(n)])
    ys = np.stack([R[b]["y"][L:] for b in range(n)])

    def st(name, shape):
        return np.stack([np.asarray(R[b][name]).reshape(shape) for b in range(n)])[None]

    return (yp.astype(np.float32), ys.astype(np.float32),
            st("newk_p", (128, 2, 64)), st("newv_p", (128, 2, 64)), st("conv_p", (3, 512)), st("h_p", (512,)),
            st("newk_s", (128, 2, 64)), st("newv_s", (128, 2, 64)), st("conv_s", (3, 512)), st("h_s", (512,)))
```

```python
import numpy as np
import ml_dtypes
from contextlib import ExitStack

import concourse.bass as bass
import concourse.mybir as mybir
from concourse.bass_utils import run_bass_kernel_spmd

F32 = mybir.dt.float32
BF16 = mybir.dt.bfloat16
AF = mybir.ActivationFunctionType
ALU = mybir.AluOpType

D = 1024
SEQ = 8192
TS = 64
DFF = 2816
NF = DFF // 128
INC = 1792
PLE = 256
WINDOW = 128
ALPHA = 2.0 ** 0.25
LN_EPS = 1e-5
ROPE_THETA = 500000.0
PAST_LEN = 4096
TILE = 512
NS = 6
SLOT = 4096
NBLK = 47

HEAD_PERM = [0, 2, 1, 3, 4, 6, 5, 7]


class Op:
    __slots__ = ("eng", "fn", "waits", "signal", "dsem", "dval", "idx", "cnt")

    def __init__(self, eng, fn):
        self.eng = eng
        self.fn = fn
        self.waits = []
        self.signal = False
        self.dsem = None
        self.dval = 0
        self.idx = 0
        self.cnt = 0


class Prog:
    ENGS = ("pe", "act", "dve", "pool", "sp")

    def __init__(self):
        self.streams = {e: [] for e in self.ENGS}
        self.last_w = {}
        self.readers = {}
        self.known = {e: {} for e in self.ENGS}
        self.dma_cnt = {}
        self.final_sems = set()

    def _dep(self, op, d):
        if d is None or d is op:
            return
        if d.dsem is not None:
            key = ("d", d.dsem)
            val = d.dval
            if d.dsem in self.final_sems:
                val = -1
                if self.known[op.eng].get(key, 0) == -1:
                    return
                self.known[op.eng][key] = -1
                op.waits.append((key, d, True))
                return
            if self.known[op.eng].get(key, 0) >= val:
                return
            self.known[op.eng][key] = val
            op.waits.append((key, d, False))
        else:
            if d.eng == op.eng and op.eng == "pe" and op.dsem is None:
                return
            key = ("e", d.eng)
            if self.known[op.eng].get(key, -1) >= d.idx:
                return
            self.known[op.eng][key] = d.idx
            d.signal = True
            op.waits.append((key, d, False))

    def op(self, eng, fn, reads=(), writes=(), dsem=None):
        o = Op(eng, fn)
        st = self.streams[eng]
        o.idx = len(st)
        if dsem is not None:
            o.dsem = dsem
            self.dma_cnt[dsem] = self.dma_cnt.get(dsem, 0) + 16
            o.dval = self.dma_cnt[dsem]
        deps = []
        for n in reads:
            deps.append(self.last_w.get(n))
            if n.startswith("ps"):
                deps.extend(r for r in self.readers.get(n, ()) if r.eng != eng)
        for n in writes:
            deps.append(self.last_w.get(n))
            deps.extend(self.readers.get(n, ()))
        best = {}
        for d in deps:
            if d is None or d is o:
                continue
            if d.dsem is not None:
                key = ("d", d.dsem)
                if key not in best or best[key].dval < d.dval:
                    best[key] = d
            else:
                key = ("e", d.eng)
                if key not in best or best[key].idx < d.idx:
                    best[key] = d
        for d in best.values():
            self._dep(o, d)
        for n in reads:
            self.readers.setdefault(n, []).append(o)
        for n in writes:
            self.last_w[n] = o
            self.readers[n] = []
        st.append(o)
        return o

    def emit(self, nc, block, esem, dsems):
        for e in self.ENGS:
            c = 0
            for o in self.streams[e]:
                if o.dsem is None and o.signal:
                    c += 1
                    o.cnt = c
        prog = self

        def run(engname):
            def body(eng):
                for o in prog.streams[engname]:
                    for key, d, fin in o.waits:
                        if key[0] == "d":
                            v = prog.dma_cnt[d.dsem] if fin else d.dval
                            eng.wait_ge(dsems[d.dsem], v)
                        else:
                            eng.wait_ge(esem[d.eng], d.cnt)
                    ins = o.fn(eng)
                    if o.dsem is not None:
                        ins.then_inc(dsems[o.dsem], 16)
                    elif o.signal:
                        ins.then_inc(esem[engname], 1)
                if engname == "sp":
                    for k, v in prog.dma_cnt.items():
                        eng.wait_ge(dsems[k], v)
            return body

        block.tensor(run("pe"))
        block.scalar(run("act"))
        block.vector(run("dve"))
        block.gpsimd(run("pool"))
        block.sync(run("sp"))


def block_table():
    t = []
    for ffn in (1, 2):
        pass
    blocks = []
    for b in range(11):
        blocks.append(("up", 1, b))
    for hf in range(2):
        for g in range(3):
            blocks.append(("dn", 1, hf, g))
    for i in (0, 2, 3, 4, 5):
        blocks.append(("in", i))
    for hf in range(2):
        blocks.append(("oa", hf))
        blocks.append(("ol", hf))
    for b in range(11):
        blocks.append(("up", 2, b))
    for hf in range(2):
        for g in range(3):
            blocks.append(("dn", 2, hf, g))
    for hf in range(2):
        blocks.append(("pg", hf))
        blocks.append(("pp", hf))
    assert len(blocks) == NBLK
    return blocks


BLOCKS = block_table()
BIDX = {b: i for i, b in enumerate(BLOCKS)}


def blk_extent(b):
    k = b[0]
    if k == "up":
        return 128, 4096
    if k == "dn":
        nf = 8 if b[3] < 2 else 6
        return 128, nf * 512
    if k == "in":
        return (128, 4096) if b[1] in (0, 3, 4) else (128, 1024)
    if k == "oa":
        return 128, 2048
    if k == "ol":
        return 128, 2048
    if k == "pg":
        return 128, 4096
    if k == "pp":
        return 128, 1024
    raise ValueError(b)


def build(ntiles_prompt=SEQ // TILE, with_sample=True):
    NTOK = ntiles_prompt * TILE + (TS if with_sample else 0)
    NPOS = SEQ + TS
    nc = bass.Bass("TRN2", target_bir_lowering=False)
    P = Prog()
    es = ExitStack()

    def din(name, shape, dt=F32):
        return nc.dram_tensor(name, list(shape), dt, kind="ExternalInput").ap()

    def dout(name, shape, dt=F32):
        return nc.dram_tensor(name, list(shape), dt, kind="ExternalOutput").ap()

    x_d = din("x", [NTOK, D])
    p_d = din("p", [NTOK, PLE])
    ck_d = din("cache_k", [128, 128])
    cv_d = din("cache_v", [128, 128])
    sc_d = din("state_conv", [3, 512])
    sh_d = din("state_h", [512])
    wg_d = {1: din("ffn1_wg", [D, DFF]), 2: din("ffn2_wg", [D, DFF])}
    wu_d = {1: din("ffn1_wu", [D, DFF]), 2: din("ffn2_wu", [D, DFF])}
    wd_d = {1: din("ffn1_wd", [DFF, D]), 2: din("ffn2_wd", [DFF, D])}
    win_d = din("w_in", [D, INC])
    wout_d = din("w_out", [D, D])
    wpg_d = din("w_ple_gate", [D, D])
    wpp_d = din("w_ple", [PLE, D])
    lng_d = [din("ln%d_g" % i, [D]) for i in (1, 2, 3)]
    lnb_d = [din("ln%d_b" % i, [D]) for i in (1, 2, 3)]
    sink_d = din("attn_sinks", [8])
    convw_d = din("conv_w", [4, 512])
    convb_d = din("conv_b", [512])
    wa_d = din("lru_wa", [8, 64, 64])
    ba_d = din("lru_ba", [512])
    wx_d = din("lru_wx", [8, 64, 64])
    bx_d = din("lru_bx", [512])
    lam_d = din("lru_lambda", [512])
    rope_d = din("rope", [2, 128, NPOS])
    ident_d = din("ident", [128, 128])
    perm_d = din("perm", [128, 3, 128])

    y_d = dout("y", [NTOK, D])
    outs_state = {}
    for tag in ("p", "s"):
        outs_state[tag] = dict(
            k=dout("newk_" + tag, [128, 128]), v=dout("newv_" + tag, [128, 128]),
            c=dout("conv_" + tag, [3, 512]), h=dout("h_" + tag, [512]))

    ws_d = nc.dram_tensor("wstream", [NBLK, 128, SLOT], BF16, kind="Internal").ap()

    def sb(name, shape, dt=F32):
        return es.enter_context(nc.sbuf_tensor(name, list(shape), dt))

    hbuf = [sb("hbuf%d" % s, [128, D]) for s in range(4)]
    actT = sb("actT", [128, 8, TILE], BF16)
    hid = sb("hid", [128, NF, TILE], BF16)
    ring = [sb("ring%d" % i, [128, SLOT], BF16) for i in range(NS)]
    qT = sb("qT", [128, 4, TILE], BF16)
    kT = sb("kT", [128, 2, 128 + TILE], BF16)
    Vb = sb("Vb", [128, 5, 2, 66], BF16)
    xbuf = sb("xbuf", [128, 4, 4 + TILE])
    attnT = sb("attnT", [128, 4, TILE], BF16)
    atm = [sb("atm%d" % i, [64, 256]) for i in range(3)]
    esl = sb("esl", [64, 8])
    sink64 = sb("sink64", [64, 8])
    lruT = sb("lruT", [128, 4, TILE], BF16)
    PT = [sb("PT%d" % i, [128, 2, 256], BF16) for i in range(4)]
    rec = [sb("rec%d" % i, [64, 4]) for i in range(3)]
    rope_c = sb("rope_c", [128, TILE])
    rope_s = sb("rope_s", [128, TILE])
    gbc = [sb("gbc%d" % i, [128, D]) for i in range(3)]
    bbc = [sb("bbc%d" % i, [128, D]) for i in range(3)]
    pbuf = sb("pbuf", [128, 4, PLE])
    pT = sb("pT", [128, 2, TILE], BF16)
    ftA = [sb("ftA%d" % i, [128, TILE]) for i in range(2)]
    ftB = [sb("ftB%d" % i, [128, TILE]) for i in range(2)]
    ident = sb("ident_sb", [128, 128])
    perm = sb("perm_b", [128, 3, 128], BF16)
    qb = [sb("qb%d" % i, [128, TILE], BF16) for i in range(2)]
    ones_b = sb("ones_b", [128, 64], BF16)
    cw = sb("cw", [128, 4, 4])
    cb = sb("cb", [128, 4])
    hba = sb("hba", [128, 4])
    hbx = sb("hbx", [128, 4])
    lam = sb("lam", [128, 4])
    chalf = sb("chalf", [128, 4])
    cfull = sb("cfull", [128, 4])
    ltmp = sb("ltmp", [128, 4])
    wbd = sb("wbd", [128, 2, 4, 128], BF16)
    hcar = sb("hcar", [128, 4])
    LXC = [sb("L_xc%d" % i, [128, TILE]) for i in range(2)]
    LXCB = [sb("L_xcb%d" % i, [128, TILE], BF16) for i in range(2)]
    L_tr = sb("L_tr", [128, TILE])
    LA = sb("LA", [128, 2, TILE])
    LS = sb("LS", [128, 2, TILE])
    LU = sb("LU", [128, 2, TILE])
    L_ti = sb("L_ti", [128, TILE])
    LHS = [sb("L_hs%d" % i, [128, TILE]) for i in range(2)]
    L_g = sb("L_g", [128, TILE])
    L_g2 = sb("L_g2", [128, TILE])
    stats_l = [sb("stats%d" % i, [128, 2, 6]) for i in range(4)]
    mv_l = [sb("mv%d" % i, [128, 2]) for i in range(4)]
    rstd_l = [sb("rstd%d" % i, [128, 1]) for i in range(4)]
    nb_l = [sb("nb%d" % i, [128, 1]) for i in range(4)]
    mhalf = sb("mhalf", [128, 1])
    kfin = sb("kfin", [128, 128])
    kfin_t = sb("kfin_t", [128, 128])
    vfin = sb("vfin", [128, 128])
    ckt = sb("ckt", [128, 2, 128])
    cvt = sb("cvt", [128, 128])

    psum_all = es.enter_context(nc.psum_tensor("ps_all", [128, 8, 512], F32))
    psum = [psum_all[:, i, :] for i in range(8)]
    bank_ctr = [0]

    def bank():
        b = bank_ctr[0] % 8
        bank_ctr[0] += 1
        return b

    def bank_pair():
        if bank_ctr[0] % 8 == 7:
            bank_ctr[0] += 1
        return bank(), bank()

    esem = {e: es.enter_context(nc.semaphore("s_" + e)) for e in Prog.ENGS}
    dsem_names = (["ring%d" % i for i in range(NS)] + ["hb%d" % s for s in range(4)] +
                  ["pb", "rope", "setup", "prep", "st_small", "ck", "cv", "car"])
    class _DS(dict):
        def __missing__(self, n):
            v = es.enter_context(nc.semaphore("d_" + n))
            self[n] = v
            return v
    dsems = _DS()
    for n in dsem_names:
        dsems[n]
    P.final_sems.add("setup")

    def dma(eng, out, in_, reads, writes, sem, nonc=False):
        if nonc:
            def fn(e, out=out, in_=in_):
                with nc.allow_non_contiguous_dma(reason="tiny strided state"):
                    return e.dma_start(out=out, in_=in_)
        else:
            def fn(e, out=out, in_=in_):
                return e.dma_start(out=out, in_=in_)
        return P.op(eng, fn, reads, writes, dsem=sem)

    def mm(out, lhsT, rhs, start, stop, reads, writes):
        return P.op("pe", lambda e: e.matmul(out, lhsT, rhs, start=start, stop=stop), reads, writes)

    def tp(out, in_, idn, reads, writes):
        return P.op("pe", lambda e: e.transpose(out, in_, idn), reads, writes)

    def act(out, in_, func, reads, writes, bias=None, scale=None):
        kw = {}
        if bias is not None:
            kw["bias"] = bias
        if scale is not None:
            kw["scale"] = scale
        return P.op("act", lambda e: e.activation(out, in_, func, **kw), reads, writes)

    def tt(eng, out, a, b, op, reads, writes):
        return P.op(eng, lambda e: e.tensor_tensor(out, a, b, op), reads, writes)

    def ts(eng, out, a, s1, s2, op0, op1, reads, writes):
        if op1 is None:
            return P.op(eng, lambda e: e.tensor_scalar(out, a, s1, None, op0), reads, writes)
        return P.op(eng, lambda e: e.tensor_scalar(out, a, s1, s2, op0, op1), reads, writes)

    def stt(out, a, s, b, op0, op1, reads, writes):
        return P.op("dve", lambda e: e.scalar_tensor_tensor(out, a, s, b, op0, op1), reads, writes)

    def cp(eng, out, in_, reads, writes):
        if eng == "act":
            return P.op("act", lambda e: e.copy(out, in_), reads, writes)
        return P.op(eng, lambda e: e.tensor_copy(out, in_), reads, writes)

    def mset(eng, out, val, writes):
        return P.op(eng, lambda e: e.memset(out, val), (), writes)

    prep_n = [0]
    prep_cur = [0]

    import os as _os
    KDBG = _os.environ.get("KDBG", "")
    STOP = int(_os.environ.get("KSTOP", "0"))

    def prep(out, in_):
        if "noprep" in KDBG:
            return
        def fn(e, out=out, in_=in_):
            with nc.allow_non_contiguous_dma(reason="one-time weight re-blocking"):
                return e.dma_start(out=out, in_=in_)
        prep_n[0] += 1
        o_ = P.op("pool", fn, (), ("wsdx%d" % prep_n[0],), dsem="prep%d" % prep_cur[0])
        P.last_w["wsd%d" % prep_cur[0]] = o_

    def wsv(bi):
        return ws_d[bi]

    def prep_block(bi):
        b = BLOCKS[bi]
        prep_cur[0] = bi
        k = b[0]
        dst = wsv(bi)
        if k == "up":
            f, blk = b[1], b[2]
            for gu, src in enumerate((wg_d[f], wu_d[f])):
                o = dst[:, gu * 2048:(gu + 1) * 2048].rearrange("p (k c) -> p k c", k=8)
                i = src[:, blk * 256:(blk + 1) * 256].rearrange("(k p) c -> p k c", p=128)
                prep(o, i)
        elif k == "dn":
            f, hf, g = b[1], b[2], b[3]
            nf = 8 if g < 2 else 6
            o = dst[:, 0:nf * 512].rearrange("p (f c) -> p f c", f=nf)
            i = wd_d[f][g * 8 * 128:(g * 8 + nf) * 128, hf * 512:(hf + 1) * 512].rearrange(
                "(f p) c -> p f c", p=128)
            prep(o, i)
        elif k == "in":
            j = b[1]
            src = win_d.rearrange("(k p) c -> p k c", p=128)
            if j in (0, 3, 4):
                o4 = dst.rearrange("p (k c) -> p k c", k=8)
            if j == 0:
                prep(o4, src[:, :, 0:512])
            elif j == 3:
                prep(o4, src[:, :, 768:1280])
            elif j == 4:
                prep(o4, src[:, :, 1280:1792])
            elif j == 5:
                o = dst[:, 0:1024].rearrange("p (k c) -> p k c", k=8)
                prep(o, src[:, :, 640:768])
            elif j == 2:
                o = dst[:, 0:1024].rearrange("p (k c) -> p k c", k=8)
                prep(o, src[:, :, 512:640])
        elif k == "oa":
            hf = b[1]
            o = dst[:, 0:2048].rearrange("p (s c) -> p s c", s=4)
            for pidx in range(4):
                for hh in range(2):
                    h = HEAD_PERM[2 * pidx + hh]
                    prep(o[hh * 64:(hh + 1) * 64, pidx, :], wout_d[h * 64:(h + 1) * 64, hf * 512:(hf + 1) * 512])
        elif k == "ol":
            hf = b[1]
            o = dst[:, 0:2048].rearrange("p (f c) -> p f c", f=4)
            i = wout_d[512:1024, hf * 512:(hf + 1) * 512].rearrange("(f p) c -> p f c", p=128)
            prep(o, i)
        elif k == "pg":
            hf = b[1]
            o = dst.rearrange("p (k c) -> p k c", k=8)
            i = wpg_d[:, hf * 512:(hf + 1) * 512].rearrange("(k p) c -> p k c", p=128)
            prep(o, i)
        elif k == "pp":
            hf = b[1]
            o = dst[:, 0:1024].rearrange("p (k c) -> p k c", k=2)
            i = wpp_d[:, hf * 512:(hf + 1) * 512].rearrange("(k p) c -> p k c", p=128)
            prep(o, i)


    setup_n = [0]

    def sdma(out, in_, nonc=False):
        setup_n[0] += 1
        o_ = dma("sp", out, in_, (), ("consts_%d" % setup_n[0],), "setup", nonc=nonc)
        P.last_w["consts"] = o_

    sdma(ident[:], ident_d)
    perm_f = ftA[0][:, 0:384].rearrange("p (a b) -> p a b", a=3)
    dma("sp", perm_f, perm_d, (), ("ftA0",), "permld")
    cp("dve", perm[:], perm_f, ("ftA0",), ("consts2",))
    for i in range(3):
        sdma(gbc[i][:], lng_d[i].partition_broadcast(128), nonc=True)
        sdma(bbc[i][:], lnb_d[i].partition_broadcast(128), nonc=True)
    for tap in range(4):
        sdma(cw[:, tap, :], convw_d[tap].rearrange("(c p) -> p c", p=128), nonc=True)
    sdma(cb[:], convb_d.rearrange("(c p) -> p c", p=128), nonc=True)
    sdma(hba[:], ba_d.rearrange("(c p) -> p c", p=128), nonc=True)
    sdma(hbx[:], bx_d.rearrange("(c p) -> p c", p=128), nonc=True)
    sdma(lam[:], lam_d.rearrange("(c p) -> p c", p=128), nonc=True)
    wn = 0
    for ax, src in enumerate((wa_d, wx_d)):
        P.final_sems.add("setup2_%d" % ax)
        stage = ftB[ax][:, :].rearrange("p (c d) -> p c d", c=4)
        mset("pool", ftB[ax][:, :], 0.0, ("ftB%d" % ax,))
        for cc in range(4):
            for hh in range(2):
                wn += 1
                o_ = dma("sp", stage[hh * 64:(hh + 1) * 64, cc, hh * 64:(hh + 1) * 64], src[cc * 2 + hh],
                         ("ftB%d" % ax,), ("wbdf_%d" % wn,), "setup2_%d" % ax, nonc=True)
                P.last_w["wbdf%d" % ax] = o_
        cp("pool", wbd[:, ax, :, :], stage, ("wbdf%d" % ax, "ftB%d" % ax), ("consts2", "ftB%d" % ax))
    mset("pool", ones_b[:], 1.0, ("consts2",))
    mset("pool", mhalf[:], -0.5, ("consts2",))
    sdma(sink64[:], sink_d.partition_broadcast(64), nonc=True)
    act(sink64[:], sink64[:], AF.Exp, ("consts",), ("esf",))
    for sl in range(8):
        h = HEAD_PERM[sl]
        cp("dve", esl[:, sl:sl + 1], sink64[:, h:h + 1], ("esf",), ("consts2",))
    mset("pool", Vb[:, :, :, 64:66], 1.0, ("Vb",))
    ts("dve", hba[:], hba[:], 0.5, None, ALU.mult, None, ("consts",), ("consts2",))
    ts("dve", hbx[:], hbx[:], 0.5, None, ALU.mult, None, ("consts",), ("consts2",))
    P.op("act", lambda e: e.activation(ltmp[:], lam[:], AF.Exp, scale=-1.0), ("consts",), ("lt",))
    lt2 = sb("lt2", [128, 4])
    lt3 = sb("lt3", [128, 4])
    lt4 = sb("lt4", [128, 4])
    ts("dve", lt2[:], ltmp[:], 2.0, None, ALU.add, None, ("lt",), ("lt2",))
    P.op("dve", lambda e: e.reciprocal(lt2[:], lt2[:]), ("lt2",), ("lt2",))
    tt("dve", lt2[:], lt2[:], ltmp[:], ALU.mult, ("lt2", "lt"), ("lt2",))
    tt("dve", lt3[:], lt2[:], lt2[:], ALU.mult, ("lt2",), ("lt3",))
    ts("dve", lt4[:], lt3[:], 1.0 / 13, 1.0 / 11, ALU.mult, ALU.add, ("lt3",), ("lt4",))
    for cst in (1.0 / 9, 1.0 / 7, 1.0 / 5, 1.0 / 3, 1.0):
        tt("dve", lt4[:], lt4[:], lt3[:], ALU.mult, ("lt4", "lt3"), ("lt4",))
        ts("dve", lt4[:], lt4[:], cst, None, ALU.add, None, ("lt4",), ("lt4",))
    tt("dve", lt4[:], lt4[:], lt2[:], ALU.mult, ("lt4", "lt2"), ("lt4",))
    ts("dve", cfull[:], lt4[:], -16.0, None, ALU.mult, None, ("lt4",), ("consts2",))
    ts("dve", chalf[:], lt4[:], -8.0, None, ALU.mult, None, ("lt4",), ("consts2",))

    CONST = ("consts", "consts2")

    ntile_total = ntiles_prompt + (1 if with_sample else 0)
    total_blocks = ntile_total * NBLK
    wstate = {"next": 0, "rel": set(), "prep": 0}
    PREP_AHEAD = 14

    def w_pump():
        while wstate["next"] < total_blocks and (wstate["next"] < NS or (wstate["next"] - NS) in wstate["rel"]):
            i = wstate["next"]
            b = BLOCKS[i % NBLK]
            npart, nel = blk_extent(b)
            slot = i % NS
            while wstate["prep"] < NBLK and wstate["prep"] <= i + PREP_AHEAD:
                prep_block(wstate["prep"])
                wstate["prep"] += 1
            dma("sp", ring[slot][0:npart, 0:nel], ws_d[i % NBLK][0:npart, 0:nel],
                ("wsd%d" % (i % NBLK),), ("ring%d" % slot,), "ring%d" % slot)
            wstate["next"] += 1

    def w_acquire(g):
        w_pump()
        assert wstate["next"] > g, (g, wstate["next"])
        return ring[g % NS], "ring%d" % (g % NS)

    def w_release(g, total):
        wstate["rel"].add(g)
        w_pump()

    pending_T = []

    def flush_T():
        while pending_T:
            pending_T.pop(0)()

    def ffn(ti, which, T, nsub, nt, lnidx, g0):
        def up_chunk(j, rg, rname, jj, t0, t1):
            ss = range(t0 // 128, (t1 + 127) // 128)
            bG, bU = bank(), bank()
            W_ = t1 - t0
            for gu, bk in ((0, bG), (1, bU)):
                for k in range(8):
                    lhsT = rg[:, gu * 2048 + k * 256 + jj * 128: gu * 2048 + k * 256 + (jj + 1) * 128]
                    mm(psum[bk][:, 0:W_], lhsT, actT[:, k, t0:t1], k == 0, k == 7,
                       (rname,) + aT_names(nsub, (k,), ss), ("ps%d" % bk,))
            fa, fb = ftA[j % 2], ftB[j % 2]
            act(fa[:, 0:W_], psum[bG][:, 0:W_], AF.Tanh, ("ps%d" % bG,), ("ftA%d" % (j % 2),), scale=0.5)
            stt(fb[:, 0:W_], fa[:, 0:W_], 1.0, psum[bG][:, 0:W_], ALU.add, ALU.mult,
                ("ftA%d" % (j % 2), "ps%d" % bG), ("ftB%d" % (j % 2),))
            tt("dve", hid[:, j, t0:t1], fb[:, 0:W_], psum[bU][:, 0:W_], ALU.mult,
               ("ftB%d" % (j % 2), "ps%d" % bU), ("hid%d" % j,))

        NSPB = 3 if T == TILE else 0
        if not NSPB:
            flush_T()
        if NSPB:
            gsp = [g0 + BIDX[("up", which, b)] for b in range(NSPB)]
            rsp = [w_acquire(g) for g in gsp]
            for h_ in range(2):
                if h_ == 1:
                    flush_T()
                for b in range(NSPB):
                    for jj in range(2):
                        up_chunk(2 * b + jj, rsp[b][0], rsp[b][1], jj, h_ * 256, (h_ + 1) * 256)
            for g in gsp:
                w_release(g, total_blocks)
        for b in range(NSPB, 11):
            g = g0 + BIDX[("up", which, b)]
            rg, rname = w_acquire(g)
            for jj in range(2):
                up_chunk(2 * b + jj, rg, rname, jj, 0, T)
            w_release(g, total_blocks)
        for hf in range(2):
            gs = [g0 + BIDX[("dn", which, hf, gg)] for gg in range(3)]
            rgs = [w_acquire(g) for g in gs]
            for s in range(nsub):
                bk = bank()
                for f in range(NF):
                    rg, rname = rgs[f // 8]
                    mm(psum[bk][0:nt, :], hid[:, f, s * 128:s * 128 + nt],
                       rg[:, (f % 8) * 512:(f % 8 + 1) * 512], f == 0, f == NF - 1,
                       (rname, "hid%d" % f), ("ps%d" % bk,))
                hv = hbuf[s][0:nt, hf * 512:(hf + 1) * 512]
                stt(hv, hv, 4.0 * ALPHA, psum[bk][0:nt, :], ALU.mult, ALU.add,
                    ("hb%d" % s, "ps%d" % bk), ("hb%d" % s,))
                if hf == 1:
                    if s >= 2:
                        to_feature_major_s(s - 2, nt)
                    layer_norm(s, nt, lnidx, 16.0 * LN_EPS)
            for g in gs:
                w_release(g, total_blocks)
        for s in range(max(0, nsub - 2), nsub):
            pending_T.append(lambda s=s: to_feature_major_s(s, nt))

    def layer_norm(s, nt, lnidx, eps):
        hn = "hb%d" % s
        hv = hbuf[s]
        stats, mv, rstd, nb = stats_l[s], mv_l[s], rstd_l[s], nb_l[s]
        sn, mn, rn, nn = "stats%d" % s, "mv%d" % s, "rstd%d" % s, "nb%d" % s
        for c in range(2):
            P.op("dve", lambda e, c=c: e.bn_stats(stats[0:nt, c, :], hv[0:nt, c * 512:(c + 1) * 512]),
                 (hn,), (sn,))
        P.op("dve", lambda e: e.bn_aggr(mv[0:nt, :], stats[0:nt, :, :].rearrange("p a b -> p (a b)")),
             (sn,), (mn,))
        ts("pool", rstd[0:nt, :], mv[0:nt, 1:2], eps, None, ALU.add, None, (mn,), (rn,))
        tt("pool", rstd[0:nt, :], rstd[0:nt, :], mhalf[0:nt, :], ALU.pow, (rn,) + CONST, (rn,))
        stt(nb[0:nt, :], mv[0:nt, 0:1], -1.0, rstd[0:nt, :], ALU.mult, ALU.mult, (mn, rn), (nn,))
        act(hv[0:nt, :], hv[0:nt, :], AF.Identity, (hn, rn, nn), (hn,), bias=nb[0:nt, :], scale=rstd[0:nt, :])
        tt("dve", hv[0:nt, :], hv[0:nt, :], gbc[lnidx][0:nt, :], ALU.mult, (hn,) + CONST, (hn,))
        tt("pool", hv[0:nt, :], hv[0:nt, :], bbc[lnidx][0:nt, :], ALU.add, (hn,) + CONST, (hn,))

    def aT_names(nsub, ks=range(8), ss=None):
        ss = range(nsub) if ss is None else ss
        return tuple(sorted({"aT%d_%d" % (s_, k_ // 4) for s_ in ss for k_ in ks}))

    def to_feature_major_s(s, nt):
        for q in range(2):
            bk = bank()
            for kk in range(4):
                k = q * 4 + kk
                tp(psum[bk][:, kk * 128:kk * 128 + nt], hbuf[s][0:nt, k * 128:(k + 1) * 128],
                   ident[0:nt, 0:nt], ("hb%d" % s,) + CONST, ("ps%d" % bk,))
            src = psum[bk][:, :].rearrange("p (k t) -> p k t", k=4)[:, :, 0:nt]
            dst = actT[:, q * 4:(q + 1) * 4, s * 128:s * 128 + nt]
            if q == 0:
                cp("act", dst, src, ("ps%d" % bk,), ("aT%d_%d" % (s, q),))
            else:
                cp("dve", dst, src, ("ps%d" % bk,), ("aT%d_%d" % (s, q),))

    def to_feature_major(nsub, nt):
        for s in range(nsub):
            to_feature_major_s(s, nt)

    def tile_prog(ti, seq, tok0, T, pos0, first, last, nxt=None, preloaded=False):
        nsub = (T + 127) // 128
        nt = min(T, 128)
        g0 = ti * NBLK
        tag = "p" if seq == "prompt" else "s"

        if first:
            if seq == "prompt":
                mset("pool", xbuf[:, :, 0:4], 0.0, ["xb%d" % cc for cc in range(4)])
                mset("pool", hcar[:], 0.0, ("hcar",))
            else:
                for cc in range(4):
                    dma("sp", xbuf[:, cc, 1:4], sc_d[:, cc * 128:(cc + 1) * 128].rearrange("r p -> p r"),
                        (), ("xb%d" % cc,), "car%d" % cc, nonc=True)
                dma("sp", hcar[:], sh_d.rearrange("(c p) -> p c", p=128), (), ("hcar",), "carh", nonc=True)
                dma("sp", ckt[:, 0, :], ck_d, (), ("ckt",), "ck")
                dma("sp", ckt[:, 1, 0:64], ck_d[:, 64:128], (), ("ckt",), "ck", nonc=True)
                dma("sp", ckt[:, 1, 64:128], ck_d[:, 0:64], (), ("ckt",), "ck", nonc=True)
                dma("sp", cvt[:], cv_d, (), ("cvt",), "cv")
                for var in range(2):
                    bk = bank()
                    tp(psum[bk][:, 0:128], ckt[:, var, :], ident[:], ("ckt",) + CONST, ("ps%d" % bk,))
                    cp("act", kT[:, var, 0:128], psum[bk][:, 0:128], ("ps%d" % bk,), ("kT",))
                cp("act", Vb[:, 0, :, 0:64], cvt[:].rearrange("p (a b) -> p a b", a=2), ("cvt",), ("Vb",))

        if not preloaded:
            for s in range(nsub):
                dma("sp", hbuf[s][0:nt, :], x_d[tok0 + s * 128: tok0 + s * 128 + nt, :], (), ("hb%d" % s,), "hb%d" % s)
        dma("sp", pbuf[0:nt, 0:nsub, :],
            p_d[tok0:tok0 + T, :].rearrange("(s p) c -> p s c", p=nt), (), ("pbuf",), "pb")
        dma("sp", rope_c[:, 0:T], rope_d[0, :, pos0:pos0 + T], (), ("rope",), "rope")
        dma("sp", rope_s[:, 0:T], rope_d[1, :, pos0:pos0 + T], (), ("rope",), "rope")

        if STOP == 1:
            return
        for s in range(nsub):
            if nsub > 2 and s >= 2:
                pending_T.append(lambda s=s: to_feature_major_s(s, nt))
            else:
                to_feature_major_s(s, nt)
        ffn(ti, 1, T, nsub, nt, 0, g0)

        if STOP == 2:
            return
        def inproj_chunk(blk, ci, t0=0, t1=None):
            t1 = T if t1 is None else t1
            ss = range(t0 // 128, (t1 + 127) // 128)
            rg, rname = w_acquire(g0 + BIDX[("in", blk)])
            bk = bank()
            for k in range(8):
                cw_ = 128 if blk == 2 else 512
                mm(psum[bk][:, 0:t1 - t0], rg[:, k * cw_ + ci * 128:k * cw_ + (ci + 1) * 128], actT[:, k, t0:t1],
                   k == 0, k == 7, (rname,) + aT_names(nsub, (k,), ss), ("ps%d" % bk,))
            return bk

        halves = ((0, 256), (256, 512)) if T == TILE else ((0, T),)
        items = [(hi_, t0, t1, qc) for hi_, (t0, t1) in enumerate(halves) for qc in range(4)]
        pend = []

        def q_finish(it, bq, ri):
            hi_, t0, t1, qc = it
            W_ = t1 - t0
            bs = bank()
            mm(psum[bs][:, 0:W_], perm[:, 0, :], qb[ri % 2][:, 0:W_], True, True, ("qb%d" % (ri % 2),) + CONST,
               ("ps%d" % bs,))
            fa, fb = ftA[ri % 2], ftB[ri % 2]
            tt("dve", fa[:, 0:W_], psum[bq][:, 0:W_], rope_c[:, t0:t1], ALU.mult,
               ("ps%d" % bq, "rope"), ("ftA%d" % (ri % 2),))
            tt("dve", fb[:, 0:W_], psum[bs][:, 0:W_], rope_s[:, t0:t1], ALU.mult,
               ("ps%d" % bs, "rope"), ("ftB%d" % (ri % 2),))
            tt("pool", qT[:, qc, t0:t1], fa[:, 0:W_], fb[:, 0:W_], ALU.add,
               ("ftA%d" % (ri % 2), "ftB%d" % (ri % 2)), ("qT",))

        for ri, it in enumerate(items):
            hi_, t0, t1, qc = it
            if hi_ == len(halves) - 1 and qc == 0:
                flush_T()
            bq = inproj_chunk(0, qc, t0, t1)
            cp("act", qb[ri % 2][:, 0:t1 - t0], psum[bq][:, 0:t1 - t0], ("ps%d" % bq,), ("qb%d" % (ri % 2),))
            if pend:
                q_finish(*pend.pop())
            pend.append((it, bq, ri))
        q_finish(*pend.pop())
        w_release(g0 + BIDX[("in", 0)], total_blocks)
        if STOP == 21:
            return
        bkk = inproj_chunk(2, 0)
        cp("act", qb[0][:, 0:T], psum[bkk][:, 0:T], ("ps%d" % bkk,), ("qb0",))
        bks, bkd, bkds = bank(), bank(), bank()
        for pi_, bb in ((0, bks), (1, bkd), (2, bkds)):
            mm(psum[bb][:, 0:T], perm[:, pi_, :], qb[0][:, 0:T], True, True, ("qb0",) + CONST, ("ps%d" % bb,))
        for var, (bq, bs) in enumerate(((bkk, bks), (bkd, bkds))):
            fa, fb = ftA[var], ftB[var]
            tt("dve", fa[:, 0:T], psum[bq][:, 0:T], rope_c[:, 0:T], ALU.mult,
               ("ps%d" % bq, "rope"), ("ftA%d" % var,))
            tt("dve", fb[:, 0:T], psum[bs][:, 0:T], rope_s[:, 0:T], ALU.mult,
               ("ps%d" % bs, "rope"), ("ftB%d" % var,))
            tt("pool", kT[:, var, 128:128 + T], fa[:, 0:T], fb[:, 0:T], ALU.add,
               ("ftA%d" % var, "ftB%d" % var), ("kT",))
            if last and var == 0:
                nl = min(T, 128)
                tt("pool", kfin[:, 0:nl], fa[:, T - nl:T], fb[:, T - nl:T], ALU.add,
                   ("ftA0", "ftB0"), ("kfin",))
        w_release(g0 + BIDX[("in", 2)], total_blocks)
        if STOP == 22:
            return
        rg, rname = w_acquire(g0 + BIDX[("in", 5)])
        for s in range(nsub):
            bk = bank()
            for k in range(8):
                mm(psum[bk][0:nt, 0:128], actT[:, k, s * 128:s * 128 + nt], rg[:, k * 128:(k + 1) * 128],
                   k == 0, k == 7, (rname,) + aT_names(nsub, (k,), (s,)), ("ps%d" % bk,))
            cp("act", Vb[0:nt, 1 + s, :, 0:64], psum[bk][0:nt, 0:128].rearrange("p (a b) -> p a b", a=2),
               ("ps%d" % bk,), ("Vb",))
            if last and s == nsub - 1:
                cp("dve", vfin[0:nt, :], psum[bk][0:nt, 0:128], ("ps%d" % bk,), ("vfin",))
        w_release(g0 + BIDX[("in", 5)], total_blocks)

        if STOP == 3:
            return
        nchunk = T // 64

        NPT = len(PT)

        def attn_blks(c):
            if c % 2 == 0:
                blks = [(c // 2, 0, 128), (c // 2 + 1, 0, 64)]
            else:
                blks = [((c - 1) // 2, 64, 128), ((c + 1) // 2, 0, 128)]
            if first and seq == "prompt":
                blks = [b_ for b_ in blks if b_[0] >= 1]
            return blks

        astate = {}

        def attn_A(i):
            c, v = divmod(i, 2)
            blks = attn_blks(c)
            pi = i % NPT
            bSs = bank_pair()
            ptn = "PT%d" % pi
            for bi_, (slot, lo, hi) in enumerate(blks):
                for par in range(2):
                    var = 0 if v == par else 1
                    bS = bSs[par]
                    if lo == 0:
                        lhsT = kT[par * 64:(par + 1) * 64, var, slot * 128: slot * 128 + hi]
                        out = psum[bS][0:hi, bi_ * 128:(bi_ + 1) * 128]
                    else:
                        lhsT = kT[par * 64:(par + 1) * 64, var, slot * 128: slot * 128 + 128]
                        out = psum[bS][0:128, bi_ * 128:(bi_ + 1) * 128]
                    rhs = qT[par * 64:(par + 1) * 64, 2 * v:2 * v + 2, c * 64:(c + 1) * 64]
                    mm(out, lhsT, rhs, True, True, ("kT", "qT"), ("ps%d" % bS,))
            for bi_, (slot, lo, hi) in enumerate(blks):
                act(PT[pi][lo:hi, bi_, :].rearrange("p (a c) -> p a c", a=2),
                    psum_all[lo:hi, bSs[0]:bSs[0] + 2, bi_ * 128:(bi_ + 1) * 128], AF.Exp,
                    ("ps%d" % bSs[0], "ps%d" % bSs[1]), (ptn,), scale=0.125)

        def attn_B(i):
            c, v = divmod(i, 2)
            blks = attn_blks(c)
            pi = i % NPT
            ptn = "PT%d" % pi
            ai = i % 3
            bO = bank()
            for j in range(4):
                for bi_, (slot, lo, hi) in enumerate(blks):
                    mm(psum[bO][0:64, j * 68:j * 68 + 65], PT[pi][lo:hi, bi_, j * 64:(j + 1) * 64],
                       Vb[lo:hi, slot, v, 0:65], bi_ == 0, bi_ == len(blks) - 1, ("Vb", ptn), ("ps%d" % bO,))
            rn = "rec%d" % ai
            o4 = psum[bO][0:64, 0:272].rearrange("p (j d) -> p j d", j=4)
            tt("dve", rec[ai][:, :], o4[:, :, 64], esl[:, 4 * v:4 * v + 4], ALU.add,
               ("ps%d" % bO,) + CONST, (rn,))
            P.op("dve", lambda e, ai=ai: e.reciprocal(rec[ai][:, :], rec[ai][:, :]), (rn,), (rn,))
            an = "atm%d" % ai
            tt("dve", atm[ai][:, :].rearrange("p (j d) -> p j d", j=4), o4[:, :, 0:64],
               rec[ai][:, :].unsqueeze(2).broadcast_to([64, 4, 64]), ALU.mult, ("ps%d" % bO, rn), (an,))

        def attn_C(i):
            c, v = divmod(i, 2)
            ai = i % 3
            an = "atm%d" % ai
            bT = bank()
            for pr_ in range(2):
                tp(psum[bT][:, pr_ * 64:(pr_ + 1) * 64], atm[ai][:, pr_ * 128:(pr_ + 1) * 128], ident[0:64, 0:64],
                   (an,) + CONST, ("ps%d" % bT,))
            cp("act", attnT[:, 2 * v:2 * v + 2, c * 64:(c + 1) * 64],
               psum[bT][:, 0:128].rearrange("p (a t) -> p a t", a=2), ("ps%d" % bT,), ("attnT",))

        nitem = 2 * nchunk
        attn_steps = []
        for t_ in range(nitem + 3):
            def step(t_=t_):
                if t_ < nitem:
                    attn_A(t_)
                if 0 <= t_ - 2 < nitem:
                    attn_B(t_ - 2)
                if 0 <= t_ - 3 < nitem:
                    attn_C(t_ - 3)
            attn_steps.append(step)

        def lru_a1(cc, j):
            xn = "xb%d" % cc
            bx = inproj_chunk(3, cc)
            cp("act", xbuf[:, cc, 4:4 + T], psum[bx][:, 0:T], ("ps%d" % bx,), (xn,))
            xc = LXC[j]
            ts("dve", xc[:, 0:T], xbuf[:, cc, 1:1 + T], cw[:, 0, cc:cc + 1], cb[:, cc:cc + 1], ALU.mult, ALU.add,
               (xn,) + CONST, ("L_xc%d" % j,))
            for tap in range(1, 4):
                stt(xc[:, 0:T], xbuf[:, cc, 1 + tap:1 + tap + T], cw[:, tap, cc:cc + 1], xc[:, 0:T],
                    ALU.mult, ALU.add, (xn, "L_xc%d" % j) + CONST, ("L_xc%d" % j,))
            if last:
                dma("sp", outs_state[tag]["c"][:, cc * 128:(cc + 1) * 128].rearrange("r p -> p r"),
                    xbuf[:, cc, 1 + T:4 + T], (xn,), (), "st_c%d" % cc, nonc=True)
            cp("pool", xbuf[:, cc, 1:4], xbuf[:, cc, 1 + T:4 + T], (xn,), (xn,))
            cp("act", LXCB[j][:, 0:T], xc[:, 0:T], ("L_xc%d" % j,), ("L_xcb%d" % j,))

        def lru_a2(cc, j):
            xc = LXC[j]
            bA, bI = bank(), bank()
            mm(psum[bA][:, 0:T], wbd[:, 0, cc, :], LXCB[j][:, 0:T], True, True, ("L_xcb%d" % j,) + CONST, ("ps%d" % bA,))
            mm(psum[bI][:, 0:T], wbd[:, 1, cc, :], LXCB[j][:, 0:T], True, True, ("L_xcb%d" % j,) + CONST, ("ps%d" % bI,))
            act(L_tr[:, 0:T], psum[bA][:, 0:T], AF.Tanh, ("ps%d" % bA,) + CONST, ("L_tr",),
                bias=hba[:, cc:cc + 1], scale=0.5)
            act(L_ti[:, 0:T], psum[bI][:, 0:T], AF.Tanh, ("ps%d" % bI,) + CONST, ("L_ti",),
                bias=hbx[:, cc:cc + 1], scale=0.5)
            act(LA[:, j, 0:T], L_tr[:, 0:T], AF.Exp, ("L_tr",) + CONST, ("LA%d" % j,),
                bias=chalf[:, cc:cc + 1], scale=chalf[:, cc:cc + 1])
            act(LS[:, j, 0:T], L_tr[:, 0:T], AF.Exp, ("L_tr",) + CONST, ("LS%d" % j,),
                bias=cfull[:, cc:cc + 1], scale=cfull[:, cc:cc + 1])
            ts("pool", LS[:, j, 0:T], LS[:, j, 0:T], -1.0, 1.0, ALU.mult, ALU.add, ("LS%d" % j,), ("LS%d" % j,))
            stt(LU[:, j, 0:T], L_ti[:, 0:T], 1.0, xc[:, 0:T], ALU.add, ALU.mult, ("L_ti", "L_xc%d" % j), ("LU%d" % j,))

        def lru_sqrt():
            act(LS[:, :, 0:T], LS[:, :, 0:T], AF.Sqrt, ("LS0", "LS1"), ("LS0", "LS1"))

        def lru_c1(cc, j):
            stt(LU[:, j, 0:T], LU[:, j, 0:T], 0.5, LS[:, j, 0:T], ALU.mult, ALU.mult,
                ("LU%d" % j, "LS%d" % j), ("LU%d" % j,))
            stt(LU[:, j, 0:1], LA[:, j, 0:1], hcar[:, cc:cc + 1], LU[:, j, 0:1], ALU.mult, ALU.add,
                ("LA%d" % j, "LU%d" % j, "hcar"), ("LU%d" % j,))
            P.op("dve", lambda e: e.tensor_tensor_scan(LHS[j][:, 0:T], LA[:, j, 0:T], LU[:, j, 0:T],
                                                       0.0, ALU.mult, ALU.add),
                 ("LA%d" % j, "LU%d" % j), ("L_hs%d" % j,))
            cp("pool", hcar[:, cc:cc + 1], LHS[j][:, T - 1:T], ("L_hs%d" % j,), ("hcar",))

        def lru_c2(cc, j):
            bg = inproj_chunk(4, cc)
            act(L_g[:, 0:T], psum[bg][:, 0:T], AF.Square, ("ps%d" % bg,), ("L_g",))
            ts("pool", L_g[:, 0:T], L_g[:, 0:T], 0.044715, 1.0, ALU.mult, ALU.add, ("L_g",), ("L_g",))
            tt("dve", L_g[:, 0:T], L_g[:, 0:T], psum[bg][:, 0:T], ALU.mult, ("L_g", "ps%d" % bg), ("L_g",))
            act(L_g2[:, 0:T], L_g[:, 0:T], AF.Tanh, ("L_g",), ("L_g2",), scale=0.7978845608028654)
            stt(L_g2[:, 0:T], L_g2[:, 0:T], 1.0, psum[bg][:, 0:T], ALU.add, ALU.mult,
                ("L_g2", "ps%d" % bg), ("L_g2",))
            stt(lruT[:, cc, 0:T], L_g2[:, 0:T], 0.5, LHS[j][:, 0:T], ALU.mult, ALU.mult,
                ("L_g2", "L_hs%d" % j), ("lruT",))

        lru_steps = []
        for pr in range(2):
            for j in range(2):
                lru_steps.append(lambda pr=pr, j=j: lru_a1(2 * pr + j, j))
            for j in range(2):
                lru_steps.append(lambda pr=pr, j=j: lru_a2(2 * pr + j, j))
            lru_steps.append(lru_sqrt)
            for j in range(2):
                lru_steps.append(lambda pr=pr, j=j: lru_c1(2 * pr + j, j))
            for j in range(2):
                lru_steps.append(lambda pr=pr, j=j: lru_c2(2 * pr + j, j))
        for i_ in range(max(len(attn_steps), len(lru_steps))):
            if i_ < len(attn_steps):
                attn_steps[i_]()
            if i_ < len(lru_steps):
                lru_steps[i_]()
        w_release(g0 + BIDX[("in", 3)], total_blocks)
        w_release(g0 + BIDX[("in", 4)], total_blocks)
        if last:
            dma("sp", outs_state[tag]["h"].rearrange("(c p) -> p c", p=128), hcar[:], ("hcar",), (),
                "st_h", nonc=True)

        if STOP == 5:
            return
        if last:
            nl = min(T, 128)
            bk = bank()
            tp(psum[bk][0:nl, 0:128], kfin[:, 0:nl], ident[:], ("kfin",) + CONST, ("ps%d" % bk,))
            cp("act", kfin_t[0:nl, :], psum[bk][0:nl, 0:128], ("ps%d" % bk,), ("kfin_t",))
            dma("sp", outs_state[tag]["k"][128 - nl:128, :], kfin_t[0:nl, :], ("kfin_t",), (), "st_k")
            dma("sp", outs_state[tag]["v"][128 - nl:128, :], vfin[0:nl, :], ("vfin",), (), "st_v")
            if nl < 128:
                dma("sp", outs_state[tag]["k"][0:128 - nl, :], ck_d[nl:128, :], (), (), "st_k2")
                dma("sp", outs_state[tag]["v"][0:128 - nl, :], cv_d[nl:128, :], (), (), "st_v2")
        else:
            cp("pool", kT[:, :, 0:128], kT[:, :, T:T + 128], ("kT",), ("kT",))
            cp("pool", Vb[:, 0, :, :], Vb[:, nsub, :, :], ("Vb",), ("Vb",))

        if STOP == 6:
            return
        for hf in range(2):
            ga = g0 + BIDX[("oa", hf)]
            gl = g0 + BIDX[("ol", hf)]
            ra, ran = w_acquire(ga)
            rl, rln = w_acquire(gl)
            for s in range(nsub):
                bk = bank()
                for sl in range(4):
                    mm(psum[bk][0:nt, :], attnT[:, sl, s * 128:s * 128 + nt], ra[:, sl * 512:(sl + 1) * 512],
                       sl == 0, False, (ran, "attnT"), ("ps%d" % bk,))
                for cc in range(4):
                    mm(psum[bk][0:nt, :], lruT[:, cc, s * 128:s * 128 + nt], rl[:, cc * 512:(cc + 1) * 512],
                       False, cc == 3, (rln, "lruT"), ("ps%d" % bk,))
                hv = hbuf[s][0:nt, hf * 512:(hf + 1) * 512]
                stt(hv, hv, ALPHA, psum[bk][0:nt, :], ALU.mult, ALU.add, ("hb%d" % s, "ps%d" % bk), ("hb%d" % s,))
                if hf == 1:
                    if s >= 2:
                        to_feature_major_s(s - 2, nt)
                    layer_norm(s, nt, 1, LN_EPS)
            w_release(ga, total_blocks)
            w_release(gl, total_blocks)
        for s in range(max(0, nsub - 2), nsub):
            pending_T.append(lambda s=s: to_feature_major_s(s, nt))

        if STOP == 7:
            return
        for s in range(nsub):
            bk = bank()
            for j in range(2):
                tp(psum[bk][:, j * 128:j * 128 + nt], pbuf[0:nt, s, j * 128:(j + 1) * 128], ident[0:nt, 0:nt],
                   ("pbuf",) + CONST, ("ps%d" % bk,))
            cp("act", pT[:, :, s * 128:s * 128 + nt],
               psum[bk][:, 0:256].rearrange("p (k t) -> p k t", k=2)[:, :, 0:nt], ("ps%d" % bk,), ("pT",))
        ffn(ti, 2, T, nsub, nt, 2, g0)

        if STOP == 8:
            return
        if nsub <= 2:
            flush_T()
        pgs = [g0 + BIDX[("pg", hf)] for hf in range(2)]
        pps = [g0 + BIDX[("pp", hf)] for hf in range(2)]
        rgs_ = [w_acquire(g) for g in pgs]
        rps_ = [w_acquire(g) for g in pps]
        for s in range(nsub):
            if s == 2:
                flush_T()
            for hf in range(2):
                rg_, rgn = rgs_[hf]
                rp_, rpn = rps_[hf]
                bG, bP = bank(), bank()
                for k in range(8):
                    mm(psum[bG][0:nt, :], actT[:, k, s * 128:s * 128 + nt], rg_[:, k * 512:(k + 1) * 512],
                       k == 0, k == 7, (rgn,) + aT_names(nsub, (k,), (s,)), ("ps%d" % bG,))
                for k in range(2):
                    mm(psum[bP][0:nt, :], pT[:, k, s * 128:s * 128 + nt], rp_[:, k * 512:(k + 1) * 512],
                       k == 0, k == 1, (rpn, "pT"), ("ps%d" % bP,))
                fa, fb = ftA[hf], ftB[hf]
                act(fa[0:nt, :], psum[bG][0:nt, :], AF.Tanh, ("ps%d" % bG,), ("ftA%d" % hf,), scale=0.5)
                stt(fb[0:nt, :], fa[0:nt, :], 1.0, psum[bP][0:nt, :], ALU.add, ALU.mult,
                    ("ftA%d" % hf, "ps%d" % bP), ("ftB%d" % hf,))
                hv = hbuf[s][0:nt, hf * 512:(hf + 1) * 512]
                stt(hv, fb[0:nt, :], 0.5, hv, ALU.mult, ALU.add, ("ftB%d" % hf, "hb%d" % s), ("hb%d" % s,))
            dma("sp", y_d[tok0 + s * 128: tok0 + s * 128 + nt, :], hbuf[s][0:nt, :], ("hb%d" % s,), (),
                "hb%d" % s)
            if nxt is not None and s < nxt[2]:
                ntok0, nnt = nxt[0], nxt[1]
                dma("pool", hbuf[s][0:nnt, :], x_d[ntok0 + s * 128: ntok0 + s * 128 + nnt, :], (),
                    ("hb%d" % s,), "hbx%d" % s)
        for g in pgs + pps:
            w_release(g, total_blocks)

    ti = 0
    if "preponly" in KDBG:
        ntiles_prompt = 0
        with_sample = False
    if STOP > 0:
        with_sample = False
    for t in range(ntiles_prompt):
        if t + 1 < ntiles_prompt:
            nxt = ((t + 1) * TILE, 128, 4)
        elif with_sample:
            nxt = (ntiles_prompt * TILE, TS, 1)
        else:
            nxt = None
        tile_prog(ti, "prompt", t * TILE, TILE, t * TILE, t == 0, t == ntiles_prompt - 1, nxt=nxt, preloaded=(t > 0))
        ti += 1
    if with_sample:
        tile_prog(ti, "sample", ntiles_prompt * TILE, TS, SEQ, True, True, preloaded=(ntiles_prompt > 0))
        ti += 1

    with nc.Block() as block:
        P.emit(nc, block, esem, dsems)
    es.close()
    return nc, P


def rope_tables():
    half = 8
    inv = np.power(np.float32(ROPE_THETA), -np.arange(half, dtype=np.float32) * np.float32(2.0 / 16)).astype(np.float32)
    pos = np.concatenate([np.arange(SEQ), PAST_LEN + np.arange(TS)]).astype(np.float32)
    ang = (pos[None, :] * inv[:, None]).astype(np.float32)
    cos = np.cos(ang).astype(np.float32)
    sin = np.sin(ang).astype(np.float32)
    tab = np.zeros((2, 128, SEQ + TS), np.float32)
    tab[0] = 1.0
    for hh in range(2):
        b = hh * 64
        tab[0, b:b + 8] = cos
        tab[0, b + 8:b + 16] = cos
        tab[1, b:b + 8] = -sin
        tab[1, b + 8:b + 16] = sin
    return tab


def perm_tables():
    pm = np.zeros((128, 3, 128), np.float32)

    def partner(d):
        dd = d % 64
        if dd < 8:
            return d + 8
        if dd < 16:
            return d - 8
        return None

    for m in range(128):
        p_ = partner(m)
        if p_ is not None:
            pm[p_, 0, m] = 1.0
        sw = (m + 64) % 128
        pm[sw, 1, m] = 1.0
        p2 = partner(sw)
        if p2 is not None:
            pm[p2, 2, m] = 1.0
    return pm


_CACHE = {}


def kernel(x_prompt, x_sample, p_prompt, p_sample, cache_k, cache_v, state_conv, state_h,
           ffn1_wg, ffn1_wu, ffn1_wd, ln1_g, ln1_b, w_in, attn_sinks, conv_w, conv_b,
           lru_wa, lru_ba, lru_wx, lru_bx, lru_lambda, w_out, ln2_g, ln2_b,
           ffn2_wg, ffn2_wu, ffn2_wd, ln3_g, ln3_b, w_ple, w_ple_gate, _ntiles=SEQ // TILE):
    f = lambda a: np.ascontiguousarray(np.asarray(a, dtype=np.float32))
    n = 8
    ntp = _ntiles
    if "nc" not in _CACHE or _CACHE.get("ntp") != ntp:
        _CACHE["nc"] = build(ntp, True)[0]
        _CACHE["ntp"] = ntp
    nc = _CACHE["nc"]
    rope = rope_tables()
    ident = np.eye(128, dtype=np.float32)
    perm = perm_tables()
    shared = {
        "ffn1_wg": f(ffn1_wg[0]), "ffn1_wu": f(ffn1_wu[0]), "ffn1_wd": f(ffn1_wd[0]),
        "ffn2_wg": f(ffn2_wg[0]), "ffn2_wu": f(ffn2_wu[0]), "ffn2_wd": f(ffn2_wd[0]),
        "w_in": f(w_in[0]), "w_out": f(w_out[0]), "w_ple_gate": f(w_ple_gate[0]), "w_ple": f(w_ple[0]),
        "ln1_g": f(ln1_g[0]), "ln1_b": f(ln1_b[0]), "ln2_g": f(ln2_g[0]), "ln2_b": f(ln2_b[0]),
        "ln3_g": f(ln3_g[0]), "ln3_b": f(ln3_b[0]), "attn_sinks": f(attn_sinks[0]),
        "conv_w": f(conv_w[0]), "conv_b": f(conv_b[0]), "lru_wa": f(lru_wa[0]), "lru_ba": f(lru_ba[0]),
        "lru_wx": f(lru_wx[0]), "lru_bx": f(lru_bx[0]), "lru_lambda": f(lru_lambda[0]),
        "rope": rope, "ident": ident, "perm": perm,
    }
    L = ntp * TILE
    in_maps = []
    for b in range(n):
        m = dict(shared)
        m["x"] = np.concatenate([f(x_prompt[b, :L]), f(x_sample[b])], axis=0)
        m["p"] = np.concatenate([f(p_prompt[0, b, :L]), f(p_sample[0, b])], axis=0)
        m["cache_k"] = f(cache_k[0, b]).reshape(128, 128)
        m["cache_v"] = f(cache_v[0, b]).reshape(128, 128)
        m["state_conv"] = f(state_conv[0, b])
        m["state_h"] = f(state_h[0, b])
        in_maps.append(m)
    res = run_bass_kernel_spmd(nc, in_maps, core_ids=list(range(n)))
    R = res.results
    yp = np.stack([R[b]["y"][:L] for b in range(n)])
    ys = np.stack([R[b]["y"][L:] for b in range(n)])

    def st(name, shape):
        return np.stack([np.asarray(R[b][name]).reshape(shape) for b in range(n)])[None]

    return (yp.astype(np.float32), ys.astype(np.float32),
            st("newk_p", (128, 2, 64)), st("newv_p", (128, 2, 64)), st("conv_p", (3, 512)), st("h_p", (512,)),
            st("newk_s", (128, 2, 64)), st("newv_s", (128, 2, 64)), st("conv_s", (3, 512)), st("h_s", (512,)))
```
